# Optimizing a Trainium2 kernel written in Bass

```python
import jax, jax.numpy as jnp
from jax import lax
import numpy as np

D_MODEL = 2048
BATCH = 2
SEQ = 8192
DEPTH = 4

N_HEADS = 16
HEAD_DIM = D_MODEL // N_HEADS
N_KV_GROUPS = 4
CMP_BLOCK = 32
CMP_STRIDE = 16
SLC_BLOCK = 64
N_SELECT = 16
WINDOW = 512
Q_BLOCK = 64
PHI_HIDDEN = HEAD_DIM
FORCE_BONUS = 1.0e4
ROPE_THETA = 10000.0
CONV_DIM = D_MODEL
CONV_WIDTH = 3
D_INNER = D_MODEL
SSD_HEAD_DIM = 64
SSD_HEADS = D_INNER // SSD_HEAD_DIM
SSD_GROUPS = 4
SSD_STATE = 128
SSD_CONV = 4
SSD_CHUNK = 128
D_FF = 4 * D_MODEL
PLE_DIM = 256
N_BRANCHES = 3
EPS = 1e-6

KV_DIM = N_KV_GROUPS * HEAD_DIM
SSD_XBC = D_INNER + 2 * SSD_GROUPS * SSD_STATE
PROJ_WIDTHS = (N_HEADS * HEAD_DIM, KV_DIM, KV_DIM, KV_DIM, KV_DIM, KV_DIM, KV_DIM, N_HEADS * 3,
               CONV_DIM, CONV_DIM, CONV_DIM, D_INNER, SSD_XBC, SSD_HEADS, N_BRANCHES * D_MODEL)
PROJ_TOTAL = sum(PROJ_WIDTHS)

kernel_name = 'hybrid_nsa_shortconv_ssd_block'


def rms_norm(x, g):
    xf = x.astype(jnp.float32)
    y = xf * lax.rsqrt(jnp.mean(xf * xf, axis=-1, keepdims=True) + EPS)
    return (y * g.astype(jnp.float32)).astype(x.dtype)


def rope_tables(positions):
    inv_freq = 1.0 / (ROPE_THETA ** (jnp.arange(0, HEAD_DIM, 2, dtype=jnp.float32) / HEAD_DIM))
    ang = positions.astype(jnp.float32)[..., None] * inv_freq
    return jnp.cos(ang)[:, :, None, :], jnp.sin(ang)[:, :, None, :]


def apply_rope(x, cos, sin):
    x1, x2 = jnp.split(x.astype(jnp.float32), 2, axis=-1)
    return jnp.concatenate([x1 * cos - x2 * sin, x2 * cos + x1 * sin], axis=-1).astype(x.dtype)


def masked_softmax(s, mask):
    s = jnp.where(mask, s.astype(jnp.float32), -jnp.inf)
    m = jnp.max(s, axis=-1, keepdims=True)
    e = jnp.where(mask, jnp.exp(s - jnp.where(jnp.isfinite(m), m, 0.0)), 0.0)
    d = jnp.sum(e, axis=-1, keepdims=True)
    return e / jnp.where(d > 0, d, 1.0)


def causal_depthwise_conv(u, w):
    k, T = w.shape[0], u.shape[1]
    up = jnp.pad(u, ((0, 0), (k - 1, 0), (0, 0)))
    y = up[:, 0:T] * w[0]
    for j in range(1, k):
        y = y + up[:, j:j + T] * w[j]
    return y


def compress_blocks(k, pe, w1, b1, w2, b2):
    bsz, T, G, dk = k.shape
    n_cmp = (T - CMP_BLOCK) // CMP_STRIDE + 1
    idx = jnp.arange(n_cmp)[:, None] * CMP_STRIDE + jnp.arange(CMP_BLOCK)[None, :]
    blk = k[:, idx] + pe[:, None, :]
    blk = blk.transpose(0, 3, 1, 2, 4).reshape(bsz, G, n_cmp, CMP_BLOCK * dk)
    return jax.nn.silu(blk @ w1 + b1) @ w2 + b2


def nsa_mixer(q, k_c, v_c, k_s, v_s, k_w, v_w, gate, pe_k, pe_v,
              pk_w1, pk_b1, pk_w2, pk_b2, pv_w1, pv_b1, pv_w2, pv_b2):
    bsz, T, H, dk = q.shape
    G = k_c.shape[2]
    R = H // G
    scale = dk ** -0.5
    n_cmp = (T - CMP_BLOCK) // CMP_STRIDE + 1
    n_slc = T // SLC_BLOCK
    n_sel = min(N_SELECT, n_slc)
    kcmp = compress_blocks(k_c, pe_k, pk_w1, pk_b1, pk_w2, pk_b2)
    vcmp = compress_blocks(v_c, pe_v, pv_w1, pv_b1, pv_w2, pv_b2)
    cstart = jnp.arange(n_cmp) * CMP_STRIDE
    sstart = jnp.arange(n_slc) * SLC_BLOCK
    cend = cstart + CMP_BLOCK - 1
    overlap = ((cstart[:, None] < sstart[None, :] + SLC_BLOCK)
               & (cstart[:, None] + CMP_BLOCK > sstart[None, :])).astype(jnp.float32)
    kblk = k_s.reshape(bsz, n_slc, SLC_BLOCK, G, dk).transpose(0, 3, 1, 2, 4)
    vblk = v_s.reshape(bsz, n_slc, SLC_BLOCK, G, dk).transpose(0, 3, 1, 2, 4)
    kw_pad = jnp.pad(k_w, ((0, 0), (WINDOW, 0), (0, 0), (0, 0)))
    vw_pad = jnp.pad(v_w, ((0, 0), (WINDOW, 0), (0, 0), (0, 0)))
    b_idx = jnp.arange(bsz)[:, None, None, None]
    g_idx = jnp.arange(G)[None, :, None, None]
    blk = jnp.arange(n_slc)

    def query_block(i):
        s = i * Q_BLOCK
        t = s + jnp.arange(Q_BLOCK)
        qb = lax.dynamic_slice_in_dim(q, s, Q_BLOCK, axis=1).reshape(bsz, Q_BLOCK, G, R, dk)
        gb = lax.dynamic_slice_in_dim(gate, s, Q_BLOCK, axis=1).reshape(bsz, Q_BLOCK, G, R, 3)
        p_c = masked_softmax(jnp.einsum('bqgrd,bgnd->bgrqn', qb, kcmp) * scale,
                             cend[None, :] <= t[:, None])
        o_c = jnp.einsum('bgrqn,bgnd->bqgrd', p_c, vcmp)
        imp = jnp.einsum('bgrqn,nj->bgqj', p_c, overlap)
        valid = sstart[None, :] <= t[:, None]
        cur = (t // SLC_BLOCK)[:, None]
        forced = (blk[None, :] == 0) | (blk[None, :] == cur) | (blk[None, :] == cur - 1)
        score = jnp.where(valid, imp + jnp.where(forced, FORCE_BONUS, 0.0), -jnp.inf)
        _, sel = lax.top_k(score, n_sel)
        kg = kblk[b_idx, g_idx, sel].reshape(bsz, G, Q_BLOCK, n_sel * SLC_BLOCK, dk)
        vg = vblk[b_idx, g_idx, sel].reshape(bsz, G, Q_BLOCK, n_sel * SLC_BLOCK, dk)
        kpos = (sel[..., None] * SLC_BLOCK + jnp.arange(SLC_BLOCK)).reshape(
            bsz, G, 1, Q_BLOCK, n_sel * SLC_BLOCK)
        p_s = masked_softmax(jnp.einsum('bqgrd,bgqkd->bgrqk', qb, kg) * scale, kpos <= t[:, None])
        o_s = jnp.einsum('bgrqk,bgqkd->bqgrd', p_s, vg)
        kwb = lax.dynamic_slice_in_dim(kw_pad, s, WINDOW + Q_BLOCK, axis=1)
        vwb = lax.dynamic_slice_in_dim(vw_pad, s, WINDOW + Q_BLOCK, axis=1)
        kidx = s - WINDOW + jnp.arange(WINDOW + Q_BLOCK)
        wmask = ((kidx[None, :] <= t[:, None]) & (kidx[None, :] > t[:, None] - WINDOW)
                 & (kidx[None, :] >= 0))
        p_w = masked_softmax(jnp.einsum('bqgrd,bkgd->bgrqk', qb, kwb) * scale, wmask)
        o_w = jnp.einsum('bgrqk,bkgd->bqgrd', p_w, vwb)
        o = gb[..., 0:1] * o_c + gb[..., 1:2] * o_s + gb[..., 2:3] * o_w
        return o.reshape(bsz, Q_BLOCK, H * dk).astype(q.dtype)

    out = lax.map(query_block, jnp.arange(T // Q_BLOCK))
    return out.transpose(1, 0, 2, 3).reshape(bsz, T, H * dk)


def short_conv_mixer(gate_b, gate_c, u, w):
    return gate_b * causal_depthwise_conv(gate_c * u, w)


def ssd_mixer(z, xbc, dt_raw, conv_w, conv_b, dt_bias, a_log, d_skip, norm_g):
    f32 = jnp.float32
    bsz, T, _ = xbc.shape
    R = SSD_HEADS // SSD_GROUPS
    nc = T // SSD_CHUNK
    xbc = jax.nn.silu(causal_depthwise_conv(xbc, conv_w) + conv_b)
    xs, bm, cm = jnp.split(xbc, [D_INNER, D_INNER + SSD_GROUPS * SSD_STATE], axis=-1)
    xs = xs.reshape(bsz, T, SSD_GROUPS, R, SSD_HEAD_DIM).astype(f32)
    bm = bm.reshape(bsz, nc, SSD_CHUNK, SSD_GROUPS, SSD_STATE).astype(f32)
    cm = cm.reshape(bsz, nc, SSD_CHUNK, SSD_GROUPS, SSD_STATE).astype(f32)
    dt = jax.nn.softplus(dt_raw.astype(f32) + dt_bias.astype(f32)).reshape(bsz, T, SSD_GROUPS, R)
    a = -jnp.exp(a_log.astype(f32)).reshape(SSD_GROUPS, R)
    xdt = (xs * dt[..., None]).reshape(bsz, nc, SSD_CHUNK, SSD_GROUPS, R, SSD_HEAD_DIM)
    da = (dt * a).reshape(bsz, nc, SSD_CHUNK, SSD_GROUPS, R).transpose(0, 3, 4, 1, 2)
    acs = jnp.cumsum(da, axis=-1)
    causal = jnp.tril(jnp.ones((SSD_CHUNK, SSD_CHUNK), dtype=bool))
    seg = jnp.exp(jnp.where(causal, acs[..., :, None] - acs[..., None, :], -jnp.inf))
    cb = jnp.einsum('bclgn,bcsgn->bgcls', cm, bm)
    y_diag = jnp.einsum('bgrcls,bcsgrp->bclgrp', cb[:, :, None] * seg, xdt)
    decay_to_end = jnp.exp(acs[..., -1:] - acs)
    chunk_states = jnp.einsum('bclgn,bgrcl,bclgrp->cbgrpn', bm, decay_to_end, xdt)
    chunk_decay = jnp.exp(acs[..., -1]).transpose(3, 0, 1, 2)

    def step(h, inp):
        st, dec = inp
        return h * dec[..., None, None] + st, h

    h0 = jnp.zeros(chunk_states.shape[1:], f32)
    _, states_in = lax.scan(step, h0, (chunk_states, chunk_decay))
    y_off = jnp.einsum('bclgn,cbgrpn,bgrcl->bclgrp', cm, states_in, jnp.exp(acs))
    y = (y_diag + y_off).reshape(bsz, T, SSD_GROUPS, R, SSD_HEAD_DIM) \
        + d_skip.astype(f32).reshape(SSD_GROUPS, R, 1) * xs
    y = y.reshape(bsz, T, D_INNER) * jax.nn.silu(z.astype(f32))
    yg = y.reshape(bsz, T, SSD_GROUPS, D_INNER // SSD_GROUPS)
    yg = yg * lax.rsqrt(jnp.mean(yg * yg, axis=-1, keepdims=True) + EPS)
    return (yg.reshape(bsz, T, D_INNER) * norm_g.astype(f32)).astype(z.dtype)


def setup_inputs(seed: int = 0) -> dict:
    key = jax.random.key(seed)
    ks = jax.random.split(key, 32)
    L = DEPTH

    def nrm(k, shape, s):
        return jax.random.normal(k, shape, jnp.float32) * s

    def gain(k, n):
        return 1.0 + nrm(k, (L, n), 0.02)

    dt0 = jnp.exp(jax.random.uniform(ks[17], (L, SSD_HEADS), jnp.float32,
                                     minval=float(np.log(1e-3)), maxval=float(np.log(1e-1))))
    offs = jax.random.randint(ks[2], (BATCH, 1), 0, 2048, dtype=jnp.int32)
    return {
        'x': nrm(ks[0], (BATCH, SEQ, D_MODEL), 1.0),
        'p': nrm(ks[1], (DEPTH, BATCH, SEQ, PLE_DIM), 1.0),
        'positions': jnp.arange(SEQ, dtype=jnp.int32)[None, :] + offs,
        'g_mix': gain(ks[3], D_MODEL),
        'w_in': nrm(ks[4], (L, D_MODEL, PROJ_TOTAL), D_MODEL ** -0.5),
        'nsa_pe_k': nrm(ks[5], (L, CMP_BLOCK, HEAD_DIM), 0.1),
        'nsa_pe_v': nrm(ks[6], (L, CMP_BLOCK, HEAD_DIM), 0.1),
        'phi_k_w1': nrm(ks[7], (L, CMP_BLOCK * HEAD_DIM, PHI_HIDDEN), (CMP_BLOCK * HEAD_DIM) ** -0.5),
        'phi_k_b1': nrm(ks[8], (L, PHI_HIDDEN), 0.01),
        'phi_k_w2': nrm(ks[9], (L, PHI_HIDDEN, HEAD_DIM), PHI_HIDDEN ** -0.5),
        'phi_k_b2': nrm(ks[10], (L, HEAD_DIM), 0.01),
        'phi_v_w1': nrm(ks[11], (L, CMP_BLOCK * HEAD_DIM, PHI_HIDDEN), (CMP_BLOCK * HEAD_DIM) ** -0.5),
        'phi_v_b1': nrm(ks[12], (L, PHI_HIDDEN), 0.01),
        'phi_v_w2': nrm(ks[13], (L, PHI_HIDDEN, HEAD_DIM), PHI_HIDDEN ** -0.5),
        'phi_v_b2': nrm(ks[14], (L, HEAD_DIM), 0.01),
        'sconv_w': nrm(ks[15], (L, CONV_WIDTH, CONV_DIM), CONV_WIDTH ** -0.5),
        'ssd_conv_w': nrm(ks[16], (L, SSD_CONV, SSD_XBC), SSD_CONV ** -0.5),
        'ssd_conv_b': nrm(ks[18], (L, SSD_XBC), 0.01),
        'ssd_dt_bias': dt0 + jnp.log(-jnp.expm1(-dt0)),
        'ssd_a_log': jnp.log(jax.random.uniform(ks[19], (L, SSD_HEADS), jnp.float32, minval=1.0, maxval=16.0)),
        'ssd_d': 1.0 + nrm(ks[20], (L, SSD_HEADS), 0.1),
        'ssd_norm_g': gain(ks[21], D_INNER),
        'w_o': nrm(ks[22], (L, D_MODEL, D_MODEL), D_MODEL ** -0.5),
        'g_mlp': gain(ks[23], D_MODEL),
        'w_up': nrm(ks[24], (L, D_MODEL, D_FF), D_MODEL ** -0.5),
        'w_down': nrm(ks[25], (L, D_FF, D_MODEL), D_FF ** -0.5),
        'g_ple': gain(ks[26], D_MODEL),
        'w_ple': nrm(ks[27], (L, PLE_DIM, D_MODEL), PLE_DIM ** -0.5),
        'w_ple_gate': nrm(ks[28], (L, D_MODEL, D_MODEL), D_MODEL ** -0.5),
        'g_final': 1.0 + nrm(ks[29], (D_MODEL,), 0.02),
    }


def reference(x, p, positions, g_mix, w_in, nsa_pe_k, nsa_pe_v,
              phi_k_w1, phi_k_b1, phi_k_w2, phi_k_b2, phi_v_w1, phi_v_b1, phi_v_w2, phi_v_b2,
              sconv_w, ssd_conv_w, ssd_conv_b, ssd_dt_bias, ssd_a_log, ssd_d, ssd_norm_g,
              w_o, g_mlp, w_up, w_down, g_ple, w_ple, w_ple_gate, g_final):
    bsz, T, _ = x.shape
    cos, sin = rope_tables(positions)
    splits = [int(v) for v in np.cumsum(PROJ_WIDTHS)[:-1]]
    for i in range(DEPTH):
        h = rms_norm(x, g_mix[i])
        proj = h @ w_in[i]
        (q, k_c, v_c, k_s, v_s, k_w, v_w, g_nsa, cb_gate, cc_gate, c_u,
         s_z, s_xbc, s_dt, g_merge) = jnp.split(proj, splits, axis=-1)
        q = apply_rope(q.reshape(bsz, T, N_HEADS, HEAD_DIM), cos, sin)
        kv = lambda a: a.reshape(bsz, T, N_KV_GROUPS, HEAD_DIM)
        k_c = apply_rope(kv(k_c), cos, sin)
        k_s = apply_rope(kv(k_s), cos, sin)
        k_w = apply_rope(kv(k_w), cos, sin)
        y_a = nsa_mixer(q, k_c, kv(v_c), k_s, kv(v_s), k_w, kv(v_w),
                        jax.nn.sigmoid(g_nsa).reshape(bsz, T, N_HEADS, 3),
                        nsa_pe_k[i], nsa_pe_v[i],
                        phi_k_w1[i], phi_k_b1[i], phi_k_w2[i], phi_k_b2[i],
                        phi_v_w1[i], phi_v_b1[i], phi_v_w2[i], phi_v_b2[i]).astype(x.dtype)
        y_b = short_conv_mixer(cb_gate, cc_gate, c_u, sconv_w[i]).astype(x.dtype)
        y_c = ssd_mixer(s_z, s_xbc, s_dt, ssd_conv_w[i], ssd_conv_b[i], ssd_dt_bias[i],
                        ssd_a_log[i], ssd_d[i], ssd_norm_g[i]).astype(x.dtype)
        gm = jax.nn.sigmoid(g_merge).reshape(bsz, T, N_BRANCHES, D_MODEL)
        merged = gm[:, :, 0] * y_a + gm[:, :, 1] * y_b + gm[:, :, 2] * y_c
        x = x + (merged @ w_o[i]).astype(x.dtype)
        h = rms_norm(x, g_mlp[i])
        x = x + (jnp.square(jax.nn.relu(h @ w_up[i])) @ w_down[i]).astype(x.dtype)
        gate = jax.nn.sigmoid(rms_norm(x, g_ple[i]) @ w_ple_gate[i])
        x = x + ((p[i] @ w_ple[i]) * gate).astype(x.dtype)
    return rms_norm(x, g_final)
```

```python
from contextlib import ExitStack
import numpy as np
import ml_dtypes
import concourse.bass as bass
import concourse.mybir as mybir
from concourse.bass_utils import run_bass_kernel_spmd

F32 = mybir.dt.float32
BF16 = mybir.dt.bfloat16
I32 = mybir.dt.int32
AF = mybir.ActivationFunctionType
ALU = mybir.AluOpType
NPBF = ml_dtypes.bfloat16

D = 2048
KC = 16
TQ = 512
NEG = -30000.0
EPS = 1e-6


class TT:
    __slots__ = ("t", "writers", "readers", "name")

    def __init__(self, t, name=""):
        self.t = t
        self.writers = {}
        self.readers = {}
        self.name = name

    def __getitem__(self, k):
        return self.t[k]


class _Eng:
    def __init__(self, name, sem):
        self.name = name
        self.sem = sem
        self.cnt = 0
        self.cmds = []
        self.waited = {}


class Sched:
    ENGS = ("pe", "act", "dve", "pool", "sp")

    def __init__(self, nc, stack, n_dma_sems=12):
        self.nc = nc
        self.stack = stack
        self.cur = stack
        self.e = {}
        self.semobj = {}
        for n in self.ENGS:
            sem = stack.enter_context(nc.semaphore("s_" + n))
            self.e[n] = _Eng(n, sem)
        self.dsem = {}
        for q in ("sp", "pool"):
            sems = [stack.enter_context(nc.semaphore("d_%s%d" % (q, i))) for i in range(n_dma_sems)]
            self.dsem[q] = {"sems": sems, "vals": [0] * n_dma_sems, "next": 0}

    def _uniq(self, name):
        self._n = getattr(self, "_n", 0) + 1
        return "%s_%d" % (name, self._n)

    def sbuf(self, name, shape, dtype):
        return TT(self.cur.enter_context(self.nc.sbuf_tensor(self._uniq(name), list(shape), dtype)), name)

    def psum(self, name, shape, dtype=F32):
        return TT(self.cur.enter_context(self.nc.psum_tensor(self._uniq(name), list(shape), dtype)), name)

    def begin_stage(self):
        self._prev = getattr(self, "_prev", [])
        self._prev.append(self.cur)
        self.cur = ExitStack()

    def push_scope(self):
        self._prev = getattr(self, "_prev", [])
        self._prev.append(self.cur)
        self.cur = ExitStack()

    def pop_scope(self):
        self.cur.close()
        self.cur = self._prev.pop()

    def collective(self, in_tt, out_tt, groups):
        if not hasattr(self, "ccsem"):
            self.ccsem = self.stack.enter_context(self.nc.semaphore("s_cc"))
            self.cccnt = 0
        E = self.e["pool"]
        need = self._collect("pool", [in_tt], [out_tt])
        self.cccnt += 1
        val = self.cccnt
        sem = self.ccsem

        def cmd(h, need=need, sem=sem, i=in_tt, o=out_tt, groups=groups):
            for s_, v in need:
                h.wait_ge(s_, v)
            h.collective_compute("AllGather", mybir.AluOpType.bypass, replica_groups=groups,
                                 ins=[i.t.opt()], outs=[o.t.opt()]).then_inc(sem)
        E.cmds.append(cmd)
        k = self._key(sem)
        out_tt.writers = {k: val}
        out_tt.readers = {}
        if val > in_tt.readers.get(k, 0):
            in_tt.readers[k] = val

    def end_stage(self):
        self.barrier()
        with self.nc.Block() as block:
            self._emit(block)
        self.cur.close()
        self.cur = self._prev.pop()

    def dram(self, name, shape, dtype, kind="Internal"):
        if kind is None:
            return TT(self.nc.dram_tensor(name, list(shape), dtype).ap(), name)
        return TT(self.nc.dram_tensor(name, list(shape), dtype, kind=kind).ap(), name)

    def view(self, tt, name=""):
        return TT(tt.t, name or tt.name)

    def _key(self, sem):
        k = id(sem)
        self.semobj[k] = sem
        return k

    def _collect(self, eng, reads, writes):
        own = self._key(self.e[eng].sem)
        waits = {}
        for t in reads:
            for k, v in t.writers.items():
                if v > waits.get(k, 0):
                    waits[k] = v
        for t in writes:
            for src in (t.writers, t.readers):
                for k, v in src.items():
                    if k == own:
                        continue
                    if v > waits.get(k, 0):
                        waits[k] = v
        E = self.e[eng]
        need = []
        for k, v in waits.items():
            if v > E.waited.get(k, 0):
                E.waited[k] = v
                need.append((self.semobj[k], v))
        return need

    def op(self, eng, fn, reads=(), writes=()):
        E = self.e[eng]
        need = self._collect(eng, reads, writes)
        E.cnt += 1
        idx = E.cnt
        sem = E.sem

        def cmd(h, need=need, fn=fn, sem=sem):
            for s, v in need:
                h.wait_ge(s, v)
            fn(h).then_inc(sem, 1)
        E.cmds.append(cmd)
        k = self._key(sem)
        for t in writes:
            t.writers = {k: idx}
            t.readers = {}
        for t in reads:
            if t in writes:
                continue
            if idx > t.readers.get(k, 0):
                t.readers[k] = idx

    def dma(self, out_ap, in_ap, reads=(), writes=(), q="sp", **kw):
        E = self.e[q]
        Dq = self.dsem[q]
        i = Dq["next"]
        Dq["next"] = (i + 1) % len(Dq["sems"])
        sem = Dq["sems"][i]
        k = self._key(sem)
        need = self._collect(q, reads, writes)
        prev = Dq["vals"][i]
        if prev > E.waited.get(k, 0):
            E.waited[k] = prev
            need.append((sem, prev))
        val = prev + 16
        Dq["vals"][i] = val

        def cmd(h, need=need, sem=sem, out_ap=out_ap, in_ap=in_ap, kw=kw):
            for s, v in need:
                h.wait_ge(s, v)
            h.dma_start(out=out_ap, in_=in_ap, **kw).then_inc(sem, 16)
        E.cmds.append(cmd)
        for t in writes:
            t.writers = {k: val}
            t.readers = {}
        for t in reads:
            if val > t.readers.get(k, 0):
                t.readers[k] = val

    def _all_final(self):
        fin = []
        for n in self.ENGS:
            E = self.e[n]
            if E.cnt:
                fin.append((E.sem, E.cnt))
        for q, Dq in self.dsem.items():
            for s, v in zip(Dq["sems"], Dq["vals"]):
                if v:
                    fin.append((s, v))
        if getattr(self, "cccnt", 0):
            fin.append((self.ccsem, self.cccnt))
        return fin

    def barrier(self):
        fin = self._all_final()
        for n in self.ENGS:
            E = self.e[n]
            need = []
            for s, v in fin:
                k = self._key(s)
                if v > E.waited.get(k, 0):
                    E.waited[k] = v
                    need.append((s, v))

            def cmd(h, need=need):
                for s, v in need:
                    h.wait_ge(s, v)
            E.cmds.append(cmd)

    def finish(self, block):
        self.barrier()
        self._emit(block)

    def _emit(self, block):
        def run(cmds):
            def f(h):
                for c in cmds:
                    c(h)
            return f
        block.tensor(run(self.e["pe"].cmds))
        block.scalar(run(self.e["act"].cmds))
        block.vector(run(self.e["dve"].cmds))
        block.gpsimd(run(self.e["pool"].cmds))
        block.sync(run(self.e["sp"].cmds))
        for n in self.ENGS:
            self.e[n].cmds = []


def mm(S, out_tt, out_ap, pairs, reads, start=True, stop=True):
    def fn(h, pairs=pairs, out_ap=out_ap, start=start, stop=stop):
        n = len(pairs)
        ins = None
        for i, (l, r) in enumerate(pairs):
            ins = h.matmul(out_ap, lhsT=l, rhs=r, start=(start and i == 0), stop=(stop and i == n - 1))
        return ins
    S.op("pe", fn, reads=reads, writes=[out_tt])


class Rot:
    def __init__(self, tiles):
        self.tiles = tiles
        self.i = 0

    def get(self):
        t = self.tiles[self.i]
        self.i = (self.i + 1) % len(self.tiles)
        return t


class WLoader:
    def __init__(self, S, nbuf=3, kc=KC, ncol=128, name="w"):
        self.S = S
        self.st = Rot([S.sbuf("%s_st%d" % (name, i), [128, kc, ncol], F32) for i in range(nbuf)])
        self.bf = Rot([S.sbuf("%s_bf%d" % (name, i), [128, kc, ncol], BF16) for i in range(nbuf)])
        self.cast_i = 0

    def load(self, dram_tt, dram_ap, kc=KC, ncol=128):
        S = self.S
        st = self.st.get()
        bf = self.bf.get()
        S.dma(st[:, 0:kc, 0:ncol], dram_ap, reads=[dram_tt], writes=[st])
        eng = "pool" if (self.cast_i % 2 == 0) else "dve"
        self.cast_i += 1
        S.op(eng, lambda h, st=st, bf=bf: h.tensor_copy(out=bf[:, 0:kc, 0:ncol], in_=st[:, 0:kc, 0:ncol]),
             reads=[st], writes=[bf])
        return bf


def rmsnorm_fm(S, x_sb, g_tt, g_off, ncols, sq_bf, ps, ones_bf, cst, eps_col, rstd, outs, nfeat=D, kc=KC):
    for c in range(kc):
        S.op("act", lambda h, c=c: h.activation(out=sq_bf[:, c, 0:ncols], in_=x_sb[:, c, 0:ncols], func=AF.Square),
             reads=[x_sb], writes=[sq_bf])
    mm(S, ps, ps[:, 0:ncols], [(ones_bf[:, :], sq_bf[:, c, 0:ncols]) for c in range(kc)], reads=[sq_bf, cst])
    S.op("act", lambda h: h.activation(out=rstd[:, 0:ncols], in_=ps[:, 0:ncols], func=AF.Ln, bias=eps_col, scale=1.0 / nfeat),
         reads=[ps, cst], writes=[rstd])
    S.op("act", lambda h: h.activation(out=rstd[:, 0:ncols], in_=rstd[:, 0:ncols], func=AF.Exp, scale=-0.5),
         reads=[rstd], writes=[rstd])
    for o in outs:
        for c in range(kc):
            S.op("dve", lambda h, c=c, o=o: h.scalar_tensor_tensor(
                out=o[:, c, 0:ncols], in0=x_sb[:, c, 0:ncols], scalar=g_tt[:, g_off + c:g_off + c + 1],
                in1=rstd[:, 0:ncols], op0=ALU.mult, op1=ALU.mult), reads=[x_sb, g_tt, rstd], writes=[o])


def emit_p2(nc, S, Tc, io):
    if True:
        if True:
            pass
        xT, xo, mT_all, oh_d, pT = io["x_in"], io["x_out"], io["mT_all"], io["oh"], io["pT"]
        w_o, w_up, w_dn, w_pg, w_pl, gv_d = io["w_o"], io["w_up"], io["w_dn"], io["w_pg"], io["w_pl"], io["gv"]
        hb, hf = io["hb"], io["hf"]
        S.begin_stage()
        oh = S.sbuf("oh_sb", [128, 4], F32)
        S.dma(oh[:, :], oh_d[:, :], reads=[oh_d], writes=[oh])
        cand = Rot([S.sbuf("cand%d" % i, [128, KC, TQ], BF16) for i in range(1)])
        cst = S.sbuf("cst", [128, 256], BF16)
        gv = S.sbuf("gvs", [128, 64], F32)
        x_sb = S.sbuf("x_sb", [128, KC, TQ], F32)
        a16 = S.sbuf("a16", [128, KC, TQ], BF16)
        h_bf = S.sbuf("h_bf", [128, KC, TQ], BF16)
        hid = S.sbuf("hid", [128, 4 * KC, TQ], BF16)
        p32 = S.sbuf("p32", [128, 2, TQ], F32)
        p16 = S.sbuf("p16", [128, 2, TQ], BF16)
        rstd = S.sbuf("rstd", [128, TQ], F32)
        tmpr = Rot([S.sbuf("tmp%d" % i, [128, TQ], F32) for i in range(2)])
        gate = S.sbuf("gate", [128, TQ], F32)
        WL = WLoader(S, nbuf=3)
        psr = Rot([S.psum("ps%d" % i, [128, TQ]) for i in range(4)])
        psn = S.psum("psn", [128, TQ])

        S.op("dve", lambda h: h.memset(cst[:, 0:128], 1.0), writes=[cst])
        S.op("dve", lambda h: h.memset(gv[:, 48:49], EPS), writes=[gv])
        S.dma(gv[:, 0:48], gv_d[:, :], reads=[gv_d], writes=[gv])
        ones_bf = cst[:, 0:128]
        eps_col = gv[:, 48:49]

        def fm(ap, t0):
            return ap.rearrange("(kc p) t -> p kc t", p=128)[:, :, t0:t0 + TQ]

        def wblk(w, r0, c0):
            return w[r0:r0 + D, c0:c0 + 128].rearrange("(kc p) n -> p kc n", p=128)

        for tt in range(Tc // TQ):
            t0 = tt * TQ
            S.dma(x_sb[:, :, :], fm(xT.t, t0), reads=[xT], writes=[x_sb])
            for j in range(4):
                cd = cand.get()
                gtok = j * Tc + t0
                mk = mT_all[gtok // 1024]
                S.dma(cd[:, :, :], mk.t.rearrange("(kc p) t -> p kc t", p=128)[:, :, (gtok % 1024):(gtok % 1024) + TQ], reads=[mk], writes=[cd])
                if j == 0:
                    S.op("dve", lambda h, cd=cd: h.tensor_scalar(out=a16[:, :, :], in0=cd[:, :, :], scalar1=oh[:, 0:1], scalar2=None, op0=ALU.mult),
                         reads=[cd, oh], writes=[a16])
                else:
                    S.op("dve", lambda h, cd=cd, j=j: h.scalar_tensor_tensor(out=a16[:, :, :], in0=cd[:, :, :], scalar=oh[:, j:j + 1], in1=a16[:, :, :],
                                                                           op0=ALU.mult, op1=ALU.add), reads=[cd, oh, a16], writes=[a16])
            S.dma(p32[:, :, :], fm(pT.t, t0), reads=[pT], writes=[p32])
            S.op("dve", lambda h: h.tensor_copy(out=p16[:, :, :], in_=p32[:, :, :]), reads=[p32], writes=[p16])
            for dc in range(KC):
                wb = WL.load(w_o, wblk(w_o.t, 0, dc * 128))
                ps = psr.get()
                mm(S, ps, ps[:, :], [(wb[:, k, :], a16[:, k, :]) for k in range(KC)], reads=[wb, a16])
                S.op("dve", lambda h, dc=dc, ps=ps: h.tensor_tensor(out=x_sb[:, dc, :], in0=ps[:, :], in1=x_sb[:, dc, :], op=ALU.add),
                     reads=[ps, x_sb], writes=[x_sb])
            rmsnorm_fm(S, x_sb, gv, 0, TQ, a16, psn, ones_bf, cst, eps_col, rstd, [h_bf])
            for fc in range(4 * KC):
                wb = WL.load(w_up, wblk(w_up.t, 0, fc * 128))
                ps = psr.get()
                mm(S, ps, ps[:, :], [(wb[:, k, :], h_bf[:, k, :]) for k in range(KC)], reads=[wb, h_bf])
                tmp = tmpr.get()
                S.op("act", lambda h, ps=ps, tmp=tmp: h.activation(out=tmp[:, :], in_=ps[:, :], func=AF.Relu), reads=[ps], writes=[tmp])
                S.op("dve", lambda h, fc=fc, tmp=tmp: h.tensor_tensor(out=hid[:, fc, :], in0=tmp[:, :], in1=tmp[:, :], op=ALU.mult),
                     reads=[tmp], writes=[hid])
            for dc in range(KC):
                ps = psr.get()
                for q4 in range(4):
                    wb = WL.load(w_dn, wblk(w_dn.t, q4 * D, dc * 128))
                    mm(S, ps, ps[:, :], [(wb[:, k, :], hid[:, q4 * KC + k, :]) for k in range(KC)], reads=[wb, hid],
                       start=(q4 == 0), stop=(q4 == 3))
                S.op("dve", lambda h, dc=dc, ps=ps: h.tensor_tensor(out=x_sb[:, dc, :], in0=ps[:, :], in1=x_sb[:, dc, :], op=ALU.add),
                     reads=[ps, x_sb], writes=[x_sb])
            rmsnorm_fm(S, x_sb, gv, 16, TQ, a16, psn, ones_bf, cst, eps_col, rstd, [h_bf])
            for dc in range(KC):
                wb = WL.load(w_pg, wblk(w_pg.t, 0, dc * 128))
                ps = psr.get()
                mm(S, ps, ps[:, :], [(wb[:, k, :], h_bf[:, k, :]) for k in range(KC)], reads=[wb, h_bf])
                S.op("act", lambda h, ps=ps: h.activation(out=gate[:, :], in_=ps[:, :], func=AF.Sigmoid), reads=[ps], writes=[gate])
                wb2 = WL.load(w_pl, w_pl.t[:, dc * 128:dc * 128 + 128].rearrange("(kc p) n -> p kc n", p=128), kc=2)
                ps2 = psr.get()
                mm(S, ps2, ps2[:, :], [(wb2[:, k, :], p16[:, k, :]) for k in range(2)], reads=[wb2, p16])
                tmp = tmpr.get()
                S.op("dve", lambda h, ps2=ps2, tmp=tmp: h.tensor_tensor(out=tmp[:, :], in0=ps2[:, :], in1=gate[:, :], op=ALU.mult),
                     reads=[ps2, gate], writes=[tmp])
                S.op("dve", lambda h, dc=dc, tmp=tmp: h.tensor_tensor(out=x_sb[:, dc, :], in0=tmp[:, :], in1=x_sb[:, dc, :], op=ALU.add),
                     reads=[tmp, x_sb], writes=[x_sb])
            S.dma(fm(xo.t, t0), x_sb[:, :, :], reads=[x_sb], writes=[xo])
            rmsnorm_fm(S, x_sb, gv, 32, TQ, a16, psn, ones_bf, cst, eps_col, rstd, [h_bf])
            for hh in range(2):
                hk = hb[t0 // 256 + hh]
                S.dma(hk.t.rearrange("(kc p) t -> p kc t", p=128), h_bf[:, :, hh * 256:(hh + 1) * 256], reads=[h_bf], writes=[hk])
            if hf is not None:
                for c in range(KC):
                    S.op("dve", lambda h, c=c: h.scalar_tensor_tensor(
                        out=x_sb[:, c, :], in0=x_sb[:, c, :], scalar=gv[:, 32 + c:33 + c], in1=rstd[:, :],
                        op0=ALU.mult, op1=ALU.mult), reads=[x_sb, gv, rstd], writes=[x_sb])
                S.dma(fm(hf.t, t0), x_sb[:, :, :], reads=[x_sb], writes=[hf])
        S.end_stage()


CF = {"ident": 0, "U": 128, "ssdneg": 256, "psw": 384, "pbig": 512, "invf": 768, "sgn": 769, "eps": 770, "one": 771,
      "npi": 772, "n": 776}
TWO_PI = 2.0 * np.pi
PI_IN = 3.1415925


def cb_layout(T):
    o = {}
    c = 0
    for nm, w in (("ident", 128), ("ones", 128), ("cmask", 5 * TQ), ("wmask", 8 * TQ), ("ebig", T), ("ovl1", 4 * 129),
                  ("onesel12", 144), ("sel12", 12 * 128)):
        o[nm] = c
        c += w
    o["n"] = c
    return o


def make_consts(T):
    p = np.arange(128)
    cF = np.zeros((128, CF["n"]), np.float32)
    cF[:, 0:128] = np.eye(128)
    cF[:, 128:256] = (p[:, None] <= p[None, :])
    cF[:, 256:384] = np.where(p[None, :] >= p[:, None], 0.0, NEG)
    cF[:, 384:512] = (p[:, None] == (p[None, :] + 64) % 128)
    xx = np.arange(256)[None, :]
    lo = (p[:, None] < 64)
    pb = np.zeros((128, 256), np.float32)
    pb[:, :] = np.where(xx > 129, -1e9, 0.0)
    pb[:, 127] = np.where(p < 64, 1e4, 0.0)
    pb[:, 128] = 1e4
    pb[:, 129] = np.where(p < 64, -1e9, 1e4)
    cF[:, 512:768] = pb
    invf = (1.0 / (np.float32(10000.0) ** (np.arange(0, 128, 2, dtype=np.float32) / np.float32(128)))).astype(np.float32)
    cF[:, 768] = invf[p % 64]
    cF[:, 769] = np.where(p < 64, -1.0, 1.0)
    cF[:, 770] = EPS
    cF[:, 771] = 1.0
    cF[:, 772] = -np.pi
    L = cb_layout(T)
    cB = np.zeros((128, L["n"]), np.float32)
    cB[:, L["ident"]:L["ident"] + 128] = np.eye(128)
    cB[:, L["ones"]:L["ones"] + 128] = 1.0
    x = np.arange(TQ)[None, :]
    for di in range(5):
        delta = -2048 + 512 * di
        cB[:, L["cmask"] + di * TQ:L["cmask"] + (di + 1) * TQ] = np.where(16 * p[:, None] + 31 + delta <= x, 0.0, NEG)
    for j in range(8):
        kk = 128 * j + p[:, None] - 512
        cB[:, L["wmask"] + j * TQ:L["wmask"] + (j + 1) * TQ] = np.where((x >= kk) & (x < kk + 512), 0.0, NEG)
    key = np.arange(T)[None, :]
    cB[:, L["ebig"]:L["ebig"] + T] = (key // 64 == p[:, None])
    for i in range(4):
        n = 128 * i + p[:, None]
        j = np.arange(128)[None, :]
        ov = (n < 4 * j + 4) & (n > 4 * j - 2)
        cB[:, L["ovl1"] + i * 129:L["ovl1"] + i * 129 + 128] = ov
        cB[:, L["ovl1"] + i * 129 + 128] = 1.0
    for r in range(12):
        cB[:, L["onesel12"] + r * 12 + r] = 1.0
        cB[r, L["sel12"] + r * 128:L["sel12"] + (r + 1) * 128] = 1.0
    return cF, cB.astype(NPBF)


FM = ["q0", "q1", "q2", "q3", "kc", "ks", "kw", "vc"] + ["cb%d" % i for i in range(4)] + ["cc%d" % i for i in range(4)] \
    + ["u%d" % i for i in range(4)] + ["z%d" % i for i in range(4)] + ["xb%d" % i for i in range(6)] \
    + ["gm%d" % i for i in range(12)]
NFM = len(FM)
COL_V = NFM * 128
COL_GN = COL_V + 256
COL_DT = COL_GN + 12
NWC = COL_DT + 8
PRM = {"scw": 0, "sdw": 12, "sdb": 36, "dsk": 42, "ng": 46, "kb1": 50, "kb2": 51, "vb1": 52, "dtb": 53, "alog": 54, "n": 56}


def make_p1_scratch(S, T):
    scr = {}
    scr["qT_d"] = S.dram("qT_d", [4, 128, T], BF16)
    scr["kT_d"] = {n: S.dram(n + "T_d", [128, T], BF16) for n in ("kc", "ks", "kw", "vc")}
    scr["vtok_d"] = S.dram("vtok_d", [T, 256], BF16)
    for nm, r in (("cb_d", 512), ("cu_d", 512), ("z_d", 512), ("xb_d", 768), ("gm_d", 1536), ("gn_d", 12), ("dt_d", 8),
                  ("mA_d", 512), ("mB_d", 512), ("mC_d", 512)):
        scr[nm] = S.dram(nm, [r, T], F32)
    return scr


def emit_p1(nc, S, T, io, scr):
    nQT = T // TQ
    NKT = T // 128
    NCMP = (T - 32) // 16 + 1
    L = cb_layout(T)
    scale = 128 ** -0.5
    if True:
        hT_fn = io["hT_fn"]
        wcat, pos, cF_d, cB_d, prm_d, peT_d = io["wcat"], io["pos"], io["cF"], io["cB"], io["prm"], io["peT"]
        w1_d, w2_d, vb2_d, mT = io["w1"], io["w2"], io["vb2"], io["mT"]
        qT_d, kT_d, vtok_d = scr["qT_d"], scr["kT_d"], scr["vtok_d"]
        cb_d, cu_d, z_d, xb_d, gm_d, gn_d, dt_d = scr["cb_d"], scr["cu_d"], scr["z_d"], scr["xb_d"], scr["gm_d"], scr["gn_d"], scr["dt_d"]
        mA_d, mB_d, mC_d = scr["mA_d"], scr["mB_d"], scr["mC_d"]
        S.push_scope()
        cF = S.sbuf("cF_sb", [128, CF["n"]], F32)
        cB = S.sbuf("cB_sb", [128, L["n"]], BF16)
        prm = S.sbuf("prm_sb", [128, PRM["n"]], F32)
        S.dma(cF[:, :], cF_d[:, :], reads=[cF_d], writes=[cF])
        S.dma(cB[:, :], cB_d[:, :], reads=[cB_d], writes=[cB])
        S.dma(prm[:, :], prm_d[:, :], reads=[prm_d], writes=[prm])
        identF = cF[:, 0:128]
        identB = cB[:, L["ident"]:L["ident"] + 128]
        onesB = cB[:, L["ones"]:L["ones"] + 128]
        eps_col = cF[:, 770:771]

        def fcol(name, i=0):
            c = CF[name] + i
            return cF[:, c:c + 1]

        def pcol(name, i=0, rows=128):
            c = PRM[name] + i
            return prm[0:rows, c:c + 1]

        S.begin_stage()
        h_sb = S.sbuf("h_sb", [128, KC, TQ], BF16)
        WL = WLoader(S, nbuf=3)
        psr = Rot([S.psum("s1ps%d" % i, [128, TQ]) for i in range(4)])
        ps_sw = S.psum("s1sw", [128, TQ])
        f32r = Rot([S.sbuf("s1f%d" % i, [128, TQ], F32) for i in range(4)])
        b16r = Rot([S.sbuf("s1b%d" % i, [128, TQ], BF16) for i in range(3)])
        posi = S.sbuf("posi", [128, TQ], I32)
        ang = S.sbuf("ang", [128, TQ], F32)
        kf = S.sbuf("kf", [128, TQ], F32)
        ki = S.sbuf("ki", [128, TQ], I32)
        rr = S.sbuf("rr", [128, TQ], F32)
        rc = S.sbuf("rc", [128, TQ], F32)
        fx = S.sbuf("fx", [128, TQ], F32)
        cos_t = S.sbuf("cos_t", [128, TQ], F32)
        sin_t = S.sbuf("sin_t", [128, TQ], F32)
        xr = S.sbuf("xr", [128, TQ], F32)
        t1 = S.sbuf("t1", [128, TQ], F32)
        t2 = S.sbuf("t2", [128, TQ], F32)
        cc_sb = S.sbuf("cc_sb", [128, 4, TQ], F32)
        vt_sb = Rot([S.sbuf("vt%d" % i, [128, 256], BF16) for i in range(2)])

        def wrap_pi(r):
            S.op("dve", lambda h: h.tensor_scalar(out=fx[:, :], in0=r[:, :], scalar1=float(np.pi), scalar2=-TWO_PI,
                                                  op0=ALU.is_gt, op1=ALU.mult), reads=[r], writes=[fx])
            S.op("dve", lambda h: h.tensor_tensor(out=r[:, :], in0=r[:, :], in1=fx[:, :], op=ALU.add), reads=[r, fx], writes=[r])
            S.op("dve", lambda h: h.tensor_scalar(out=r[:, :], in0=r[:, :], scalar1=-PI_IN, scalar2=PI_IN,
                                                  op0=ALU.max, op1=ALU.min), reads=[r], writes=[r])

        for tt in range(nQT):
            t0 = tt * TQ
            for (c_lo, c_hi, h_ap, h_tt) in hT_fn(tt):
                S.dma(h_sb[:, :, c_lo:c_hi], h_ap, reads=[h_tt], writes=[h_sb])
            S.dma(posi[:, :], pos.t[0:1, t0:t0 + TQ].partition_broadcast(128), reads=[pos], writes=[posi])
            S.op("dve", lambda h: h.tensor_copy(out=ang[:, :], in_=posi[:, :]), reads=[posi], writes=[ang])
            S.op("dve", lambda h: h.tensor_scalar(out=ang[:, :], in0=ang[:, :], scalar1=fcol("invf"), scalar2=None, op0=ALU.mult),
                 reads=[ang, cF], writes=[ang])
            S.op("dve", lambda h: h.tensor_scalar(out=kf[:, :], in0=ang[:, :], scalar1=float(1.0 / TWO_PI), scalar2=None, op0=ALU.mult),
                 reads=[ang], writes=[kf])
            S.op("dve", lambda h: h.tensor_copy(out=ki[:, :], in_=kf[:, :]), reads=[kf], writes=[ki])
            S.op("dve", lambda h: h.tensor_copy(out=kf[:, :], in_=ki[:, :]), reads=[ki], writes=[kf])
            C1 = 6.28125
            C2 = float(TWO_PI - 6.28125)
            S.op("dve", lambda h: h.scalar_tensor_tensor(out=rr[:, :], in0=kf[:, :], scalar=-C1, in1=ang[:, :], op0=ALU.mult, op1=ALU.add),
                 reads=[kf, ang], writes=[rr])
            S.op("dve", lambda h: h.scalar_tensor_tensor(out=rr[:, :], in0=kf[:, :], scalar=-C2, in1=rr[:, :], op0=ALU.mult, op1=ALU.add),
                 reads=[kf, rr], writes=[rr])
            S.op("dve", lambda h: h.tensor_scalar(out=fx[:, :], in0=rr[:, :], scalar1=float(-np.pi), scalar2=TWO_PI,
                                                  op0=ALU.is_lt, op1=ALU.mult), reads=[rr], writes=[fx])
            S.op("dve", lambda h: h.tensor_tensor(out=rr[:, :], in0=rr[:, :], in1=fx[:, :], op=ALU.add), reads=[rr, fx], writes=[rr])
            S.op("dve", lambda h: h.tensor_scalar(out=rc[:, :], in0=rr[:, :], scalar1=float(np.pi / 2), scalar2=None, op0=ALU.add),
                 reads=[rr], writes=[rc])
            wrap_pi(rr)
            wrap_pi(rc)
            S.op("act", lambda h: h.activation(out=sin_t[:, :], in_=rr[:, :], func=AF.Sin, scale=fcol("sgn")), reads=[rr, cF], writes=[sin_t])
            S.op("act", lambda h: h.activation(out=cos_t[:, :], in_=rc[:, :], func=AF.Sin), reads=[rc], writes=[cos_t])
            for ci, nm in enumerate(FM):
                wb = WL.load(wcat, wcat.t[:, ci * 128:(ci + 1) * 128].rearrange("(kc p) n -> p kc n", p=128))
                ps = psr.get()
                mm(S, ps, ps[:, :], [(wb[:, k, :], h_sb[:, k, :]) for k in range(KC)], reads=[wb, h_sb])
                if nm in ("q0", "q1", "q2", "q3", "kc", "ks", "kw"):
                    S.op("act", lambda h, ps=ps: h.activation(out=xr[:, :], in_=ps[:, :], func=AF.Copy), reads=[ps], writes=[xr])
                    mm(S, ps_sw, ps_sw[:, :], [(cF[:, 384:512], xr[:, :])], reads=[cF, xr])
                    S.op("dve", lambda h: h.tensor_tensor(out=t1[:, :], in0=xr[:, :], in1=cos_t[:, :], op=ALU.mult), reads=[xr, cos_t], writes=[t1])
                    S.op("dve", lambda h: h.tensor_tensor(out=t2[:, :], in0=ps_sw[:, :], in1=sin_t[:, :], op=ALU.mult), reads=[ps_sw, sin_t], writes=[t2])
                    ob = b16r.get()
                    S.op("dve", lambda h, ob=ob: h.tensor_tensor(out=ob[:, :], in0=t1[:, :], in1=t2[:, :], op=ALU.add), reads=[t1, t2], writes=[ob])
                    if nm[0] == "q":
                        S.dma(qT_d.t[int(nm[1]), :, t0:t0 + TQ], ob[:, :], reads=[ob], writes=[qT_d])
                    else:
                        S.dma(kT_d[nm].t[:, t0:t0 + TQ], ob[:, :], reads=[ob], writes=[kT_d[nm]])
                elif nm == "vc":
                    ob = b16r.get()
                    S.op("act", lambda h, ps=ps, ob=ob: h.activation(out=ob[:, :], in_=ps[:, :], func=AF.Copy), reads=[ps], writes=[ob])
                    S.dma(kT_d["vc"].t[:, t0:t0 + TQ], ob[:, :], reads=[ob], writes=[kT_d["vc"]])
                elif nm.startswith("cc"):
                    c = int(nm[2])
                    S.op("act", lambda h, ps=ps, c=c: h.activation(out=cc_sb[:, c, :], in_=ps[:, :], func=AF.Copy), reads=[ps], writes=[cc_sb])
                elif nm[0] == "u":
                    c = int(nm[1])
                    of = f32r.get()
                    S.op("dve", lambda h, ps=ps, c=c, of=of: h.tensor_tensor(out=of[:, :], in0=ps[:, :], in1=cc_sb[:, c, :], op=ALU.mult),
                         reads=[ps, cc_sb], writes=[of])
                    S.dma(cu_d.t[c * 128:(c + 1) * 128, t0:t0 + TQ], of[:, :], reads=[of], writes=[cu_d])
                else:
                    of = f32r.get()
                    fn = AF.Sigmoid if nm.startswith("gm") else AF.Copy
                    S.op("act", lambda h, ps=ps, of=of, fn=fn: h.activation(out=of[:, :], in_=ps[:, :], func=fn), reads=[ps], writes=[of])
                    if nm.startswith("cb"):
                        dst, c = cb_d, int(nm[2])
                    elif nm[0] == "z":
                        dst, c = z_d, int(nm[1])
                    elif nm.startswith("xb"):
                        dst, c = xb_d, int(nm[2])
                    else:
                        dst, c = gm_d, int(nm[2:])
                    S.dma(dst.t[c * 128:(c + 1) * 128, t0:t0 + TQ], of[:, :], reads=[of], writes=[dst])
            for (c0, ncol, dst, fn) in ((COL_GN, 12, gn_d, AF.Sigmoid), (COL_DT, 8, dt_d, AF.Copy)):
                wb = WL.load(wcat, wcat.t[:, c0:c0 + ncol].rearrange("(kc p) n -> p kc n", p=128), ncol=ncol)
                ps = psr.get()
                mm(S, ps, ps[0:ncol, :], [(wb[:, k, 0:ncol], h_sb[:, k, :]) for k in range(KC)], reads=[wb, h_sb])
                of = f32r.get()
                S.op("act", lambda h, ps=ps, of=of, fn=fn, ncol=ncol: h.activation(out=of[0:ncol, :], in_=ps[0:ncol, :], func=fn), reads=[ps], writes=[of])
                S.dma(dst.t[:, t0:t0 + TQ], of[0:ncol, :], reads=[of], writes=[dst])
            wv = [WL.load(wcat, wcat.t[:, COL_V + i * 128:COL_V + (i + 1) * 128].rearrange("(kc p) n -> p kc n", p=128)) for i in range(2)]
            for s4 in range(4):
                ps = psr.get()
                for i in range(2):
                    mm(S, ps, ps[:, i * 128:(i + 1) * 128], [(h_sb[:, k, s4 * 128:(s4 + 1) * 128], wv[i][:, k, :]) for k in range(KC)],
                       reads=[wv[i], h_sb])
                vt = vt_sb.get()
                S.op("act", lambda h, ps=ps, vt=vt: h.activation(out=vt[:, :], in_=ps[:, 0:256], func=AF.Copy), reads=[ps], writes=[vt])
                S.dma(vtok_d.t[t0 + s4 * 128:t0 + (s4 + 1) * 128, :], vt[:, :], reads=[vt], writes=[vtok_d])
        S.end_stage()
        build_p1_rest(nc, S, locals())
        S.pop_scope()


def build_p1_rest(nc, S, V):
    T, nQT, NKT, NCMP, L, scale = V["T"], V["nQT"], V["NKT"], V["NCMP"], V["L"], V["scale"]
    cF, cB, prm = V["cF"], V["cB"], V["prm"]
    identF, identB, onesB, eps_col = V["identF"], V["identB"], V["onesB"], V["eps_col"]
    fcol, pcol = V["fcol"], V["pcol"]
    qT_d, kT_d, vtok_d = V["qT_d"], V["kT_d"], V["vtok_d"]
    cb_d, cu_d, z_d, xb_d, gm_d, gn_d, dt_d = V["cb_d"], V["cu_d"], V["z_d"], V["xb_d"], V["gm_d"], V["gn_d"], V["dt_d"]
    mA_d, mB_d, mC_d, mT = V["mA_d"], V["mB_d"], V["mC_d"], V["mT"]
    peT_d, w1_d, w2_d, vb2_d = V["peT_d"], V["w1_d"], V["w2_d"], V["vb2_d"]

    def bc(ap, shape):
        return ap.to_broadcast(list(shape))

    S.begin_stage()
    cur = Rot([S.sbuf("cu%d" % i, [128, TQ + 2], F32) for i in range(2)])
    cbr = Rot([S.sbuf("cbt%d" % i, [128, TQ], F32) for i in range(2)])
    gmr = Rot([S.sbuf("gmt%d" % i, [128, TQ], F32) for i in range(2)])
    acr = Rot([S.sbuf("acc%d" % i, [128, TQ], F32) for i in range(2)])
    for tt in range(nQT):
        t0 = tt * TQ
        for c in range(4):
            cu, cbt, gmt, acc = cur.get(), cbr.get(), gmr.get(), acr.get()
            rows = slice(c * 128, (c + 1) * 128)
            if tt == 0:
                S.op("dve", lambda h, cu=cu: h.memset(cu[:, 0:2], 0.0), writes=[cu])
                S.dma(cu[:, 2:TQ + 2], cu_d.t[rows, 0:TQ], reads=[cu_d], writes=[cu])
            else:
                S.dma(cu[:, :], cu_d.t[rows, t0 - 2:t0 + TQ], reads=[cu_d], writes=[cu])
            S.dma(cbt[:, :], cb_d.t[rows, t0:t0 + TQ], reads=[cb_d], writes=[cbt])
            S.dma(gmt[:, :], gm_d.t[(4 + c) * 128:(5 + c) * 128, t0:t0 + TQ], reads=[gm_d], writes=[gmt])
            S.op("dve", lambda h, cu=cu, acc=acc, c=c: h.tensor_scalar(out=acc[:, :], in0=cu[:, 0:TQ], scalar1=pcol("scw", c * 3), scalar2=None, op0=ALU.mult),
                 reads=[cu, prm], writes=[acc])
            for j in (1, 2):
                S.op("dve", lambda h, cu=cu, acc=acc, c=c, j=j: h.scalar_tensor_tensor(
                    out=acc[:, :], in0=cu[:, j:j + TQ], scalar=pcol("scw", c * 3 + j), in1=acc[:, :], op0=ALU.mult, op1=ALU.add),
                    reads=[cu, prm, acc], writes=[acc])
            S.op("dve", lambda h, acc=acc, cbt=cbt: h.tensor_tensor(out=acc[:, :], in0=acc[:, :], in1=cbt[:, :], op=ALU.mult), reads=[acc, cbt], writes=[acc])
            S.op("dve", lambda h, acc=acc, gmt=gmt: h.tensor_tensor(out=acc[:, :], in0=acc[:, :], in1=gmt[:, :], op=ALU.mult), reads=[acc, gmt], writes=[acc])
            S.dma(mB_d.t[rows, t0:t0 + TQ], acc[:, :], reads=[acc], writes=[mB_d])
    S.end_stage()

    S.begin_stage()
    CH = 128
    xbr = Rot([S.sbuf("xb%d" % i, [128, 6, CH + 3], F32) for i in range(2)])
    xc = S.sbuf("xc", [128, 6, CH], F32)
    acc6 = S.sbuf("acc6", [128, 6, CH], F32)
    BT = S.sbuf("BT", [128, CH], BF16)
    CT = S.sbuf("CT", [128, CH], BF16)
    Bk = S.sbuf("Bk", [128, CH], BF16)
    dtr = S.sbuf("dtr", [8, CH], F32)
    dtT = S.sbuf("dtT", [8, CH], F32)
    daT = S.sbuf("daT", [8, CH], F32)
    nA = S.sbuf("nA", [8, 2], F32)
    dtk = S.sbuf("dtk", [128, 8], F32)
    dak = S.sbuf("dak", [128, 8], F32)
    nacol = S.sbuf("nacol", [128, 8], F32)
    alast = S.sbuf("alast", [128, 8], F32)
    decay = S.sbuf("decay", [128, 8], F32)
    wcol = S.sbuf("wcol", [128, 8], F32)
    darep = S.sbuf("darep", [128, 8, CH], F32)
    diffm = S.sbuf("diffm", [128, 8, CH], F32)
    seg = S.sbuf("seg", [128, 8, CH], F32)
    ea = S.sbuf("ea", [128, 8, CH], F32)
    G = S.sbuf("G", [128, 8, CH], BF16)
    Cexp = S.sbuf("Cexp", [128, 8, CH], BF16)
    xdt32 = S.sbuf("xdt32", [128, 8, 64], F32)
    xdtw = S.sbuf("xdtw", [128, 8, 64], BF16)
    xdt_pad = S.sbuf("xdt_pad", [128, 8, 128], BF16)
    S_pad = S.sbuf("S_pad", [128, 8, 128], BF16)
    S32 = S.sbuf("S32", [128, 8, 64], F32)
    zr = Rot([S.sbuf("zs%d" % i, [128, 4, CH], F32) for i in range(2)])
    g2r = Rot([S.sbuf("g2s%d" % i, [128, 4, CH], F32) for i in range(2)])
    sz = S.sbuf("sz", [128, 4, CH], F32)
    yv = S.sbuf("yv", [128, 4, CH], F32)
    sq4 = S.sbuf("sq4", [128, 4, CH], BF16)
    rs = S.sbuf("rs", [128, CH], F32)
    ycr = Rot([S.sbuf("yc%d" % i, [128, 4, CH], F32) for i in range(2)])
    p_t = S.psum("p_t", [128, TQ])
    p_t2 = S.psum("p_t2", [128, TQ])
    p_ar = S.psum("p_ar", [128, 1024])
    p_cb = S.psum("p_cb", [128, TQ])
    p_y = S.psum("p_y", [128, TQ])
    p_cs = S.psum("p_cs", [128, TQ])
    p_bt = S.psum("p_bt", [128, 256], BF16)
    Umat = cF[:, 128:256]
    ssdneg = cF[:, 256:384]
    S.op("dve", lambda h: h.memset(xdt_pad[:, :, :], 0.0), writes=[xdt_pad])
    S.op("dve", lambda h: h.memset(S_pad[:, :, :], 0.0), writes=[S_pad])
    S.op("dve", lambda h: h.memset(S32[:, :, :], 0.0), writes=[S32])
    S.op("act", lambda h: h.activation(out=nA[:, 0:1], in_=pcol("alog", rows=8), func=AF.Exp), reads=[prm], writes=[nA])
    S.op("dve", lambda h: h.tensor_scalar(out=nA[:, 1:2], in0=nA[:, 0:1], scalar1=-1.0, scalar2=None, op0=ALU.mult), reads=[nA], writes=[nA])
    ar3 = p_ar.t[:, :].rearrange("p (h l) -> p h l", h=8)

    def pad_copy(dst, src32):
        d5 = dst.t[:, :, :].rearrange("p (a e) (s q) -> p a e s q", e=2, s=2)
        s4 = src32.t[:, :, :].rearrange("p (a e) q -> p a e q", e=2)
        for e in range(2):
            S.op("dve", lambda h, e=e: h.tensor_copy(out=d5[:, :, e, e, :], in_=s4[:, :, e, :]), reads=[src32], writes=[dst])

    for ch in range(T // CH):
        t0 = ch * CH
        xb = xbr.get()
        src = xb_d.t.rearrange("(c p) t -> p c t", p=128)
        if ch == 0:
            S.op("dve", lambda h, xb=xb: h.memset(xb[:, :, 0:3], 0.0), writes=[xb])
            S.dma(xb[:, :, 3:CH + 3], src[:, :, 0:CH], reads=[xb_d], writes=[xb])
        else:
            S.dma(xb[:, :, :], src[:, :, t0 - 3:t0 + CH], reads=[xb_d], writes=[xb])
        S.dma(dtr[:, :], dt_d.t[:, t0:t0 + CH], reads=[dt_d], writes=[dtr])
        zs, g2s = zr.get(), g2r.get()
        S.dma(zs[:, :, :], z_d.t.rearrange("(c p) t -> p c t", p=128)[:, :, t0:t0 + CH], reads=[z_d], writes=[zs])
        S.dma(g2s[:, :, :], gm_d.t[1024:1536, :].rearrange("(c p) t -> p c t", p=128)[:, :, t0:t0 + CH], reads=[gm_d], writes=[g2s])
        for c in range(6):
            S.op("dve", lambda h, xb=xb, c=c: h.tensor_scalar(out=acc6[:, c, :], in0=xb[:, c, 0:CH], scalar1=pcol("sdw", c * 4), scalar2=None, op0=ALU.mult),
                 reads=[xb, prm], writes=[acc6])
            for j in (1, 2, 3):
                S.op("dve", lambda h, xb=xb, c=c, j=j: h.scalar_tensor_tensor(
                    out=acc6[:, c, :], in0=xb[:, c, j:j + CH], scalar=pcol("sdw", c * 4 + j), in1=acc6[:, c, :], op0=ALU.mult, op1=ALU.add),
                    reads=[xb, prm, acc6], writes=[acc6])
            S.op("act", lambda h, c=c: h.activation(out=xc[:, c, :], in_=acc6[:, c, :], func=AF.Silu, bias=pcol("sdb", c)), reads=[acc6, prm], writes=[xc])
        S.op("dve", lambda h: h.tensor_copy(out=BT[:, :], in_=xc[:, 4, :]), reads=[xc], writes=[BT])
        S.op("dve", lambda h: h.tensor_copy(out=CT[:, :], in_=xc[:, 5, :]), reads=[xc], writes=[CT])
        S.op("act", lambda h: h.activation(out=dtT[:, :], in_=dtr[:, :], func=AF.Exp, bias=pcol("dtb", rows=8)), reads=[dtr, prm], writes=[dtT])
        S.op("act", lambda h: h.activation(out=dtT[:, :], in_=dtT[:, :], func=AF.Ln, bias=cF[0:8, 771:772]), reads=[dtT, cF], writes=[dtT])
        S.op("dve", lambda h: h.tensor_scalar(out=daT[:, :], in0=dtT[:, :], scalar1=nA[:, 1:2], scalar2=None, op0=ALU.mult), reads=[dtT, nA], writes=[daT])
        S.op("pe", lambda h: h.transpose(out=p_t[:, 0:8], in_=dtT[:, :], identity=cF[0:8, 0:8]), reads=[dtT, cF], writes=[p_t])
        S.op("pe", lambda h: h.transpose(out=p_t[:, 8:16], in_=daT[:, :], identity=cF[0:8, 0:8]), reads=[daT, cF], writes=[p_t])
        S.op("dve", lambda h: h.tensor_copy(out=dtk[:, :], in_=p_t[:, 0:8]), reads=[p_t], writes=[dtk])
        S.op("dve", lambda h: h.tensor_copy(out=dak[:, :], in_=p_t[:, 8:16]), reads=[p_t], writes=[dak])
        mm(S, p_t, p_t[:, 16:24], [(Umat, dak[:, :])], reads=[cF, dak])
        S.op("dve", lambda h: h.tensor_copy(out=darep[:, :, :], in_=bc(dak[:, 0:8].unsqueeze(2), [128, 8, CH])), reads=[dak], writes=[darep])
        for hd in range(8):
            mm(S, p_ar, ar3[:, hd, :], [(darep[:, hd, :], Umat)], reads=[darep, cF])
        S.op("dve", lambda h: h.tensor_scalar(out=nacol[:, :], in0=p_t[:, 16:24], scalar1=-1.0, scalar2=None, op0=ALU.mult), reads=[p_t], writes=[nacol])
        S.op("dve", lambda h: h.tensor_copy(out=alast[:, :], in_=ar3[:, :, CH - 1]), reads=[p_ar], writes=[alast])
        S.op("act", lambda h: h.activation(out=decay[:, :], in_=alast[:, :], func=AF.Exp), reads=[alast], writes=[decay])
        S.op("dve", lambda h: h.tensor_tensor(out=wcol[:, :], in0=alast[:, :], in1=nacol[:, :], op=ALU.add), reads=[alast, nacol], writes=[wcol])
        S.op("act", lambda h: h.activation(out=wcol[:, :], in_=wcol[:, :], func=AF.Exp), reads=[wcol], writes=[wcol])
        for c in range(4):
            S.op("pe", lambda h, c=c: h.transpose(out=p_t2[:, c * 128:(c + 1) * 128], in_=xc[:, c, :], identity=identF), reads=[xc, cF], writes=[p_t2])
        pt3 = p_t2.t[:, :].rearrange("p (h q) -> p h q", h=8)
        S.op("dve", lambda h: h.tensor_tensor(out=xdt32[:, :, :], in0=pt3, in1=bc(dtk[:, 0:8].unsqueeze(2), [128, 8, 64]), op=ALU.mult),
             reads=[p_t2, dtk], writes=[xdt32])
        pad_copy(xdt_pad, xdt32)
        S.op("dve", lambda h: h.tensor_tensor(out=xdtw[:, :, :], in0=xdt32[:, :, :], in1=bc(wcol[:, 0:8].unsqueeze(2), [128, 8, 64]), op=ALU.mult),
             reads=[xdt32, wcol], writes=[xdtw])
        S.op("pe", lambda h: h.transpose(out=p_bt[:, 0:128], in_=BT[:, :], identity=identB), reads=[BT, cB], writes=[p_bt])
        S.op("dve", lambda h: h.tensor_copy(out=Bk[:, :], in_=p_bt[:, 0:128]), reads=[p_bt], writes=[Bk])
        mm(S, p_cb, p_cb[:, 0:CH], [(BT[:, :], CT[:, :])], reads=[BT, CT])
        S.op("dve", lambda h: h.tensor_tensor(out=diffm[:, :, :], in0=ar3, in1=bc(ssdneg.unsqueeze(1), [128, 8, CH]), op=ALU.add),
             reads=[p_ar, cF], writes=[diffm])
        for hd in range(8):
            S.op("act", lambda h, hd=hd: h.activation(out=seg[:, hd, :], in_=diffm[:, hd, :], func=AF.Exp, bias=nacol[:, hd:hd + 1]),
                 reads=[diffm, nacol], writes=[seg])
        S.op("dve", lambda h: h.tensor_tensor(out=G[:, :, :], in0=seg[:, :, :], in1=bc(p_cb[:, 0:CH].unsqueeze(1), [128, 8, CH]), op=ALU.mult),
             reads=[seg, p_cb], writes=[G])
        for half in range(2):
            S.op("act", lambda h, half=half: h.activation(out=ea[:, half * 4:(half + 1) * 4, :], in_=ar3[:, half * 4:(half + 1) * 4, :], func=AF.Exp),
                 reads=[p_ar], writes=[ea])
        S.op("dve", lambda h: h.tensor_tensor(out=Cexp[:, :, :], in0=ea[:, :, :], in1=bc(CT[:, :].unsqueeze(1), [128, 8, CH]), op=ALU.mult),
             reads=[ea, CT], writes=[Cexp])
        for c in range(4):
            pairs = []
            for e in range(2):
                pairs.append((xdt_pad[:, 2 * c + e, :], G[:, 2 * c + e, :]))
                pairs.append((S_pad[:, 2 * c + e, :], Cexp[:, 2 * c + e, :]))
            mm(S, p_y, p_y[:, c * 128:(c + 1) * 128], pairs, reads=[xdt_pad, G, S_pad, Cexp])
        py3 = p_y.t[:, :].rearrange("p (c l) -> p c l", c=4)
        for c in range(4):
            S.op("dve", lambda h, c=c: h.scalar_tensor_tensor(out=yv[:, c, :], in0=xc[:, c, :], scalar=pcol("dsk", c), in1=py3[:, c, :],
                                                              op0=ALU.mult, op1=ALU.add), reads=[xc, prm, p_y], writes=[yv])
        mm(S, p_cs, p_cs[:, :], [(Bk[:, :], xdtw[:, :, :].rearrange("p h q -> p (h q)"))], reads=[Bk, xdtw])
        S.op("dve", lambda h: h.tensor_tensor(out=S32[:, :, :], in0=S32[:, :, :], in1=bc(decay[:, 0:8].unsqueeze(2), [128, 8, 64]), op=ALU.mult),
             reads=[S32, decay], writes=[S32])
        S.op("dve", lambda h: h.tensor_tensor(out=S32[:, :, :], in0=S32[:, :, :], in1=p_cs.t[:, :].rearrange("p (h q) -> p h q", h=8), op=ALU.add),
             reads=[S32, p_cs], writes=[S32])
        pad_copy(S_pad, S32)
        S.op("act", lambda h, zs=zs: h.activation(out=sz[:, :, :], in_=zs[:, :, :], func=AF.Silu), reads=[zs], writes=[sz])
        S.op("dve", lambda h: h.tensor_tensor(out=yv[:, :, :], in0=yv[:, :, :], in1=sz[:, :, :], op=ALU.mult), reads=[yv, sz], writes=[yv])
        S.op("act", lambda h: h.activation(out=sq4[:, :, :], in_=yv[:, :, :], func=AF.Square), reads=[yv], writes=[sq4])
        mm(S, p_cb, p_cb[:, 128:256], [(onesB, sq4[:, c, :]) for c in range(4)], reads=[cB, sq4])
        S.op("act", lambda h: h.activation(out=rs[:, :], in_=p_cb[:, 128:256], func=AF.Ln, bias=eps_col, scale=1.0 / 512), reads=[p_cb, cF], writes=[rs])
        S.op("act", lambda h: h.activation(out=rs[:, :], in_=rs[:, :], func=AF.Exp, scale=-0.5), reads=[rs], writes=[rs])
        yc = ycr.get()
        for c in range(4):
            S.op("dve", lambda h, c=c, yc=yc: h.scalar_tensor_tensor(out=yc[:, c, :], in0=yv[:, c, :], scalar=pcol("ng", c), in1=rs[:, :],
                                                                     op0=ALU.mult, op1=ALU.mult), reads=[yv, prm, rs], writes=[yc])
        S.op("dve", lambda h, yc=yc, g2s=g2s: h.tensor_tensor(out=yc[:, :, :], in0=yc[:, :, :], in1=g2s[:, :, :], op=ALU.mult), reads=[yc, g2s], writes=[yc])
        S.dma(mC_d.t.rearrange("(c p) t -> p c t", p=128)[:, :, t0:t0 + CH], yc[:, :, :], reads=[yc], writes=[mC_d])
    S.end_stage()
    build_p1_nsa(nc, S, V)


def build_p1_nsa(nc, S, V):
    T, nQT, NKT, NCMP, L, scale = V["T"], V["nQT"], V["NKT"], V["NCMP"], V["L"], V["scale"]
    cF, cB, prm = V["cF"], V["cB"], V["prm"]
    identF, identB, onesB, eps_col = V["identF"], V["identB"], V["onesB"], V["eps_col"]
    fcol, pcol = V["fcol"], V["pcol"]
    qT_d, kT_d, vtok_d = V["qT_d"], V["kT_d"], V["vtok_d"]
    gm_d, gn_d = V["gm_d"], V["gn_d"]
    mA_d, mB_d, mC_d, mT = V["mA_d"], V["mB_d"], V["mC_d"], V["mT"]
    peT_d, w1_d, w2_d, vb2_d = V["peT_d"], V["w1_d"], V["w2_d"], V["vb2_d"]
    NC4 = (NCMP + 127) // 128
    NCP = NC4 * 128

    def cmask(di):
        return cB[:, L["cmask"] + di * TQ:L["cmask"] + (di + 1) * TQ]

    def wmask(j):
        return cB[:, L["wmask"] + j * TQ:L["wmask"] + (j + 1) * TQ]

    def ebig(kt):
        return cB[:, L["ebig"] + kt * 128:L["ebig"] + (kt + 1) * 128]

    def ovl1(i):
        return cB[:, L["ovl1"] + i * 129:L["ovl1"] + (i + 1) * 129]

    def onesel(r):
        return cB[:, L["onesel12"] + r * 12:L["onesel12"] + (r + 1) * 12]

    def sel12(r):
        return cB[0:12, L["sel12"] + r * 128:L["sel12"] + (r + 1) * 128]

    S.begin_stage()
    srcT = {n: S.sbuf(n + "T", [128, T], BF16) for n in ("kc", "ks", "kw")}
    srcT["vc"] = srcT["kc"]
    vs = S.sbuf("vs", [128, NKT, 128], BF16)
    vw = S.sbuf("vw", [128, NKT, 128], BF16)
    kcmpT = S.sbuf("kcmpT", [128, NCP], BF16)
    vcmp = S.sbuf("vcmp", [128, NC4, 128], BF16)
    for n in ("ks", "kw"):
        S.dma(srcT[n][:, :], kT_d[n].t[:, :], reads=[kT_d[n]], writes=[srcT[n]])
    vt3 = vtok_d.t.rearrange("(kt p) d -> p kt d", p=128)
    S.dma(vs[:, :, :], vt3[:, :, 0:128], reads=[vtok_d], writes=[vs])
    S.dma(vw[:, :, :], vt3[:, :, 128:256], reads=[vtok_d], writes=[vw])
    w1st = S.sbuf("w1st", [128, 32, 128], F32)
    w1bf = S.sbuf("w1bf", [128, 32, 128], BF16)
    w2st = S.sbuf("w2st", [128, 128], F32)
    w2bf = S.sbuf("w2bf", [128, 128], BF16)
    pest = S.sbuf("pest", [128, 64], F32)
    pebf = S.sbuf("pebf", [128, 64], BF16)
    vb2s = S.sbuf("vb2s", [1, 128], F32)
    vb2b = S.sbuf("vb2b", [1, 128], BF16)
    btot = S.sbuf("btot", [128, 1], F32)
    hs = S.sbuf("hs", [128, NCP], BF16)
    p_sc = Rot([S.psum("p_sc%d" % i, [128, TQ]) for i in range(2)])
    p_o = [S.psum("p_o%d" % i, [128, TQ]) for i in range(3)]
    p_den = S.psum("p_den", [128, TQ])
    p_u = S.psum("p_u", [128, TQ])
    p_x = S.psum("p_x", [128, TQ])
    S.dma(pest[:, :], peT_d[:, :], reads=[peT_d], writes=[pest])
    S.op("dve", lambda h: h.tensor_copy(out=pebf[:, :], in_=pest[:, :]), reads=[pest], writes=[pebf])
    S.dma(vb2s[:, :], vb2_d[:, :], reads=[vb2_d], writes=[vb2s])
    S.op("dve", lambda h: h.tensor_copy(out=vb2b[:, :], in_=vb2s[:, :]), reads=[vb2s], writes=[vb2b])
    S.op("dve", lambda h: h.memset(kcmpT[:, :], 0.0), writes=[kcmpT])
    for wi, nm in enumerate(("kc", "vc")):
        S.dma(w1st[:, :, :], w1_d[wi].t.rearrange("(j d) h -> d j h", d=128), reads=[w1_d[wi]], writes=[w1st])
        S.op("pool", lambda h: h.tensor_copy(out=w1bf[:, :, :], in_=w1st[:, :, :]), reads=[w1st], writes=[w1bf])
        S.dma(w2st[:, :], w2_d[wi].t[:, :], reads=[w2_d[wi]], writes=[w2st])
        S.op("dve", lambda h: h.tensor_copy(out=w2bf[:, :], in_=w2st[:, :]), reads=[w2st], writes=[w2bf])
        S.op("dve", lambda h: h.memset(hs[:, :], 0.0), writes=[hs])
        mm(S, p_x, p_x[:, 0:1], [(w1bf[:, j, :], pebf[:, wi * 32 + j:wi * 32 + j + 1]) for j in range(32)], reads=[w1bf, pebf])
        S.op("dve", lambda h, wi=wi: h.tensor_tensor(out=btot[:, :], in0=p_x[:, 0:1], in1=pcol("kb1" if wi == 0 else "vb1"), op=ALU.add),
             reads=[p_x, prm], writes=[btot])
        src = srcT[nm]
        S.dma(src[:, :], kT_d[nm].t[:, :], reads=[kT_d[nm]], writes=[src])
        for n0 in range(0, NCMP, 512):
            nn = min(512, NCMP - n0)
            ps = p_sc.get()
            mm(S, ps, ps[:, 0:nn], [(w1bf[:, j, :], src[:, 16 * n0 + j:16 * n0 + j + 16 * (nn - 1) + 1:16]) for j in range(32)], reads=[w1bf, src])
            S.op("act", lambda h, ps=ps, n0=n0, nn=nn: h.activation(out=hs[:, n0:n0 + nn], in_=ps[:, 0:nn], func=AF.Silu, bias=btot[:, 0:1]),
                 reads=[ps, btot], writes=[hs])
            if wi == 0:
                ps2 = p_sc.get()
                mm(S, ps2, ps2[:, 0:nn], [(w2bf[:, :], hs[:, n0:n0 + nn])], reads=[w2bf, hs])
                S.op("act", lambda h, ps2=ps2, n0=n0, nn=nn: h.activation(out=kcmpT[:, n0:n0 + nn], in_=ps2[:, 0:nn], func=AF.Identity, bias=pcol("kb2")),
                     reads=[ps2, prm], writes=[kcmpT])
        if wi == 1:
            for i in range(NC4):
                ps3 = p_sc.get()
                mm(S, ps3, ps3[:, 0:128], [(hs[:, 128 * i:128 * i + 128], w2bf[:, :]), (cB[0:1, L["ones"]:L["ones"] + 128], vb2b[0:1, :])],
                   reads=[hs, w2bf, cB, vb2b])
                S.op("act", lambda h, ps3=ps3, i=i: h.activation(out=vcmp[:, i, :], in_=ps3[:, 0:128], func=AF.Copy), reads=[ps3], writes=[vcmp])
    q_sb = S.sbuf("q_sb", [128, 4, TQ], BF16)
    gn12 = S.sbuf("gn12", [12, TQ], F32)
    gm0 = S.sbuf("gm0", [128, 4, TQ], F32)
    ec = [S.sbuf("ec%d" % i, [128, TQ], BF16) for i in range(NC4)]
    er = Rot([S.sbuf("er%d" % i, [128, TQ], BF16) for i in range(3)])
    o_sb = [[S.sbuf("o%d_%d" % (b, h), [128, TQ], F32) for h in range(4)] for b in range(3)]
    imp = S.sbuf("imp", [128, 4, 128], F32)
    rd = S.sbuf("rd", [128, 1], F32)
    sc2 = S.sbuf("sc2", [128, 128], F32)
    sc3 = S.sbuf("sc3", [128, 128], F32)
    m8a = S.sbuf("m8a", [128, 8], F32)
    m8b = S.sbuf("m8b", [128, 8], F32)
    negm = S.sbuf("negm", [128, 128], F32)
    negmT = S.sbuf("negmT", [128, TQ], BF16)
    den_sb = S.sbuf("den_sb", [12, TQ], F32)
    fct = S.sbuf("fct", [12, TQ], BF16)
    ya = S.sbuf("ya", [128, TQ], F32)
    tmpm = S.sbuf("tmpm", [128, TQ], F32)

    for qt in range(nQT):
        t0 = qt * TQ
        S.dma(q_sb[:, :, :], qT_d.t.rearrange("h p t -> p h t")[:, :, t0:t0 + TQ], reads=[qT_d], writes=[q_sb])
        S.dma(gn12[:, :], gn_d.t[:, t0:t0 + TQ], reads=[gn_d], writes=[gn12])
        S.dma(gm0[:, :, :], gm_d.t[0:512, :].rearrange("(c p) t -> p c t", p=128)[:, :, t0:t0 + TQ], reads=[gm_d], writes=[gm0])
        ntc = min(NC4, (32 * qt + 30) // 128 + 1)
        sel_kts = list(range(0, 4 * qt + 4))
        win_kts = list(range(max(0, 4 * qt - 4), 4 * qt + 4))
        n_den_total = 4 * (ntc + len(sel_kts) + len(win_kts))
        den_i = [0]

        def den_mm(r, e_t, den_i=den_i, n_den_total=n_den_total):
            i = den_i[0]
            den_i[0] += 1
            mm(S, p_den, p_den[0:12, :], [(onesel(r), e_t[:, :])], reads=[cB, e_t], start=(i == 0), stop=(i == n_den_total - 1))

        for h in range(4):
            for i in range(ntc):
                delta = 2048 * i - 512 * qt
                pairs = [(kcmpT[:, 128 * i:128 * i + 128], q_sb[:, h, :])]
                if -2048 <= delta <= 0:
                    pairs.append((identB, cmask((delta + 2048) // 512)))
                ps = p_sc.get()
                mm(S, ps, ps[:, :], pairs, reads=[kcmpT, q_sb, cB])
                S.op("act", lambda hh, ps=ps, i=i: hh.activation(out=ec[i][:, :], in_=ps[:, :], func=AF.Exp, scale=scale), reads=[ps], writes=[ec[i]])
                mm(S, p_o[0], p_o[0][:, :], [(vcmp[:, i, :], ec[i][:, :])], reads=[vcmp, ec[i]], start=(i == 0), stop=(i == ntc - 1))
                den_mm(3 * h + 0, ec[i])
            S.op("act", lambda hh, h=h: hh.activation(out=o_sb[0][h][:, :], in_=p_o[0][:, :], func=AF.Copy), reads=[p_o[0]], writes=[o_sb[0][h]])
            for s4 in range(4):
                mm(S, p_u, p_u[:, 0:129], [(ec[i][:, s4 * 128:(s4 + 1) * 128], ovl1(i)) for i in range(ntc)], reads=[cB] + ec[0:ntc])
                S.op("dve", lambda hh: hh.tensor_scalar(out=rd[:, :], in0=p_u[:, 128:129], scalar1=1e-30, scalar2=None, op0=ALU.max), reads=[p_u], writes=[rd])
                S.op("dve", lambda hh: hh.reciprocal(out=rd[:, :], in_=rd[:, :]), reads=[rd], writes=[rd])
                if h == 0:
                    S.op("dve", lambda hh, s4=s4: hh.tensor_scalar(out=imp[:, s4, :], in0=p_u[:, 0:128], scalar1=rd[:, 0:1], scalar2=None, op0=ALU.mult),
                         reads=[p_u, rd], writes=[imp])
                else:
                    S.op("dve", lambda hh, s4=s4: hh.scalar_tensor_tensor(out=imp[:, s4, :], in0=p_u[:, 0:128], scalar=rd[:, 0:1], in1=imp[:, s4, :],
                                                                           op0=ALU.mult, op1=ALU.add), reads=[p_u, rd, imp], writes=[imp])
        for s4 in range(4):
            g4 = 4 * qt + s4
            pb0 = CF["pbig"] + 128 - 2 * g4
            S.op("dve", lambda hh, s4=s4, pb0=pb0: hh.tensor_tensor(out=sc2[:, :], in0=imp[:, s4, :], in1=cF[:, pb0:pb0 + 128], op=ALU.add),
                 reads=[imp, cF], writes=[sc2])
            S.op("dve", lambda hh: hh.tensor_scalar(out=sc2[:, 0:1], in0=sc2[:, 0:1], scalar1=1e4, scalar2=None, op0=ALU.add), reads=[sc2], writes=[sc2])
            S.op("dve", lambda hh: hh.max(out=m8a[:, :], in_=sc2[:, :]), reads=[sc2], writes=[m8a])
            S.op("dve", lambda hh: hh.match_replace(out=sc3[:, :], in_to_replace=m8a[:, :], in_values=sc2[:, :], imm_value=-2e9), reads=[sc2, m8a], writes=[sc3])
            S.op("dve", lambda hh: hh.max(out=m8b[:, :], in_=sc3[:, :]), reads=[sc3], writes=[m8b])
            S.op("dve", lambda hh: hh.tensor_scalar(out=negm[:, :], in0=sc2[:, :], scalar1=m8b[:, 7:8], scalar2=NEG, op0=ALU.is_lt, op1=ALU.mult),
                 reads=[sc2, m8b], writes=[negm])
            S.op("pe", lambda hh: hh.transpose(out=p_u[:, 0:128], in_=negm[:, :], identity=identF), reads=[negm, cF], writes=[p_u])
            S.op("dve", lambda hh, s4=s4: hh.tensor_copy(out=negmT[:, s4 * 128:(s4 + 1) * 128], in_=p_u[:, 0:128]), reads=[p_u], writes=[negmT])
        for h in range(4):
            for bi, kts in ((1, sel_kts), (2, win_kts)):
                for ii, kt in enumerate(kts):
                    if bi == 1:
                        pairs = [(srcT["ks"][:, 128 * kt:128 * kt + 128], q_sb[:, h, :]), (ebig(kt), negmT[:, :])]
                        if kt >= 4 * qt:
                            pairs.append((identB, wmask(4 + kt - 4 * qt)))
                        rds = [srcT["ks"], q_sb, cB, negmT]
                        vt = vs
                    else:
                        pairs = [(srcT["kw"][:, 128 * kt:128 * kt + 128], q_sb[:, h, :]), (identB, wmask(kt - (4 * qt - 4)))]
                        rds = [srcT["kw"], q_sb, cB]
                        vt = vw
                    ps = p_sc.get()
                    mm(S, ps, ps[:, :], pairs, reads=rds)
                    e_t = er.get()
                    S.op("act", lambda hh, ps=ps, e_t=e_t: hh.activation(out=e_t[:, :], in_=ps[:, :], func=AF.Exp, scale=scale), reads=[ps], writes=[e_t])
                    mm(S, p_o[bi], p_o[bi][:, :], [(vt[:, kt, :], e_t[:, :])], reads=[vt, e_t], start=(ii == 0), stop=(ii == len(kts) - 1))
                    den_mm(3 * h + bi, e_t)
                S.op("act", lambda hh, h=h, bi=bi: hh.activation(out=o_sb[bi][h][:, :], in_=p_o[bi][:, :], func=AF.Copy), reads=[p_o[bi]], writes=[o_sb[bi][h]])
        assert den_i[0] == n_den_total
        S.op("dve", lambda hh: hh.tensor_scalar(out=den_sb[:, :], in0=p_den[0:12, :], scalar1=1e-30, scalar2=None, op0=ALU.max), reads=[p_den], writes=[den_sb])
        S.op("dve", lambda hh: hh.reciprocal(out=den_sb[:, :], in_=den_sb[:, :]), reads=[den_sb], writes=[den_sb])
        S.op("dve", lambda hh: hh.tensor_tensor(out=fct[:, :], in0=den_sb[:, :], in1=gn12[:, :], op=ALU.mult), reads=[den_sb, gn12], writes=[fct])
        for h in range(4):
            for b in range(3):
                mm(S, p_x, p_x[:, :], [(sel12(3 * h + b), fct[:, :])], reads=[cB, fct])
                if b == 0:
                    S.op("dve", lambda hh, h=h, b=b: hh.tensor_tensor(out=ya[:, :], in0=o_sb[b][h][:, :], in1=p_x[:, :], op=ALU.mult), reads=[o_sb[b][h], p_x], writes=[ya])
                else:
                    S.op("dve", lambda hh, h=h, b=b: hh.tensor_tensor(out=tmpm[:, :], in0=o_sb[b][h][:, :], in1=p_x[:, :], op=ALU.mult), reads=[o_sb[b][h], p_x], writes=[tmpm])
                    S.op("dve", lambda hh: hh.tensor_tensor(out=ya[:, :], in0=ya[:, :], in1=tmpm[:, :], op=ALU.add), reads=[ya, tmpm], writes=[ya])
            S.op("dve", lambda hh, h=h: hh.tensor_tensor(out=ya[:, :], in0=ya[:, :], in1=gm0[:, h, :], op=ALU.mult), reads=[ya, gm0], writes=[ya])
            S.dma(mA_d.t[h * 128:(h + 1) * 128, t0:t0 + TQ], ya[:, :], reads=[ya], writes=[mA_d])
    S.end_stage()

    S.begin_stage()
    ar_ = Rot([S.sbuf("ca%d" % i, [128, 4, TQ], F32) for i in range(2)])
    br_ = Rot([S.sbuf("cbb%d" % i, [128, 4, TQ], F32) for i in range(2)])
    cr_ = Rot([S.sbuf("ccc%d" % i, [128, 4, TQ], F32) for i in range(2)])
    or_ = Rot([S.sbuf("co%d" % i, [128, 4, TQ], BF16) for i in range(2)])
    for qt in range(nQT):
        t0 = qt * TQ
        a_, b_, c_, o_ = ar_.get(), br_.get(), cr_.get(), or_.get()
        for tl, src in ((a_, mA_d), (b_, mB_d), (c_, mC_d)):
            S.dma(tl[:, :, :], src.t.rearrange("(c p) t -> p c t", p=128)[:, :, t0:t0 + TQ], reads=[src], writes=[tl])
        S.op("dve", lambda hh, a_=a_, b_=b_: hh.tensor_tensor(out=a_[:, :, :], in0=a_[:, :, :], in1=b_[:, :, :], op=ALU.add), reads=[a_, b_], writes=[a_])
        S.op("dve", lambda hh, a_=a_, c_=c_, o_=o_: hh.tensor_tensor(out=o_[:, :, :], in0=a_[:, :, :], in1=c_[:, :, :], op=ALU.add), reads=[a_, c_], writes=[o_])
        mk = mT[t0 // 1024]
        S.dma(mk.t.rearrange("(c p) t -> p c t", p=128)[:, :, (t0 % 1024):(t0 % 1024) + TQ], o_[:, :, :], reads=[o_], writes=[mk])
    S.end_stage()


OFF = {"q": 0, "kc": 2048, "vc": 2560, "ks": 3072, "vs": 3584, "kw": 4096, "vw": 4608, "gn": 5120, "cb": 5168, "cc": 7216,
       "u": 9264, "z": 11312, "xbc": 13360, "dt": 16432, "gm": 16464}


def wcat_cols(g):
    cols = []
    for h in range(4):
        cols.append(np.arange(OFF["q"] + 512 * g + 128 * h, OFF["q"] + 512 * g + 128 * h + 128))
    for nm in ("kc", "ks", "kw", "vc"):
        cols.append(np.arange(OFF[nm] + 128 * g, OFF[nm] + 128 * g + 128))
    for nm in ("cb", "cc", "u", "z"):
        for i in range(4):
            cols.append(np.arange(OFF[nm] + 512 * g + 128 * i, OFF[nm] + 512 * g + 128 * i + 128))
    for i in range(4):
        cols.append(np.arange(OFF["xbc"] + 512 * g + 128 * i, OFF["xbc"] + 512 * g + 128 * i + 128))
    cols.append(np.arange(OFF["xbc"] + 2048 + 128 * g, OFF["xbc"] + 2048 + 128 * g + 128))
    cols.append(np.arange(OFF["xbc"] + 2560 + 128 * g, OFF["xbc"] + 2560 + 128 * g + 128))
    for i in range(12):
        j, c = i // 4, i % 4
        s0 = OFF["gm"] + j * 2048 + 512 * g + 128 * c
        cols.append(np.arange(s0, s0 + 128))
    cols.append(np.arange(OFF["vs"] + 128 * g, OFF["vs"] + 128 * g + 128))
    cols.append(np.arange(OFF["vw"] + 128 * g, OFF["vw"] + 128 * g + 128))
    cols.append(np.arange(OFF["gn"] + 12 * g, OFF["gn"] + 12 * g + 12))
    cols.append(np.arange(OFF["dt"] + 8 * g, OFF["dt"] + 8 * g + 8))
    cols = np.concatenate(cols)
    assert cols.shape[0] == NWC
    return cols


def ssd_ch(g, c):
    p = np.arange(128)
    if c < 4:
        return 512 * g + 128 * c + p
    return (2048 if c == 4 else 2560) + 128 * g + p


def prep_p1_layer(inp, l, g):
    p = np.arange(128)
    prm = np.zeros((128, PRM["n"]), np.float32)
    for c in range(4):
        ch = 512 * g + 128 * c + p
        for j in range(3):
            prm[:, PRM["scw"] + c * 3 + j] = inp["sconv_w"][l, j, ch]
        prm[:, PRM["dsk"] + c] = inp["ssd_d"][l, 8 * g + 2 * c + (p >= 64)]
        prm[:, PRM["ng"] + c] = inp["ssd_norm_g"][l, ch]
    for c in range(6):
        ch = ssd_ch(g, c)
        for j in range(4):
            prm[:, PRM["sdw"] + c * 4 + j] = inp["ssd_conv_w"][l, j, ch]
        prm[:, PRM["sdb"] + c] = inp["ssd_conv_b"][l, ch]
    prm[:, PRM["kb1"]] = inp["phi_k_b1"][l]
    prm[:, PRM["kb2"]] = inp["phi_k_b2"][l]
    prm[:, PRM["vb1"]] = inp["phi_v_b1"][l]
    prm[0:8, PRM["dtb"]] = inp["ssd_dt_bias"][l, 8 * g:8 * g + 8]
    prm[0:8, PRM["alog"]] = inp["ssd_a_log"][l, 8 * g:8 * g + 8]
    return {
        "wcat": np.ascontiguousarray(inp["w_in"][l][:, wcat_cols(g)]),
        "prm": prm,
        "peT": np.ascontiguousarray(np.concatenate([inp["nsa_pe_k"][l].T, inp["nsa_pe_v"][l].T], axis=1)).astype(np.float32),
        "kw1": np.ascontiguousarray(inp["phi_k_w1"][l]), "vw1": np.ascontiguousarray(inp["phi_v_w1"][l]),
        "kw2": np.ascontiguousarray(inp["phi_k_w2"][l]), "vw2": np.ascontiguousarray(inp["phi_v_w2"][l]),
        "vb2": np.ascontiguousarray(inp["phi_v_b2"][l][None, :]),
    }


def emit_p0(nc, S, Tc, xT, gv_d, hb):
    S.begin_stage()
    cst = S.sbuf("cst", [128, 128], BF16)
    gv = S.sbuf("gvs", [128, 32], F32)
    x_sb = S.sbuf("x_sb", [128, KC, TQ], F32)
    a16 = S.sbuf("a16", [128, KC, TQ], BF16)
    h_bf = S.sbuf("h_bf", [128, KC, TQ], BF16)
    rstd = S.sbuf("rstd", [128, TQ], F32)
    psn = S.psum("psn", [128, TQ])
    S.op("dve", lambda h: h.memset(cst[:, :], 1.0), writes=[cst])
    S.op("dve", lambda h: h.memset(gv[:, 16:17], EPS), writes=[gv])
    S.dma(gv[:, 0:16], gv_d[:, :], reads=[gv_d], writes=[gv])
    for tt in range(Tc // TQ):
        t0 = tt * TQ
        S.dma(x_sb[:, :, :], xT.t.rearrange("(kc p) t -> p kc t", p=128)[:, :, t0:t0 + TQ], reads=[xT], writes=[x_sb])
        rmsnorm_fm(S, x_sb, gv, 0, TQ, a16, psn, cst[:, 0:128], cst, gv[:, 16:17], rstd, [h_bf])
        for hh in range(2):
            hk = hb[t0 // 256 + hh]
            S.dma(hk.t.rearrange("(kc p) t -> p kc t", p=128), h_bf[:, :, hh * 256:(hh + 1) * 256], reads=[h_bf], writes=[hk])
    S.end_stage()


NCORES = 8
B_, T_, TC_ = 2, 8192, 2048
DEPTH_ = 4
GROUPS = [[0, 1, 2, 3], [4, 5, 6, 7]]
P1_KEYS = ("wcat", "prm", "peT", "kw1", "vw1", "kw2", "vw2", "vb2")
P1_SHAPES = {"wcat": [D, NWC], "prm": [128, PRM["n"]], "peT": [128, 64], "kw1": [4096, 128], "vw1": [4096, 128],
             "kw2": [128, 128], "vw2": [128, 128], "vb2": [1, 128]}
P2_SHAPES = {"w_o": [D, D], "w_up": [D, 4 * D], "w_dn": [4 * D, D], "w_pg": [D, D], "w_pl": [256, D], "gv": [128, 48], "pT": [256, TC_]}


def build_fused(T=T_, Tc=TC_, depth=DEPTH_):
    nc = bass.Bass("TRN2", target_bir_lowering=False)
    L = cb_layout(T)
    with ExitStack() as st:
        S = Sched(nc, st)
        ext = lambda n, s, d, k="ExternalInput": S.dram(n, s, d, kind=k)
        xT = ext("xT", [D, Tc], F32)
        gv0 = ext("gv0", [128, 16], F32)
        pos = ext("pos", [1, T], I32)
        cF_d = ext("cF", [128, CF["n"]], F32)
        cB_d = ext("cB", [128, L["n"]], BF16)
        oh_d = ext("oh", [128, 4], F32)
        hf = ext("hf", [D, Tc], F32, "ExternalOutput")
        lay = []
        for l in range(depth):
            dct = {k: ext("%s_%d" % (k, l), P1_SHAPES[k], F32) for k in P1_KEYS}
            dct.update({k: ext("%s_%d" % (k, l), ([256, Tc] if k == "pT" else P2_SHAPES[k]), F32) for k in P2_SHAPES})
            lay.append(dct)
        xres = S.dram("xres", [D, Tc], F32)
        hb_loc = [S.dram("hb_loc%d" % k, [D, 256], BF16, kind=None) for k in range(Tc // 256)]
        hT_all = [S.dram("hT_all%d" % k, [4 * D, 256], BF16, kind=None) for k in range(Tc // 256)]
        mT_loc = [S.dram("mT_loc%d" % k, [512, 1024], BF16, kind=None) for k in range(T // 1024)]
        mT_all = [S.dram("mT_all%d" % k, [D, 1024], BF16, kind=None) for k in range(T // 1024)]
        scr = make_p1_scratch(S, T)
        nq = Tc // TQ

        def hT_fn(tt):
            tq, c0 = tt // nq, (tt % nq) * TQ
            outl = []
            for hh in range(2):
                hk = hT_all[c0 // 256 + hh]
                outl.append((hh * 256, (hh + 1) * 256, hk.t[tq * D:(tq + 1) * D, :].rearrange("(kc p) t -> p kc t", p=128), hk))
            return outl

        emit_p0(nc, S, Tc, xT, gv0, hb_loc)
        for l in range(depth):
            d_ = lay[l]
            for k in range(len(hb_loc)):
                S.collective(hb_loc[k], hT_all[k], GROUPS)
            emit_p1(nc, S, T, {"hT_fn": hT_fn, "wcat": d_["wcat"], "pos": pos, "cF": cF_d, "cB": cB_d, "prm": d_["prm"], "peT": d_["peT"],
                               "w1": [d_["kw1"], d_["vw1"]], "w2": [d_["kw2"], d_["vw2"]], "vb2": d_["vb2"], "mT": mT_loc}, scr)
            for k in range(len(mT_loc)):
                S.collective(mT_loc[k], mT_all[k], GROUPS)
            emit_p2(nc, S, Tc, {"x_in": xT if l == 0 else xres, "x_out": xres, "mT_all": mT_all, "oh": oh_d, "pT": d_["pT"],
                                "w_o": d_["w_o"], "w_up": d_["w_up"], "w_dn": d_["w_dn"], "w_pg": d_["w_pg"], "w_pl": d_["w_pl"], "gv": d_["gv"],
                                "hb": hb_loc, "hf": hf if l == depth - 1 else None})
        with nc.Block() as block:
            S.finish(block)
    return nc


_PROG = {}


def _gcol(gvec):
    return np.ascontiguousarray(np.asarray(gvec, np.float32).reshape(16, 128).T)


def kernel(**inputs):
    return _run(inputs, T_, TC_, DEPTH_)


def _run(inputs, T_, TC_, DEPTH_):
    inp = {k: np.asarray(v) for k, v in inputs.items()}
    cores = list(range(NCORES))
    key = (T_, TC_, DEPTH_)
    if key not in _PROG:
        _PROG[key] = (build_fused(T_, TC_, DEPTH_), make_consts(T_))
    prog, (cF, cB) = _PROG[key]
    x = inp["x"].astype(np.float32, copy=False)
    p1 = [[prep_p1_layer(inp, l, g) for g in range(4)] for l in range(DEPTH_)]
    p2 = []
    for l in range(DEPTH_):
        g_next = inp["g_mix"][l + 1] if l + 1 < DEPTH_ else inp["g_final"]
        p2.append({"w_o": np.ascontiguousarray(inp["w_o"][l]), "w_up": np.ascontiguousarray(inp["w_up"][l]),
                   "w_dn": np.ascontiguousarray(inp["w_down"][l]), "w_pg": np.ascontiguousarray(inp["w_ple_gate"][l]),
                   "w_pl": np.ascontiguousarray(inp["w_ple"][l]),
                   "gv": np.ascontiguousarray(np.concatenate([_gcol(inp["g_mlp"][l]), _gcol(inp["g_ple"][l]), _gcol(g_next)], axis=1))})
    g0 = _gcol(inp["g_mix"][0])
    maps = []
    for c in cores:
        b, g = c // 4, c % 4
        sl = slice(g * TC_, (g + 1) * TC_)
        oh = np.zeros((128, 4), np.float32)
        oh[:, g] = 1.0
        m = {"xT": np.ascontiguousarray(x[b, sl, :].T), "gv0": g0, "pos": np.ascontiguousarray(inp["positions"][b:b + 1, :]).astype(np.int32),
             "cF": cF, "cB": cB, "oh": oh}
        m["pos"] = np.ascontiguousarray(m["pos"][:, :T_])
        for l in range(DEPTH_):
            for k in P1_KEYS:
                m["%s_%d" % (k, l)] = p1[l][g][k]
            for k in ("w_o", "w_up", "w_dn", "w_pg", "w_pl", "gv"):
                m["%s_%d" % (k, l)] = p2[l][k]
            m["pT_%d" % l] = np.ascontiguousarray(inp["p"][l, b, sl, :].T.astype(np.float32))
        maps.append(m)
    res = run_bass_kernel_spmd(prog, maps, core_ids=cores)
    out = np.empty((B_, T_, D), np.float32)
    for c in cores:
        b, g = c // 4, c % 4
        out[b, g * TC_:(g + 1) * TC_, :] = np.asarray(res.results[c]["hf"]).T
    return out
```

```python
from contextlib import ExitStack
import numpy as np
import ml_dtypes
import concourse.bass as bass
import concourse.mybir as mybir
from concourse.bass_utils import run_bass_kernel_spmd

F32 = mybir.dt.float32
BF16 = mybir.dt.bfloat16
I32 = mybir.dt.int32
AF = mybir.ActivationFunctionType
ALU = mybir.AluOpType
NPBF = ml_dtypes.bfloat16

D = 2048
KC = 16
TQ = 512
NEG = -30000.0
EPS = 1e-6


class TT:
    __slots__ = ("t", "writers", "readers", "name")

    def __init__(self, t, name=""):
        self.t = t
        self.writers = {}
        self.readers = {}
        self.name = name

    def __getitem__(self, k):
        return self.t[k]


class _Eng:
    def __init__(self, name, sem):
        self.name = name
        self.sem = sem
        self.cnt = 0
        self.cmds = []
        self.waited = {}


class Sched:
    ENGS = ("pe", "act", "dve", "pool", "sp")

    def __init__(self, nc, stack, n_dma_sems=12):
        self.nc = nc
        self.stack = stack
        self.cur = stack
        self.e = {}
        self.semobj = {}
        for n in self.ENGS:
            sem = stack.enter_context(nc.semaphore("s_" + n))
            self.e[n] = _Eng(n, sem)
        self.dsem = {}
        for q in ("sp", "pool"):
            sems = [stack.enter_context(nc.semaphore("d_%s%d" % (q, i))) for i in range(n_dma_sems)]
            self.dsem[q] = {"sems": sems, "vals": [0] * n_dma_sems, "next": 0}

    def _uniq(self, name):
        self._n = getattr(self, "_n", 0) + 1
        return "%s_%d" % (name, self._n)

    def sbuf(self, name, shape, dtype):
        return TT(self.cur.enter_context(self.nc.sbuf_tensor(self._uniq(name), list(shape), dtype)), name)

    def psum(self, name, shape, dtype=F32):
        return TT(self.cur.enter_context(self.nc.psum_tensor(self._uniq(name), list(shape), dtype)), name)

    def begin_stage(self):
        self._prev = getattr(self, "_prev", [])
        self._prev.append(self.cur)
        self.cur = ExitStack()

    def push_scope(self):
        self._prev = getattr(self, "_prev", [])
        self._prev.append(self.cur)
        self.cur = ExitStack()

    def pop_scope(self):
        self.cur.close()
        self.cur = self._prev.pop()

    def collective(self, in_tt, out_tt, groups):
        if not hasattr(self, "ccsem"):
            self.ccsem = self.stack.enter_context(self.nc.semaphore("s_cc"))
            self.cccnt = 0
        E = self.e["pool"]
        need = self._collect("pool", [in_tt], [out_tt])
        self.cccnt += 1
        val = self.cccnt
        sem = self.ccsem

        def cmd(h, need=need, sem=sem, i=in_tt, o=out_tt, groups=groups):
            for s_, v in need:
                h.wait_ge(s_, v)
            h.collective_compute("AllGather", mybir.AluOpType.bypass, replica_groups=groups,
                                 ins=[i.t.opt()], outs=[o.t.opt()]).then_inc(sem)
        E.cmds.append(cmd)
        k = self._key(sem)
        out_tt.writers = {k: val}
        out_tt.readers = {}
        if val > in_tt.readers.get(k, 0):
            in_tt.readers[k] = val

    def end_stage(self):
        self.barrier()
        with self.nc.Block() as block:
            self._emit(block)
        self.cur.close()
        self.cur = self._prev.pop()

    def dram(self, name, shape, dtype, kind="Internal"):
        if kind is None:
            return TT(self.nc.dram_tensor(name, list(shape), dtype).ap(), name)
        return TT(self.nc.dram_tensor(name, list(shape), dtype, kind=kind).ap(), name)

    def view(self, tt, name=""):
        return TT(tt.t, name or tt.name)

    def _key(self, sem):
        k = id(sem)
        self.semobj[k] = sem
        return k

    def _collect(self, eng, reads, writes):
        own = self._key(self.e[eng].sem)
        waits = {}
        for t in reads:
            for k, v in t.writers.items():
                if v > waits.get(k, 0):
                    waits[k] = v
        for t in writes:
            for src in (t.writers, t.readers):
                for k, v in src.items():
                    if k == own:
                        continue
                    if v > waits.get(k, 0):
                        waits[k] = v
        E = self.e[eng]
        need = []
        for k, v in waits.items():
            if v > E.waited.get(k, 0):
                E.waited[k] = v
                need.append((self.semobj[k], v))
        return need

    def op(self, eng, fn, reads=(), writes=()):
        E = self.e[eng]
        need = self._collect(eng, reads, writes)
        E.cnt += 1
        idx = E.cnt
        sem = E.sem

        def cmd(h, need=need, fn=fn, sem=sem):
            for s, v in need:
                h.wait_ge(s, v)
            fn(h).then_inc(sem, 1)
        E.cmds.append(cmd)
        k = self._key(sem)
        for t in writes:
            t.writers = {k: idx}
            t.readers = {}
        for t in reads:
            if t in writes:
                continue
            if idx > t.readers.get(k, 0):
                t.readers[k] = idx

    def dma(self, out_ap, in_ap, reads=(), writes=(), q="sp", **kw):
        E = self.e[q]
        Dq = self.dsem[q]
        i = Dq["next"]
        Dq["next"] = (i + 1) % len(Dq["sems"])
        sem = Dq["sems"][i]
        k = self._key(sem)
        need = self._collect(q, reads, writes)
        prev = Dq["vals"][i]
        if prev > E.waited.get(k, 0):
            E.waited[k] = prev
            need.append((sem, prev))
        val = prev + 16
        Dq["vals"][i] = val

        def cmd(h, need=need, sem=sem, out_ap=out_ap, in_ap=in_ap, kw=kw):
            for s, v in need:
                h.wait_ge(s, v)
            h.dma_start(out=out_ap, in_=in_ap, **kw).then_inc(sem, 16)
        E.cmds.append(cmd)
        for t in writes:
            t.writers = {k: val}
            t.readers = {}
        for t in reads:
            if val > t.readers.get(k, 0):
                t.readers[k] = val

    def _all_final(self):
        fin = []
        for n in self.ENGS:
            E = self.e[n]
            if E.cnt:
                fin.append((E.sem, E.cnt))
        for q, Dq in self.dsem.items():
            for s, v in zip(Dq["sems"], Dq["vals"]):
                if v:
                    fin.append((s, v))
        if getattr(self, "cccnt", 0):
            fin.append((self.ccsem, self.cccnt))
        return fin

    def barrier(self):
        fin = self._all_final()
        for n in self.ENGS:
            E = self.e[n]
            need = []
            for s, v in fin:
                k = self._key(s)
                if v > E.waited.get(k, 0):
                    E.waited[k] = v
                    need.append((s, v))

            def cmd(h, need=need):
                for s, v in need:
                    h.wait_ge(s, v)
            E.cmds.append(cmd)

    def finish(self, block):
        self.barrier()
        self._emit(block)

    def _emit(self, block):
        def run(cmds):
            def f(h):
                for c in cmds:
                    c(h)
            return f
        block.tensor(run(self.e["pe"].cmds))
        block.scalar(run(self.e["act"].cmds))
        block.vector(run(self.e["dve"].cmds))
        block.gpsimd(run(self.e["pool"].cmds))
        block.sync(run(self.e["sp"].cmds))
        for n in self.ENGS:
            self.e[n].cmds = []


def mm(S, out_tt, out_ap, pairs, reads, start=True, stop=True):
    def fn(h, pairs=pairs, out_ap=out_ap, start=start, stop=stop):
        n = len(pairs)
        ins = None
        for i, (l, r) in enumerate(pairs):
            ins = h.matmul(out_ap, lhsT=l, rhs=r, start=(start and i == 0), stop=(stop and i == n - 1))
        return ins
    S.op("pe", fn, reads=reads, writes=[out_tt])


class Rot:
    def __init__(self, tiles):
        self.tiles = tiles
        self.i = 0

    def get(self):
        t = self.tiles[self.i]
        self.i = (self.i + 1) % len(self.tiles)
        return t


class WLoader:
    def __init__(self, S, nbuf=3, kc=KC, ncol=128, name="w"):
        self.S = S
        self.st = Rot([S.sbuf("%s_st%d" % (name, i), [128, kc, ncol], F32) for i in range(nbuf)])
        self.bf = Rot([S.sbuf("%s_bf%d" % (name, i), [128, kc, ncol], BF16) for i in range(nbuf)])
        self.cast_i = 0

    def load(self, dram_tt, dram_ap, kc=KC, ncol=128):
        S = self.S
        st = self.st.get()
        bf = self.bf.get()
        S.dma(st[:, 0:kc, 0:ncol], dram_ap, reads=[dram_tt], writes=[st])
        eng = "pool" if (self.cast_i % 2 == 0) else "dve"
        self.cast_i += 1
        S.op(eng, lambda h, st=st, bf=bf: h.tensor_copy(out=bf[:, 0:kc, 0:ncol], in_=st[:, 0:kc, 0:ncol]),
             reads=[st], writes=[bf])
        return bf


def rmsnorm_fm(S, x_sb, g_tt, g_off, ncols, sq_bf, ps, ones_bf, cst, eps_col, rstd, outs, nfeat=D, kc=KC):
    for c in range(kc):
        S.op("act", lambda h, c=c: h.activation(out=sq_bf[:, c, 0:ncols], in_=x_sb[:, c, 0:ncols], func=AF.Square),
             reads=[x_sb], writes=[sq_bf])
    mm(S, ps, ps[:, 0:ncols], [(ones_bf[:, :], sq_bf[:, c, 0:ncols]) for c in range(kc)], reads=[sq_bf, cst])
    S.op("act", lambda h: h.activation(out=rstd[:, 0:ncols], in_=ps[:, 0:ncols], func=AF.Ln, bias=eps_col, scale=1.0 / nfeat),
         reads=[ps, cst], writes=[rstd])
    S.op("act", lambda h: h.activation(out=rstd[:, 0:ncols], in_=rstd[:, 0:ncols], func=AF.Exp, scale=-0.5),
         reads=[rstd], writes=[rstd])
    for o in outs:
        for c in range(kc):
            S.op("dve", lambda h, c=c, o=o: h.scalar_tensor_tensor(
                out=o[:, c, 0:ncols], in0=x_sb[:, c, 0:ncols], scalar=g_tt[:, g_off + c:g_off + c + 1],
                in1=rstd[:, 0:ncols], op0=ALU.mult, op1=ALU.mult), reads=[x_sb, g_tt, rstd], writes=[o])


def emit_p2(nc, S, Tc, io):
    if True:
        if True:
            pass
        xT, xo, mT_all, oh_d, pT = io["x_in"], io["x_out"], io["mT_all"], io["oh"], io["pT"]
        w_o, w_up, w_dn, w_pg, w_pl, gv_d = io["w_o"], io["w_up"], io["w_dn"], io["w_pg"], io["w_pl"], io["gv"]
        hb, hf = io["hb"], io["hf"]
        S.begin_stage()
        oh = S.sbuf("oh_sb", [128, 4], F32)
        S.dma(oh[:, :], oh_d[:, :], reads=[oh_d], writes=[oh])
        cand = Rot([S.sbuf("cand%d" % i, [128, KC, TQ], BF16) for i in range(1)])
        cst = S.sbuf("cst", [128, 256], BF16)
        gv = S.sbuf("gvs", [128, 64], F32)
        x_sb = S.sbuf("x_sb", [128, KC, TQ], F32)
        a16 = S.sbuf("a16", [128, KC, TQ], BF16)
        h_bf = S.sbuf("h_bf", [128, KC, TQ], BF16)
        hid = S.sbuf("hid", [128, 4 * KC, TQ], BF16)
        p32 = S.sbuf("p32", [128, 2, TQ], F32)
        p16 = S.sbuf("p16", [128, 2, TQ], BF16)
        rstd = S.sbuf("rstd", [128, TQ], F32)
        tmpr = Rot([S.sbuf("tmp%d" % i, [128, TQ], F32) for i in range(2)])
        gate = S.sbuf("gate", [128, TQ], F32)
        WL = WLoader(S, nbuf=3)
        psr = Rot([S.psum("ps%d" % i, [128, TQ]) for i in range(4)])
        psn = S.psum("psn", [128, TQ])

        S.op("dve", lambda h: h.memset(cst[:, 0:128], 1.0), writes=[cst])
        S.op("dve", lambda h: h.memset(gv[:, 48:49], EPS), writes=[gv])
        S.dma(gv[:, 0:48], gv_d[:, :], reads=[gv_d], writes=[gv])
        ones_bf = cst[:, 0:128]
        eps_col = gv[:, 48:49]

        def fm(ap, t0):
            return ap.rearrange("(kc p) t -> p kc t", p=128)[:, :, t0:t0 + TQ]

        def wblk(w, r0, c0):
            return w[r0:r0 + D, c0:c0 + 128].rearrange("(kc p) n -> p kc n", p=128)

        for tt in range(Tc // TQ):
            t0 = tt * TQ
            S.dma(x_sb[:, :, :], fm(xT.t, t0), reads=[xT], writes=[x_sb])
            for j in range(4):
                cd = cand.get()
                gtok = j * Tc + t0
                mk = mT_all[gtok // 1024]
                S.dma(cd[:, :, :], mk.t.rearrange("(kc p) t -> p kc t", p=128)[:, :, (gtok % 1024):(gtok % 1024) + TQ], reads=[mk], writes=[cd])
                if j == 0:
                    S.op("dve", lambda h, cd=cd: h.tensor_scalar(out=a16[:, :, :], in0=cd[:, :, :], scalar1=oh[:, 0:1], scalar2=None, op0=ALU.mult),
                         reads=[cd, oh], writes=[a16])
                else:
                    S.op("dve", lambda h, cd=cd, j=j: h.scalar_tensor_tensor(out=a16[:, :, :], in0=cd[:, :, :], scalar=oh[:, j:j + 1], in1=a16[:, :, :],
                                                                           op0=ALU.mult, op1=ALU.add), reads=[cd, oh, a16], writes=[a16])
            S.dma(p32[:, :, :], fm(pT.t, t0), reads=[pT], writes=[p32])
            S.op("dve", lambda h: h.tensor_copy(out=p16[:, :, :], in_=p32[:, :, :]), reads=[p32], writes=[p16])
            for dc in range(KC):
                wb = WL.load(w_o, wblk(w_o.t, 0, dc * 128))
                ps = psr.get()
                mm(S, ps, ps[:, :], [(wb[:, k, :], a16[:, k, :]) for k in range(KC)], reads=[wb, a16])
                S.op("dve", lambda h, dc=dc, ps=ps: h.tensor_tensor(out=x_sb[:, dc, :], in0=ps[:, :], in1=x_sb[:, dc, :], op=ALU.add),
                     reads=[ps, x_sb], writes=[x_sb])
            rmsnorm_fm(S, x_sb, gv, 0, TQ, a16, psn, ones_bf, cst, eps_col, rstd, [h_bf])
            for fc in range(4 * KC):
                wb = WL.load(w_up, wblk(w_up.t, 0, fc * 128))
                ps = psr.get()
                mm(S, ps, ps[:, :], [(wb[:, k, :], h_bf[:, k, :]) for k in range(KC)], reads=[wb, h_bf])
                tmp = tmpr.get()
                S.op("act", lambda h, ps=ps, tmp=tmp: h.activation(out=tmp[:, :], in_=ps[:, :], func=AF.Relu), reads=[ps], writes=[tmp])
                S.op("dve", lambda h, fc=fc, tmp=tmp: h.tensor_tensor(out=hid[:, fc, :], in0=tmp[:, :], in1=tmp[:, :], op=ALU.mult),
                     reads=[tmp], writes=[hid])
            for dc in range(KC):
                ps = psr.get()
                for q4 in range(4):
                    wb = WL.load(w_dn, wblk(w_dn.t, q4 * D, dc * 128))
                    mm(S, ps, ps[:, :], [(wb[:, k, :], hid[:, q4 * KC + k, :]) for k in range(KC)], reads=[wb, hid],
                       start=(q4 == 0), stop=(q4 == 3))
                S.op("dve", lambda h, dc=dc, ps=ps: h.tensor_tensor(out=x_sb[:, dc, :], in0=ps[:, :], in1=x_sb[:, dc, :], op=ALU.add),
                     reads=[ps, x_sb], writes=[x_sb])
            rmsnorm_fm(S, x_sb, gv, 16, TQ, a16, psn, ones_bf, cst, eps_col, rstd, [h_bf])
            for dc in range(KC):
                wb = WL.load(w_pg, wblk(w_pg.t, 0, dc * 128))
                ps = psr.get()
                mm(S, ps, ps[:, :], [(wb[:, k, :], h_bf[:, k, :]) for k in range(KC)], reads=[wb, h_bf])
                S.op("act", lambda h, ps=ps: h.activation(out=gate[:, :], in_=ps[:, :], func=AF.Sigmoid), reads=[ps], writes=[gate])
                wb2 = WL.load(w_pl, w_pl.t[:, dc * 128:dc * 128 + 128].rearrange("(kc p) n -> p kc n", p=128), kc=2)
                ps2 = psr.get()
                mm(S, ps2, ps2[:, :], [(wb2[:, k, :], p16[:, k, :]) for k in range(2)], reads=[wb2, p16])
                tmp = tmpr.get()
                S.op("dve", lambda h, ps2=ps2, tmp=tmp: h.tensor_tensor(out=tmp[:, :], in0=ps2[:, :], in1=gate[:, :], op=ALU.mult),
                     reads=[ps2, gate], writes=[tmp])
                S.op("dve", lambda h, dc=dc, tmp=tmp: h.tensor_tensor(out=x_sb[:, dc, :], in0=tmp[:, :], in1=x_sb[:, dc, :], op=ALU.add),
                     reads=[tmp, x_sb], writes=[x_sb])
            S.dma(fm(xo.t, t0), x_sb[:, :, :], reads=[x_sb], writes=[xo])
            rmsnorm_fm(S, x_sb, gv, 32, TQ, a16, psn, ones_bf, cst, eps_col, rstd, [h_bf])
            for hh in range(2):
                hk = hb[t0 // 256 + hh]
                S.dma(hk.t.rearrange("(kc p) t -> p kc t", p=128), h_bf[:, :, hh * 256:(hh + 1) * 256], reads=[h_bf], writes=[hk])
            if hf is not None:
                for c in range(KC):
                    S.op("dve", lambda h, c=c: h.scalar_tensor_tensor(
                        out=x_sb[:, c, :], in0=x_sb[:, c, :], scalar=gv[:, 32 + c:33 + c], in1=rstd[:, :],
                        op0=ALU.mult, op1=ALU.mult), reads=[x_sb, gv, rstd], writes=[x_sb])
                S.dma(fm(hf.t, t0), x_sb[:, :, :], reads=[x_sb], writes=[hf])
        S.end_stage()


CF = {"ident": 0, "U": 128, "ssdneg": 256, "psw": 384, "pbig": 512, "invf": 768, "sgn": 769, "eps": 770, "one": 771,
      "npi": 772, "n": 776}
TWO_PI = 2.0 * np.pi
PI_IN = 3.1415925


def cb_layout(T):
    o = {}
    c = 0
    for nm, w in (("ident", 128), ("ones", 128), ("cmask", 5 * TQ), ("wmask", 8 * TQ), ("ebig", T), ("ovl1", 4 * 129),
                  ("onesel12", 144), ("sel12", 12 * 128)):
        o[nm] = c
        c += w
    o["n"] = c
    return o


def make_consts(T):
    p = np.arange(128)
    cF = np.zeros((128, CF["n"]), np.float32)
    cF[:, 0:128] = np.eye(128)
    cF[:, 128:256] = (p[:, None] <= p[None, :])
    cF[:, 256:384] = np.where(p[None, :] >= p[:, None], 0.0, NEG)
    cF[:, 384:512] = (p[:, None] == (p[None, :] + 64) % 128)
    xx = np.arange(256)[None, :]
    lo = (p[:, None] < 64)
    pb = np.zeros((128, 256), np.float32)
    pb[:, :] = np.where(xx > 129, -1e9, 0.0)
    pb[:, 127] = np.where(p < 64, 1e4, 0.0)
    pb[:, 128] = 1e4
    pb[:, 129] = np.where(p < 64, -1e9, 1e4)
    cF[:, 512:768] = pb
    invf = (1.0 / (np.float32(10000.0) ** (np.arange(0, 128, 2, dtype=np.float32) / np.float32(128)))).astype(np.float32)
    cF[:, 768] = invf[p % 64]
    cF[:, 769] = np.where(p < 64, -1.0, 1.0)
    cF[:, 770] = EPS
    cF[:, 771] = 1.0
    cF[:, 772] = -np.pi
    L = cb_layout(T)
    cB = np.zeros((128, L["n"]), np.float32)
    cB[:, L["ident"]:L["ident"] + 128] = np.eye(128)
    cB[:, L["ones"]:L["ones"] + 128] = 1.0
    x = np.arange(TQ)[None, :]
    for di in range(5):
        delta = -2048 + 512 * di
        cB[:, L["cmask"] + di * TQ:L["cmask"] + (di + 1) * TQ] = np.where(16 * p[:, None] + 31 + delta <= x, 0.0, NEG)
    for j in range(8):
        kk = 128 * j + p[:, None] - 512
        cB[:, L["wmask"] + j * TQ:L["wmask"] + (j + 1) * TQ] = np.where((x >= kk) & (x < kk + 512), 0.0, NEG)
    key = np.arange(T)[None, :]
    cB[:, L["ebig"]:L["ebig"] + T] = (key // 64 == p[:, None])
    for i in range(4):
        n = 128 * i + p[:, None]
        j = np.arange(128)[None, :]
        ov = (n < 4 * j + 4) & (n > 4 * j - 2)
        cB[:, L["ovl1"] + i * 129:L["ovl1"] + i * 129 + 128] = ov
        cB[:, L["ovl1"] + i * 129 + 128] = 1.0
    for r in range(12):
        cB[:, L["onesel12"] + r * 12 + r] = 1.0
        cB[r, L["sel12"] + r * 128:L["sel12"] + (r + 1) * 128] = 1.0
    return cF, cB.astype(NPBF)


FM = ["q0", "q1", "q2", "q3", "kc", "ks", "kw", "vc"] + ["cb%d" % i for i in range(4)] + ["cc%d" % i for i in range(4)] \
    + ["u%d" % i for i in range(4)] + ["z%d" % i for i in range(4)] + ["xb%d" % i for i in range(6)] \
    + ["gm%d" % i for i in range(12)]
NFM = len(FM)
COL_V = NFM * 128
COL_GN = COL_V + 256
COL_DT = COL_GN + 12
NWC = COL_DT + 8
PRM = {"scw": 0, "sdw": 12, "sdb": 36, "dsk": 42, "ng": 46, "kb1": 50, "kb2": 51, "vb1": 52, "dtb": 53, "alog": 54, "n": 56}


def make_p1_scratch(S, T):
    scr = {}
    scr["qT_d"] = S.dram("qT_d", [4, 128, T], BF16)
    scr["kT_d"] = {n: S.dram(n + "T_d", [128, T], BF16) for n in ("kc", "ks", "kw", "vc")}
    scr["vtok_d"] = S.dram("vtok_d", [T, 256], BF16)
    for nm, r in (("cb_d", 512), ("cu_d", 512), ("z_d", 512), ("xb_d", 768), ("gm_d", 1536), ("gn_d", 12), ("dt_d", 8),
                  ("mA_d", 512), ("mB_d", 512), ("mC_d", 512)):
        scr[nm] = S.dram(nm, [r, T], F32)
    return scr


def emit_p1(nc, S, T, io, scr):
    nQT = T // TQ
    NKT = T // 128
    NCMP = (T - 32) // 16 + 1
    L = cb_layout(T)
    scale = 128 ** -0.5
    if True:
        hT_fn = io["hT_fn"]
        wcat, pos, cF_d, cB_d, prm_d, peT_d = io["wcat"], io["pos"], io["cF"], io["cB"], io["prm"], io["peT"]
        w1_d, w2_d, vb2_d, mT = io["w1"], io["w2"], io["vb2"], io["mT"]
        qT_d, kT_d, vtok_d = scr["qT_d"], scr["kT_d"], scr["vtok_d"]
        cb_d, cu_d, z_d, xb_d, gm_d, gn_d, dt_d = scr["cb_d"], scr["cu_d"], scr["z_d"], scr["xb_d"], scr["gm_d"], scr["gn_d"], scr["dt_d"]
        mA_d, mB_d, mC_d = scr["mA_d"], scr["mB_d"], scr["mC_d"]
        S.push_scope()
        cF = S.sbuf("cF_sb", [128, CF["n"]], F32)
        cB = S.sbuf("cB_sb", [128, L["n"]], BF16)
        prm = S.sbuf("prm_sb", [128, PRM["n"]], F32)
        S.dma(cF[:, :], cF_d[:, :], reads=[cF_d], writes=[cF])
        S.dma(cB[:, :], cB_d[:, :], reads=[cB_d], writes=[cB])
        S.dma(prm[:, :], prm_d[:, :], reads=[prm_d], writes=[prm])
        identF = cF[:, 0:128]
        identB = cB[:, L["ident"]:L["ident"] + 128]
        onesB = cB[:, L["ones"]:L["ones"] + 128]
        eps_col = cF[:, 770:771]

        def fcol(name, i=0):
            c = CF[name] + i
            return cF[:, c:c + 1]

        def pcol(name, i=0, rows=128):
            c = PRM[name] + i
            return prm[0:rows, c:c + 1]

        S.begin_stage()
        h_sb = S.sbuf("h_sb", [128, KC, TQ], BF16)
        WL = WLoader(S, nbuf=3)
        psr = Rot([S.psum("s1ps%d" % i, [128, TQ]) for i in range(4)])
        ps_sw = S.psum("s1sw", [128, TQ])
        f32r = Rot([S.sbuf("s1f%d" % i, [128, TQ], F32) for i in range(4)])
        b16r = Rot([S.sbuf("s1b%d" % i, [128, TQ], BF16) for i in range(3)])
        posi = S.sbuf("posi", [128, TQ], I32)
        ang = S.sbuf("ang", [128, TQ], F32)
        kf = S.sbuf("kf", [128, TQ], F32)
        ki = S.sbuf("ki", [128, TQ], I32)
        rr = S.sbuf("rr", [128, TQ], F32)
        rc = S.sbuf("rc", [128, TQ], F32)
        fx = S.sbuf("fx", [128, TQ], F32)
        cos_t = S.sbuf("cos_t", [128, TQ], F32)
        sin_t = S.sbuf("sin_t", [128, TQ], F32)
        xr = S.sbuf("xr", [128, TQ], F32)
        t1 = S.sbuf("t1", [128, TQ], F32)
        t2 = S.sbuf("t2", [128, TQ], F32)
        cc_sb = S.sbuf("cc_sb", [128, 4, TQ], F32)
        vt_sb = Rot([S.sbuf("vt%d" % i, [128, 256], BF16) for i in range(2)])

        def wrap_pi(r):
            S.op("dve", lambda h: h.tensor_scalar(out=fx[:, :], in0=r[:, :], scalar1=float(np.pi), scalar2=-TWO_PI,
                                                  op0=ALU.is_gt, op1=ALU.mult), reads=[r], writes=[fx])
            S.op("dve", lambda h: h.tensor_tensor(out=r[:, :], in0=r[:, :], in1=fx[:, :], op=ALU.add), reads=[r, fx], writes=[r])
            S.op("dve", lambda h: h.tensor_scalar(out=r[:, :], in0=r[:, :], scalar1=-PI_IN, scalar2=PI_IN,
                                                  op0=ALU.max, op1=ALU.min), reads=[r], writes=[r])

        G_ = 2 if nQT % 2 == 0 else 1
        h_sbs = [h_sb] + [S.sbuf("h_sbx%d" % i, [128, KC, TQ], BF16) for i in range(G_ - 1)]
        cos_ts = [cos_t] + [S.sbuf("cos_x%d" % i, [128, TQ], F32) for i in range(G_ - 1)]
        sin_ts = [sin_t] + [S.sbuf("sin_x%d" % i, [128, TQ], F32) for i in range(G_ - 1)]
        cc_sbs = [cc_sb] + [S.sbuf("cc_x%d" % i, [128, 4, TQ], F32) for i in range(G_ - 1)]
        for tg in range(nQT // G_):
            for s_ in range(G_):
                tt = tg * G_ + s_
                h_sb, cos_t, sin_t, cc_sb = h_sbs[s_], cos_ts[s_], sin_ts[s_], cc_sbs[s_]
                t0 = tt * TQ
                for (c_lo, c_hi, h_ap, h_tt) in hT_fn(tt):
                    S.dma(h_sb[:, :, c_lo:c_hi], h_ap, reads=[h_tt], writes=[h_sb])
                S.dma(posi[:, :], pos.t[0:1, t0:t0 + TQ].partition_broadcast(128), reads=[pos], writes=[posi])
                S.op("dve", lambda h: h.tensor_copy(out=ang[:, :], in_=posi[:, :]), reads=[posi], writes=[ang])
                S.op("dve", lambda h: h.tensor_scalar(out=ang[:, :], in0=ang[:, :], scalar1=fcol("invf"), scalar2=None, op0=ALU.mult),
                     reads=[ang, cF], writes=[ang])
                S.op("dve", lambda h: h.tensor_scalar(out=kf[:, :], in0=ang[:, :], scalar1=float(1.0 / TWO_PI), scalar2=None, op0=ALU.mult),
                     reads=[ang], writes=[kf])
                S.op("dve", lambda h: h.tensor_copy(out=ki[:, :], in_=kf[:, :]), reads=[kf], writes=[ki])
                S.op("dve", lambda h: h.tensor_copy(out=kf[:, :], in_=ki[:, :]), reads=[ki], writes=[kf])
                C1 = 6.28125
                C2 = float(TWO_PI - 6.28125)
                S.op("dve", lambda h: h.scalar_tensor_tensor(out=rr[:, :], in0=kf[:, :], scalar=-C1, in1=ang[:, :], op0=ALU.mult, op1=ALU.add),
                     reads=[kf, ang], writes=[rr])
                S.op("dve", lambda h: h.scalar_tensor_tensor(out=rr[:, :], in0=kf[:, :], scalar=-C2, in1=rr[:, :], op0=ALU.mult, op1=ALU.add),
                     reads=[kf, rr], writes=[rr])
                S.op("dve", lambda h: h.tensor_scalar(out=fx[:, :], in0=rr[:, :], scalar1=float(-np.pi), scalar2=TWO_PI,
                                                      op0=ALU.is_lt, op1=ALU.mult), reads=[rr], writes=[fx])
                S.op("dve", lambda h: h.tensor_tensor(out=rr[:, :], in0=rr[:, :], in1=fx[:, :], op=ALU.add), reads=[rr, fx], writes=[rr])
                S.op("dve", lambda h: h.tensor_scalar(out=rc[:, :], in0=rr[:, :], scalar1=float(np.pi / 2), scalar2=None, op0=ALU.add),
                     reads=[rr], writes=[rc])
                wrap_pi(rr)
                wrap_pi(rc)
                S.op("act", lambda h, sin_t=sin_t: h.activation(out=sin_t[:, :], in_=rr[:, :], func=AF.Sin, scale=fcol("sgn")), reads=[rr, cF], writes=[sin_t])
                S.op("act", lambda h, cos_t=cos_t: h.activation(out=cos_t[:, :], in_=rc[:, :], func=AF.Sin), reads=[rc], writes=[cos_t])
            for ci, nm in enumerate(FM):
                wb = WL.load(wcat, wcat.t[:, ci * 128:(ci + 1) * 128].rearrange("(kc p) n -> p kc n", p=128))
                for s_ in range(G_):
                    t0 = (tg * G_ + s_) * TQ
                    h_sb, cos_t, sin_t, cc_sb = h_sbs[s_], cos_ts[s_], sin_ts[s_], cc_sbs[s_]
                    ps = psr.get()
                    mm(S, ps, ps[:, :], [(wb[:, k, :], h_sb[:, k, :]) for k in range(KC)], reads=[wb, h_sb])
                    if nm in ("q0", "q1", "q2", "q3", "kc", "ks", "kw"):
                        S.op("act", lambda h, ps=ps: h.activation(out=xr[:, :], in_=ps[:, :], func=AF.Copy), reads=[ps], writes=[xr])
                        mm(S, ps_sw, ps_sw[:, :], [(cF[:, 384:512], xr[:, :])], reads=[cF, xr])
                        S.op("dve", lambda h, cos_t=cos_t: h.tensor_tensor(out=t1[:, :], in0=xr[:, :], in1=cos_t[:, :], op=ALU.mult), reads=[xr, cos_t], writes=[t1])
                        S.op("dve", lambda h, sin_t=sin_t: h.tensor_tensor(out=t2[:, :], in0=ps_sw[:, :], in1=sin_t[:, :], op=ALU.mult), reads=[ps_sw, sin_t], writes=[t2])
                        ob = b16r.get()
                        S.op("dve", lambda h, ob=ob: h.tensor_tensor(out=ob[:, :], in0=t1[:, :], in1=t2[:, :], op=ALU.add), reads=[t1, t2], writes=[ob])
                        if nm[0] == "q":
                            S.dma(qT_d.t[int(nm[1]), :, t0:t0 + TQ], ob[:, :], reads=[ob], writes=[qT_d])
                        else:
                            S.dma(kT_d[nm].t[:, t0:t0 + TQ], ob[:, :], reads=[ob], writes=[kT_d[nm]])
                    elif nm == "vc":
                        ob = b16r.get()
                        S.op("act", lambda h, ps=ps, ob=ob: h.activation(out=ob[:, :], in_=ps[:, :], func=AF.Copy), reads=[ps], writes=[ob])
                        S.dma(kT_d["vc"].t[:, t0:t0 + TQ], ob[:, :], reads=[ob], writes=[kT_d["vc"]])
                    elif nm.startswith("cc"):
                        c = int(nm[2])
                        S.op("act", lambda h, ps=ps, c=c, cc_sb=cc_sb: h.activation(out=cc_sb[:, c, :], in_=ps[:, :], func=AF.Copy), reads=[ps], writes=[cc_sb])
                    elif nm[0] == "u":
                        c = int(nm[1])
                        of = f32r.get()
                        S.op("dve", lambda h, ps=ps, c=c, of=of, cc_sb=cc_sb: h.tensor_tensor(out=of[:, :], in0=ps[:, :], in1=cc_sb[:, c, :], op=ALU.mult),
                             reads=[ps, cc_sb], writes=[of])
                        S.dma(cu_d.t[c * 128:(c + 1) * 128, t0:t0 + TQ], of[:, :], reads=[of], writes=[cu_d])
                    else:
                        of = f32r.get()
                        fn = AF.Sigmoid if nm.startswith("gm") else AF.Copy
                        S.op("act", lambda h, ps=ps, of=of, fn=fn: h.activation(out=of[:, :], in_=ps[:, :], func=fn), reads=[ps], writes=[of])
                        if nm.startswith("cb"):
                            dst, c = cb_d, int(nm[2])
                        elif nm[0] == "z":
                            dst, c = z_d, int(nm[1])
                        elif nm.startswith("xb"):
                            dst, c = xb_d, int(nm[2])
                        else:
                            dst, c = gm_d, int(nm[2:])
                        S.dma(dst.t[c * 128:(c + 1) * 128, t0:t0 + TQ], of[:, :], reads=[of], writes=[dst])
            for (c0, ncol, dst, fn) in ((COL_GN, 12, gn_d, AF.Sigmoid), (COL_DT, 8, dt_d, AF.Copy)):
                wb = WL.load(wcat, wcat.t[:, c0:c0 + ncol].rearrange("(kc p) n -> p kc n", p=128), ncol=ncol)
                for s_ in range(G_):
                    t0 = (tg * G_ + s_) * TQ
                    h_sb, cos_t, sin_t, cc_sb = h_sbs[s_], cos_ts[s_], sin_ts[s_], cc_sbs[s_]
                    ps = psr.get()
                    mm(S, ps, ps[0:ncol, :], [(wb[:, k, 0:ncol], h_sb[:, k, :]) for k in range(KC)], reads=[wb, h_sb])
                    of = f32r.get()
                    S.op("act", lambda h, ps=ps, of=of, fn=fn, ncol=ncol: h.activation(out=of[0:ncol, :], in_=ps[0:ncol, :], func=fn), reads=[ps], writes=[of])
                    S.dma(dst.t[:, t0:t0 + TQ], of[0:ncol, :], reads=[of], writes=[dst])
            wv = [WL.load(wcat, wcat.t[:, COL_V + i * 128:COL_V + (i + 1) * 128].rearrange("(kc p) n -> p kc n", p=128)) for i in range(2)]
            for s_ in range(G_):
                t0 = (tg * G_ + s_) * TQ
                h_sb, cos_t, sin_t, cc_sb = h_sbs[s_], cos_ts[s_], sin_ts[s_], cc_sbs[s_]
                for s4 in range(4):
                    ps = psr.get()
                    for i in range(2):
                        mm(S, ps, ps[:, i * 128:(i + 1) * 128], [(h_sb[:, k, s4 * 128:(s4 + 1) * 128], wv[i][:, k, :]) for k in range(KC)],
                           reads=[wv[i], h_sb])
                    vt = vt_sb.get()
                    S.op("act", lambda h, ps=ps, vt=vt: h.activation(out=vt[:, :], in_=ps[:, 0:256], func=AF.Copy), reads=[ps], writes=[vt])
                    S.dma(vtok_d.t[t0 + s4 * 128:t0 + (s4 + 1) * 128, :], vt[:, :], reads=[vt], writes=[vtok_d])
        S.end_stage()
        build_p1_rest(nc, S, locals())
        S.pop_scope()


def build_p1_rest(nc, S, V):
    T, nQT, NKT, NCMP, L, scale = V["T"], V["nQT"], V["NKT"], V["NCMP"], V["L"], V["scale"]
    cF, cB, prm = V["cF"], V["cB"], V["prm"]
    identF, identB, onesB, eps_col = V["identF"], V["identB"], V["onesB"], V["eps_col"]
    fcol, pcol = V["fcol"], V["pcol"]
    qT_d, kT_d, vtok_d = V["qT_d"], V["kT_d"], V["vtok_d"]
    cb_d, cu_d, z_d, xb_d, gm_d, gn_d, dt_d = V["cb_d"], V["cu_d"], V["z_d"], V["xb_d"], V["gm_d"], V["gn_d"], V["dt_d"]
    mA_d, mB_d, mC_d, mT = V["mA_d"], V["mB_d"], V["mC_d"], V["mT"]
    peT_d, w1_d, w2_d, vb2_d = V["peT_d"], V["w1_d"], V["w2_d"], V["vb2_d"]

    def bc(ap, shape):
        return ap.to_broadcast(list(shape))

    S.begin_stage()
    cur = Rot([S.sbuf("cu%d" % i, [128, TQ + 2], F32) for i in range(2)])
    cbr = Rot([S.sbuf("cbt%d" % i, [128, TQ], F32) for i in range(2)])
    gmr = Rot([S.sbuf("gmt%d" % i, [128, TQ], F32) for i in range(2)])
    acr = Rot([S.sbuf("acc%d" % i, [128, TQ], F32) for i in range(2)])
    for tt in range(nQT):
        t0 = tt * TQ
        for c in range(4):
            cu, cbt, gmt, acc = cur.get(), cbr.get(), gmr.get(), acr.get()
            rows = slice(c * 128, (c + 1) * 128)
            if tt == 0:
                S.op("dve", lambda h, cu=cu: h.memset(cu[:, 0:2], 0.0), writes=[cu])
                S.dma(cu[:, 2:TQ + 2], cu_d.t[rows, 0:TQ], reads=[cu_d], writes=[cu])
            else:
                S.dma(cu[:, :], cu_d.t[rows, t0 - 2:t0 + TQ], reads=[cu_d], writes=[cu])
            S.dma(cbt[:, :], cb_d.t[rows, t0:t0 + TQ], reads=[cb_d], writes=[cbt])
            S.dma(gmt[:, :], gm_d.t[(4 + c) * 128:(5 + c) * 128, t0:t0 + TQ], reads=[gm_d], writes=[gmt])
            S.op("dve", lambda h, cu=cu, acc=acc, c=c: h.tensor_scalar(out=acc[:, :], in0=cu[:, 0:TQ], scalar1=pcol("scw", c * 3), scalar2=None, op0=ALU.mult),
                 reads=[cu, prm], writes=[acc])
            for j in (1, 2):
                S.op("dve", lambda h, cu=cu, acc=acc, c=c, j=j: h.scalar_tensor_tensor(
                    out=acc[:, :], in0=cu[:, j:j + TQ], scalar=pcol("scw", c * 3 + j), in1=acc[:, :], op0=ALU.mult, op1=ALU.add),
                    reads=[cu, prm, acc], writes=[acc])
            S.op("dve", lambda h, acc=acc, cbt=cbt: h.tensor_tensor(out=acc[:, :], in0=acc[:, :], in1=cbt[:, :], op=ALU.mult), reads=[acc, cbt], writes=[acc])
            S.op("dve", lambda h, acc=acc, gmt=gmt: h.tensor_tensor(out=acc[:, :], in0=acc[:, :], in1=gmt[:, :], op=ALU.mult), reads=[acc, gmt], writes=[acc])
            S.dma(mB_d.t[rows, t0:t0 + TQ], acc[:, :], reads=[acc], writes=[mB_d])
    S.end_stage()

    S.begin_stage()
    CH = 128
    xbr = Rot([S.sbuf("xb%d" % i, [128, 6, CH + 3], F32) for i in range(2)])
    xc = S.sbuf("xc", [128, 6, CH], F32)
    acc6 = S.sbuf("acc6", [128, 6, CH], F32)
    BT = S.sbuf("BT", [128, CH], BF16)
    CT = S.sbuf("CT", [128, CH], BF16)
    Bk = S.sbuf("Bk", [128, CH], BF16)
    dtr = S.sbuf("dtr", [8, CH], F32)
    dtT = S.sbuf("dtT", [8, CH], F32)
    daT = S.sbuf("daT", [8, CH], F32)
    nA = S.sbuf("nA", [8, 2], F32)
    dtk = S.sbuf("dtk", [128, 8], F32)
    dak = S.sbuf("dak", [128, 8], F32)
    nacol = S.sbuf("nacol", [128, 8], F32)
    alast = S.sbuf("alast", [128, 8], F32)
    decay = S.sbuf("decay", [128, 8], F32)
    wcol = S.sbuf("wcol", [128, 8], F32)
    darep = S.sbuf("darep", [128, 8, CH], F32)
    diffm = S.sbuf("diffm", [128, 8, CH], F32)
    seg = S.sbuf("seg", [128, 8, CH], F32)
    ea = S.sbuf("ea", [128, 8, CH], F32)
    G = S.sbuf("G", [128, 8, CH], BF16)
    Cexp = S.sbuf("Cexp", [128, 8, CH], BF16)
    xdt32 = S.sbuf("xdt32", [128, 8, 64], F32)
    xdtw = S.sbuf("xdtw", [128, 8, 64], BF16)
    xdt_pad = S.sbuf("xdt_pad", [128, 8, 128], BF16)
    S_pad = S.sbuf("S_pad", [128, 8, 128], BF16)
    S32 = S.sbuf("S32", [128, 8, 64], F32)
    zr = Rot([S.sbuf("zs%d" % i, [128, 4, CH], F32) for i in range(2)])
    g2r = Rot([S.sbuf("g2s%d" % i, [128, 4, CH], F32) for i in range(2)])
    sz = S.sbuf("sz", [128, 4, CH], F32)
    yv = S.sbuf("yv", [128, 4, CH], F32)
    sq4 = S.sbuf("sq4", [128, 4, CH], BF16)
    rs = S.sbuf("rs", [128, CH], F32)
    ycr = Rot([S.sbuf("yc%d" % i, [128, 4, CH], F32) for i in range(2)])
    p_t = S.psum("p_t", [128, TQ])
    p_t2 = S.psum("p_t2", [128, TQ])
    p_ar = S.psum("p_ar", [128, 1024])
    p_cb = S.psum("p_cb", [128, TQ])
    p_y = S.psum("p_y", [128, TQ])
    p_cs = S.psum("p_cs", [128, TQ])
    p_bt = S.psum("p_bt", [128, 256], BF16)
    Umat = cF[:, 128:256]
    ssdneg = cF[:, 256:384]
    S.op("dve", lambda h: h.memset(xdt_pad[:, :, :], 0.0), writes=[xdt_pad])
    S.op("dve", lambda h: h.memset(S_pad[:, :, :], 0.0), writes=[S_pad])
    S.op("dve", lambda h: h.memset(S32[:, :, :], 0.0), writes=[S32])
    S.op("act", lambda h: h.activation(out=nA[:, 0:1], in_=pcol("alog", rows=8), func=AF.Exp), reads=[prm], writes=[nA])
    S.op("dve", lambda h: h.tensor_scalar(out=nA[:, 1:2], in0=nA[:, 0:1], scalar1=-1.0, scalar2=None, op0=ALU.mult), reads=[nA], writes=[nA])
    ar3 = p_ar.t[:, :].rearrange("p (h l) -> p h l", h=8)

    def pad_copy(dst, src32):
        d5 = dst.t[:, :, :].rearrange("p (a e) (s q) -> p a e s q", e=2, s=2)
        s4 = src32.t[:, :, :].rearrange("p (a e) q -> p a e q", e=2)
        for e in range(2):
            S.op("dve", lambda h, e=e: h.tensor_copy(out=d5[:, :, e, e, :], in_=s4[:, :, e, :]), reads=[src32], writes=[dst])

    for ch in range(T // CH):
        t0 = ch * CH
        xb = xbr.get()
        src = xb_d.t.rearrange("(c p) t -> p c t", p=128)
        if ch == 0:
            S.op("dve", lambda h, xb=xb: h.memset(xb[:, :, 0:3], 0.0), writes=[xb])
            S.dma(xb[:, :, 3:CH + 3], src[:, :, 0:CH], reads=[xb_d], writes=[xb])
        else:
            S.dma(xb[:, :, :], src[:, :, t0 - 3:t0 + CH], reads=[xb_d], writes=[xb])
        S.dma(dtr[:, :], dt_d.t[:, t0:t0 + CH], reads=[dt_d], writes=[dtr])
        zs, g2s = zr.get(), g2r.get()
        S.dma(zs[:, :, :], z_d.t.rearrange("(c p) t -> p c t", p=128)[:, :, t0:t0 + CH], reads=[z_d], writes=[zs])
        S.dma(g2s[:, :, :], gm_d.t[1024:1536, :].rearrange("(c p) t -> p c t", p=128)[:, :, t0:t0 + CH], reads=[gm_d], writes=[g2s])
        for c in range(6):
            S.op("dve", lambda h, xb=xb, c=c: h.tensor_scalar(out=acc6[:, c, :], in0=xb[:, c, 0:CH], scalar1=pcol("sdw", c * 4), scalar2=None, op0=ALU.mult),
                 reads=[xb, prm], writes=[acc6])
            for j in (1, 2, 3):
                S.op("dve", lambda h, xb=xb, c=c, j=j: h.scalar_tensor_tensor(
                    out=acc6[:, c, :], in0=xb[:, c, j:j + CH], scalar=pcol("sdw", c * 4 + j), in1=acc6[:, c, :], op0=ALU.mult, op1=ALU.add),
                    reads=[xb, prm, acc6], writes=[acc6])
            S.op("act", lambda h, c=c: h.activation(out=xc[:, c, :], in_=acc6[:, c, :], func=AF.Silu, bias=pcol("sdb", c)), reads=[acc6, prm], writes=[xc])
        S.op("dve", lambda h: h.tensor_copy(out=BT[:, :], in_=xc[:, 4, :]), reads=[xc], writes=[BT])
        S.op("dve", lambda h: h.tensor_copy(out=CT[:, :], in_=xc[:, 5, :]), reads=[xc], writes=[CT])
        S.op("act", lambda h: h.activation(out=dtT[:, :], in_=dtr[:, :], func=AF.Exp, bias=pcol("dtb", rows=8)), reads=[dtr, prm], writes=[dtT])
        S.op("act", lambda h: h.activation(out=dtT[:, :], in_=dtT[:, :], func=AF.Ln, bias=cF[0:8, 771:772]), reads=[dtT, cF], writes=[dtT])
        S.op("dve", lambda h: h.tensor_scalar(out=daT[:, :], in0=dtT[:, :], scalar1=nA[:, 1:2], scalar2=None, op0=ALU.mult), reads=[dtT, nA], writes=[daT])
        S.op("pe", lambda h: h.transpose(out=p_t[:, 0:8], in_=dtT[:, :], identity=cF[0:8, 0:8]), reads=[dtT, cF], writes=[p_t])
        S.op("pe", lambda h: h.transpose(out=p_t[:, 8:16], in_=daT[:, :], identity=cF[0:8, 0:8]), reads=[daT, cF], writes=[p_t])
        S.op("dve", lambda h: h.tensor_copy(out=dtk[:, :], in_=p_t[:, 0:8]), reads=[p_t], writes=[dtk])
        S.op("dve", lambda h: h.tensor_copy(out=dak[:, :], in_=p_t[:, 8:16]), reads=[p_t], writes=[dak])
        mm(S, p_t, p_t[:, 16:24], [(Umat, dak[:, :])], reads=[cF, dak])
        S.op("dve", lambda h: h.tensor_copy(out=darep[:, :, :], in_=bc(dak[:, 0:8].unsqueeze(2), [128, 8, CH])), reads=[dak], writes=[darep])
        for hd in range(8):
            mm(S, p_ar, ar3[:, hd, :], [(darep[:, hd, :], Umat)], reads=[darep, cF])
        S.op("dve", lambda h: h.tensor_scalar(out=nacol[:, :], in0=p_t[:, 16:24], scalar1=-1.0, scalar2=None, op0=ALU.mult), reads=[p_t], writes=[nacol])
        S.op("dve", lambda h: h.tensor_copy(out=alast[:, :], in_=ar3[:, :, CH - 1]), reads=[p_ar], writes=[alast])
        S.op("act", lambda h: h.activation(out=decay[:, :], in_=alast[:, :], func=AF.Exp), reads=[alast], writes=[decay])
        S.op("dve", lambda h: h.tensor_tensor(out=wcol[:, :], in0=alast[:, :], in1=nacol[:, :], op=ALU.add), reads=[alast, nacol], writes=[wcol])
        S.op("act", lambda h: h.activation(out=wcol[:, :], in_=wcol[:, :], func=AF.Exp), reads=[wcol], writes=[wcol])
        for c in range(4):
            S.op("pe", lambda h, c=c: h.transpose(out=p_t2[:, c * 128:(c + 1) * 128], in_=xc[:, c, :], identity=identF), reads=[xc, cF], writes=[p_t2])
        pt3 = p_t2.t[:, :].rearrange("p (h q) -> p h q", h=8)
        S.op("dve", lambda h: h.tensor_tensor(out=xdt32[:, :, :], in0=pt3, in1=bc(dtk[:, 0:8].unsqueeze(2), [128, 8, 64]), op=ALU.mult),
             reads=[p_t2, dtk], writes=[xdt32])
        pad_copy(xdt_pad, xdt32)
        S.op("dve", lambda h: h.tensor_tensor(out=xdtw[:, :, :], in0=xdt32[:, :, :], in1=bc(wcol[:, 0:8].unsqueeze(2), [128, 8, 64]), op=ALU.mult),
             reads=[xdt32, wcol], writes=[xdtw])
        S.op("pe", lambda h: h.transpose(out=p_bt[:, 0:128], in_=BT[:, :], identity=identB), reads=[BT, cB], writes=[p_bt])
        S.op("dve", lambda h: h.tensor_copy(out=Bk[:, :], in_=p_bt[:, 0:128]), reads=[p_bt], writes=[Bk])
        mm(S, p_cb, p_cb[:, 0:CH], [(BT[:, :], CT[:, :])], reads=[BT, CT])
        S.op("dve", lambda h: h.tensor_tensor(out=diffm[:, :, :], in0=ar3, in1=bc(ssdneg.unsqueeze(1), [128, 8, CH]), op=ALU.add),
             reads=[p_ar, cF], writes=[diffm])
        for hd in range(8):
            S.op("act", lambda h, hd=hd: h.activation(out=seg[:, hd, :], in_=diffm[:, hd, :], func=AF.Exp, bias=nacol[:, hd:hd + 1]),
                 reads=[diffm, nacol], writes=[seg])
        S.op("dve", lambda h: h.tensor_tensor(out=G[:, :, :], in0=seg[:, :, :], in1=bc(p_cb[:, 0:CH].unsqueeze(1), [128, 8, CH]), op=ALU.mult),
             reads=[seg, p_cb], writes=[G])
        for half in range(2):
            S.op("act", lambda h, half=half: h.activation(out=ea[:, half * 4:(half + 1) * 4, :], in_=ar3[:, half * 4:(half + 1) * 4, :], func=AF.Exp),
                 reads=[p_ar], writes=[ea])
        S.op("dve", lambda h: h.tensor_tensor(out=Cexp[:, :, :], in0=ea[:, :, :], in1=bc(CT[:, :].unsqueeze(1), [128, 8, CH]), op=ALU.mult),
             reads=[ea, CT], writes=[Cexp])
        for c in range(4):
            pairs = []
            for e in range(2):
                pairs.append((xdt_pad[:, 2 * c + e, :], G[:, 2 * c + e, :]))
                pairs.append((S_pad[:, 2 * c + e, :], Cexp[:, 2 * c + e, :]))
            mm(S, p_y, p_y[:, c * 128:(c + 1) * 128], pairs, reads=[xdt_pad, G, S_pad, Cexp])
        py3 = p_y.t[:, :].rearrange("p (c l) -> p c l", c=4)
        for c in range(4):
            S.op("dve", lambda h, c=c: h.scalar_tensor_tensor(out=yv[:, c, :], in0=xc[:, c, :], scalar=pcol("dsk", c), in1=py3[:, c, :],
                                                              op0=ALU.mult, op1=ALU.add), reads=[xc, prm, p_y], writes=[yv])
        mm(S, p_cs, p_cs[:, :], [(Bk[:, :], xdtw[:, :, :].rearrange("p h q -> p (h q)"))], reads=[Bk, xdtw])
        S.op("dve", lambda h: h.tensor_tensor(out=S32[:, :, :], in0=S32[:, :, :], in1=bc(decay[:, 0:8].unsqueeze(2), [128, 8, 64]), op=ALU.mult),
             reads=[S32, decay], writes=[S32])
        S.op("dve", lambda h: h.tensor_tensor(out=S32[:, :, :], in0=S32[:, :, :], in1=p_cs.t[:, :].rearrange("p (h q) -> p h q", h=8), op=ALU.add),
             reads=[S32, p_cs], writes=[S32])
        pad_copy(S_pad, S32)
        S.op("act", lambda h, zs=zs: h.activation(out=sz[:, :, :], in_=zs[:, :, :], func=AF.Silu), reads=[zs], writes=[sz])
        S.op("dve", lambda h: h.tensor_tensor(out=yv[:, :, :], in0=yv[:, :, :], in1=sz[:, :, :], op=ALU.mult), reads=[yv, sz], writes=[yv])
        S.op("act", lambda h: h.activation(out=sq4[:, :, :], in_=yv[:, :, :], func=AF.Square), reads=[yv], writes=[sq4])
        mm(S, p_cb, p_cb[:, 128:256], [(onesB, sq4[:, c, :]) for c in range(4)], reads=[cB, sq4])
        S.op("act", lambda h: h.activation(out=rs[:, :], in_=p_cb[:, 128:256], func=AF.Ln, bias=eps_col, scale=1.0 / 512), reads=[p_cb, cF], writes=[rs])
        S.op("act", lambda h: h.activation(out=rs[:, :], in_=rs[:, :], func=AF.Exp, scale=-0.5), reads=[rs], writes=[rs])
        yc = ycr.get()
        for c in range(4):
            S.op("dve", lambda h, c=c, yc=yc: h.scalar_tensor_tensor(out=yc[:, c, :], in0=yv[:, c, :], scalar=pcol("ng", c), in1=rs[:, :],
                                                                     op0=ALU.mult, op1=ALU.mult), reads=[yv, prm, rs], writes=[yc])
        S.op("dve", lambda h, yc=yc, g2s=g2s: h.tensor_tensor(out=yc[:, :, :], in0=yc[:, :, :], in1=g2s[:, :, :], op=ALU.mult), reads=[yc, g2s], writes=[yc])
        S.dma(mC_d.t.rearrange("(c p) t -> p c t", p=128)[:, :, t0:t0 + CH], yc[:, :, :], reads=[yc], writes=[mC_d])
    S.end_stage()
    build_p1_nsa(nc, S, V)


def build_p1_nsa(nc, S, V):
    T, nQT, NKT, NCMP, L, scale = V["T"], V["nQT"], V["NKT"], V["NCMP"], V["L"], V["scale"]
    cF, cB, prm = V["cF"], V["cB"], V["prm"]
    identF, identB, onesB, eps_col = V["identF"], V["identB"], V["onesB"], V["eps_col"]
    fcol, pcol = V["fcol"], V["pcol"]
    qT_d, kT_d, vtok_d = V["qT_d"], V["kT_d"], V["vtok_d"]
    gm_d, gn_d = V["gm_d"], V["gn_d"]
    mA_d, mB_d, mC_d, mT = V["mA_d"], V["mB_d"], V["mC_d"], V["mT"]
    peT_d, w1_d, w2_d, vb2_d = V["peT_d"], V["w1_d"], V["w2_d"], V["vb2_d"]
    NC4 = (NCMP + 127) // 128
    NCP = NC4 * 128

    def cmask(di):
        return cB[:, L["cmask"] + di * TQ:L["cmask"] + (di + 1) * TQ]

    def wmask(j):
        return cB[:, L["wmask"] + j * TQ:L["wmask"] + (j + 1) * TQ]

    def ebig(kt):
        return cB[:, L["ebig"] + kt * 128:L["ebig"] + (kt + 1) * 128]

    def ovl1(i):
        return cB[:, L["ovl1"] + i * 129:L["ovl1"] + (i + 1) * 129]

    def onesel(r):
        return cB[:, L["onesel12"] + r * 12:L["onesel12"] + (r + 1) * 12]

    def sel12(r):
        return cB[0:12, L["sel12"] + r * 128:L["sel12"] + (r + 1) * 128]

    S.begin_stage()
    srcT = {n: S.sbuf(n + "T", [128, T], BF16) for n in ("kc", "ks", "kw")}
    srcT["vc"] = srcT["kc"]
    vs = S.sbuf("vs", [128, NKT, 128], BF16)
    vw = S.sbuf("vw", [128, NKT, 128], BF16)
    kcmpT = S.sbuf("kcmpT", [128, NCP], BF16)
    vcmp = S.sbuf("vcmp", [128, NC4, 128], BF16)
    for n in ("ks", "kw"):
        S.dma(srcT[n][:, :], kT_d[n].t[:, :], reads=[kT_d[n]], writes=[srcT[n]])
    vt3 = vtok_d.t.rearrange("(kt p) d -> p kt d", p=128)
    S.dma(vs[:, :, :], vt3[:, :, 0:128], reads=[vtok_d], writes=[vs])
    S.dma(vw[:, :, :], vt3[:, :, 128:256], reads=[vtok_d], writes=[vw])
    w1st = S.sbuf("w1st", [128, 32, 128], F32)
    w1bf = S.sbuf("w1bf", [128, 32, 128], BF16)
    w2st = S.sbuf("w2st", [128, 128], F32)
    w2bf = S.sbuf("w2bf", [128, 128], BF16)
    pest = S.sbuf("pest", [128, 64], F32)
    pebf = S.sbuf("pebf", [128, 64], BF16)
    vb2s = S.sbuf("vb2s", [1, 128], F32)
    vb2b = S.sbuf("vb2b", [1, 128], BF16)
    btot = S.sbuf("btot", [128, 1], F32)
    hs = S.sbuf("hs", [128, NCP], BF16)
    p_sc = Rot([S.psum("p_sc%d" % i, [128, TQ]) for i in range(2)])
    p_o = [S.psum("p_o%d" % i, [128, TQ]) for i in range(3)]
    p_den = S.psum("p_den", [128, TQ])
    p_u = S.psum("p_u", [128, TQ])
    p_x = S.psum("p_x", [128, TQ])
    S.dma(pest[:, :], peT_d[:, :], reads=[peT_d], writes=[pest])
    S.op("dve", lambda h: h.tensor_copy(out=pebf[:, :], in_=pest[:, :]), reads=[pest], writes=[pebf])
    S.dma(vb2s[:, :], vb2_d[:, :], reads=[vb2_d], writes=[vb2s])
    S.op("dve", lambda h: h.tensor_copy(out=vb2b[:, :], in_=vb2s[:, :]), reads=[vb2s], writes=[vb2b])
    S.op("dve", lambda h: h.memset(kcmpT[:, :], 0.0), writes=[kcmpT])
    for wi, nm in enumerate(("kc", "vc")):
        S.dma(w1st[:, :, :], w1_d[wi].t.rearrange("(j d) h -> d j h", d=128), reads=[w1_d[wi]], writes=[w1st])
        S.op("pool", lambda h: h.tensor_copy(out=w1bf[:, :, :], in_=w1st[:, :, :]), reads=[w1st], writes=[w1bf])
        S.dma(w2st[:, :], w2_d[wi].t[:, :], reads=[w2_d[wi]], writes=[w2st])
        S.op("dve", lambda h: h.tensor_copy(out=w2bf[:, :], in_=w2st[:, :]), reads=[w2st], writes=[w2bf])
        S.op("dve", lambda h: h.memset(hs[:, :], 0.0), writes=[hs])
        mm(S, p_x, p_x[:, 0:1], [(w1bf[:, j, :], pebf[:, wi * 32 + j:wi * 32 + j + 1]) for j in range(32)], reads=[w1bf, pebf])
        S.op("dve", lambda h, wi=wi: h.tensor_tensor(out=btot[:, :], in0=p_x[:, 0:1], in1=pcol("kb1" if wi == 0 else "vb1"), op=ALU.add),
             reads=[p_x, prm], writes=[btot])
        src = srcT[nm]
        S.dma(src[:, :], kT_d[nm].t[:, :], reads=[kT_d[nm]], writes=[src])
        for n0 in range(0, NCMP, 512):
            nn = min(512, NCMP - n0)
            ps = p_sc.get()
            mm(S, ps, ps[:, 0:nn], [(w1bf[:, j, :], src[:, 16 * n0 + j:16 * n0 + j + 16 * (nn - 1) + 1:16]) for j in range(32)], reads=[w1bf, src])
            S.op("act", lambda h, ps=ps, n0=n0, nn=nn: h.activation(out=hs[:, n0:n0 + nn], in_=ps[:, 0:nn], func=AF.Silu, bias=btot[:, 0:1]),
                 reads=[ps, btot], writes=[hs])
            if wi == 0:
                ps2 = p_sc.get()
                mm(S, ps2, ps2[:, 0:nn], [(w2bf[:, :], hs[:, n0:n0 + nn])], reads=[w2bf, hs])
                S.op("act", lambda h, ps2=ps2, n0=n0, nn=nn: h.activation(out=kcmpT[:, n0:n0 + nn], in_=ps2[:, 0:nn], func=AF.Identity, bias=pcol("kb2")),
                     reads=[ps2, prm], writes=[kcmpT])
        if wi == 1:
            for i in range(NC4):
                ps3 = p_sc.get()
                mm(S, ps3, ps3[:, 0:128], [(hs[:, 128 * i:128 * i + 128], w2bf[:, :]), (cB[0:1, L["ones"]:L["ones"] + 128], vb2b[0:1, :])],
                   reads=[hs, w2bf, cB, vb2b])
                S.op("act", lambda h, ps3=ps3, i=i: h.activation(out=vcmp[:, i, :], in_=ps3[:, 0:128], func=AF.Copy), reads=[ps3], writes=[vcmp])
    q_sb = S.sbuf("q_sb", [128, 4, TQ], BF16)
    gn12 = S.sbuf("gn12", [12, TQ], F32)
    gm0 = S.sbuf("gm0", [128, 4, TQ], F32)
    ec = [S.sbuf("ec%d" % i, [128, TQ], BF16) for i in range(NC4)]
    er = Rot([S.sbuf("er%d" % i, [128, TQ], BF16) for i in range(3)])
    o_sb = [[S.sbuf("o%d_%d" % (b, h), [128, TQ], F32) for h in range(4)] for b in range(3)]
    imp = S.sbuf("imp", [128, 4, 128], F32)
    rd = S.sbuf("rd", [128, 1], F32)
    sc2 = S.sbuf("sc2", [128, 128], F32)
    sc3 = S.sbuf("sc3", [128, 128], F32)
    m8a = S.sbuf("m8a", [128, 8], F32)
    m8b = S.sbuf("m8b", [128, 8], F32)
    negm = S.sbuf("negm", [128, 128], F32)
    negmT = S.sbuf("negmT", [128, TQ], BF16)
    den_sb = S.sbuf("den_sb", [12, TQ], F32)
    fct = S.sbuf("fct", [12, TQ], BF16)
    ya = S.sbuf("ya", [128, TQ], F32)
    tmpm = S.sbuf("tmpm", [128, TQ], F32)

    for qt in range(nQT):
        t0 = qt * TQ
        S.dma(q_sb[:, :, :], qT_d.t.rearrange("h p t -> p h t")[:, :, t0:t0 + TQ], reads=[qT_d], writes=[q_sb])
        S.dma(gn12[:, :], gn_d.t[:, t0:t0 + TQ], reads=[gn_d], writes=[gn12])
        S.dma(gm0[:, :, :], gm_d.t[0:512, :].rearrange("(c p) t -> p c t", p=128)[:, :, t0:t0 + TQ], reads=[gm_d], writes=[gm0])
        ntc = min(NC4, (32 * qt + 30) // 128 + 1)
        sel_kts = list(range(0, 4 * qt + 4))
        win_kts = list(range(max(0, 4 * qt - 4), 4 * qt + 4))
        n_den_total = 4 * (ntc + len(sel_kts) + len(win_kts))
        den_i = [0]

        def den_mm(r, e_t, den_i=den_i, n_den_total=n_den_total):
            i = den_i[0]
            den_i[0] += 1
            mm(S, p_den, p_den[0:12, :], [(onesel(r), e_t[:, :])], reads=[cB, e_t], start=(i == 0), stop=(i == n_den_total - 1))

        for h in range(4):
            for i in range(ntc):
                delta = 2048 * i - 512 * qt
                pairs = [(kcmpT[:, 128 * i:128 * i + 128], q_sb[:, h, :])]
                if -2048 <= delta <= 0:
                    pairs.append((identB, cmask((delta + 2048) // 512)))
                ps = p_sc.get()
                mm(S, ps, ps[:, :], pairs, reads=[kcmpT, q_sb, cB])
                S.op("act", lambda hh, ps=ps, i=i: hh.activation(out=ec[i][:, :], in_=ps[:, :], func=AF.Exp, scale=scale), reads=[ps], writes=[ec[i]])
                mm(S, p_o[0], p_o[0][:, :], [(vcmp[:, i, :], ec[i][:, :])], reads=[vcmp, ec[i]], start=(i == 0), stop=(i == ntc - 1))
                den_mm(3 * h + 0, ec[i])
            S.op("act", lambda hh, h=h: hh.activation(out=o_sb[0][h][:, :], in_=p_o[0][:, :], func=AF.Copy), reads=[p_o[0]], writes=[o_sb[0][h]])
            for s4 in range(4):
                mm(S, p_u, p_u[:, 0:129], [(ec[i][:, s4 * 128:(s4 + 1) * 128], ovl1(i)) for i in range(ntc)], reads=[cB] + ec[0:ntc])
                S.op("dve", lambda hh: hh.tensor_scalar(out=rd[:, :], in0=p_u[:, 128:129], scalar1=1e-30, scalar2=None, op0=ALU.max), reads=[p_u], writes=[rd])
                S.op("dve", lambda hh: hh.reciprocal(out=rd[:, :], in_=rd[:, :]), reads=[rd], writes=[rd])
                if h == 0:
                    S.op("dve", lambda hh, s4=s4: hh.tensor_scalar(out=imp[:, s4, :], in0=p_u[:, 0:128], scalar1=rd[:, 0:1], scalar2=None, op0=ALU.mult),
                         reads=[p_u, rd], writes=[imp])
                else:
                    S.op("dve", lambda hh, s4=s4: hh.scalar_tensor_tensor(out=imp[:, s4, :], in0=p_u[:, 0:128], scalar=rd[:, 0:1], in1=imp[:, s4, :],
                                                                           op0=ALU.mult, op1=ALU.add), reads=[p_u, rd, imp], writes=[imp])
        for s4 in range(4):
            g4 = 4 * qt + s4
            pb0 = CF["pbig"] + 128 - 2 * g4
            S.op("dve", lambda hh, s4=s4, pb0=pb0: hh.tensor_tensor(out=sc2[:, :], in0=imp[:, s4, :], in1=cF[:, pb0:pb0 + 128], op=ALU.add),
                 reads=[imp, cF], writes=[sc2])
            S.op("dve", lambda hh: hh.tensor_scalar(out=sc2[:, 0:1], in0=sc2[:, 0:1], scalar1=1e4, scalar2=None, op0=ALU.add), reads=[sc2], writes=[sc2])
            S.op("dve", lambda hh: hh.max(out=m8a[:, :], in_=sc2[:, :]), reads=[sc2], writes=[m8a])
            S.op("dve", lambda hh: hh.match_replace(out=sc3[:, :], in_to_replace=m8a[:, :], in_values=sc2[:, :], imm_value=-2e9), reads=[sc2, m8a], writes=[sc3])
            S.op("dve", lambda hh: hh.max(out=m8b[:, :], in_=sc3[:, :]), reads=[sc3], writes=[m8b])
            S.op("dve", lambda hh: hh.tensor_scalar(out=negm[:, :], in0=sc2[:, :], scalar1=m8b[:, 7:8], scalar2=NEG, op0=ALU.is_lt, op1=ALU.mult),
                 reads=[sc2, m8b], writes=[negm])
            S.op("pe", lambda hh: hh.transpose(out=p_u[:, 0:128], in_=negm[:, :], identity=identF), reads=[negm, cF], writes=[p_u])
            S.op("dve", lambda hh, s4=s4: hh.tensor_copy(out=negmT[:, s4 * 128:(s4 + 1) * 128], in_=p_u[:, 0:128]), reads=[p_u], writes=[negmT])
        items = []
        for h in range(4):
            for bi, kts in ((1, sel_kts), (2, win_kts)):
                for ii, kt in enumerate(kts):
                    if bi == 1:
                        pairs = [(srcT["ks"][:, 128 * kt:128 * kt + 128], q_sb[:, h, :]), (ebig(kt), negmT[:, :])]
                        if kt >= 4 * qt:
                            pairs.append((identB, wmask(4 + kt - 4 * qt)))
                        rds = [srcT["ks"], q_sb, cB, negmT]
                        vt = vs
                    else:
                        pairs = [(srcT["kw"][:, 128 * kt:128 * kt + 128], q_sb[:, h, :]), (identB, wmask(kt - (4 * qt - 4)))]
                        rds = [srcT["kw"], q_sb, cB]
                        vt = vw
                    items.append((h, bi, kt, pairs, rds, vt, ii == 0, ii == len(kts) - 1))

        def issue_scores(it):
            ps = p_sc.get()
            mm(S, ps, ps[:, :], it[3], reads=it[4])
            e_t = er.get()
            S.op("act", lambda hh, ps=ps, e_t=e_t: hh.activation(out=e_t[:, :], in_=ps[:, :], func=AF.Exp, scale=scale), reads=[ps], writes=[e_t])
            return e_t

        pend = issue_scores(items[0])
        for idx, it in enumerate(items):
            h, bi, kt, _, _, vt, first, last = it
            e_t = pend
            if idx + 1 < len(items):
                pend = issue_scores(items[idx + 1])
            mm(S, p_o[bi], p_o[bi][:, :], [(vt[:, kt, :], e_t[:, :])], reads=[vt, e_t], start=first, stop=last)
            den_mm(3 * h + bi, e_t)
            if last:
                S.op("act", lambda hh, h=h, bi=bi: hh.activation(out=o_sb[bi][h][:, :], in_=p_o[bi][:, :], func=AF.Copy), reads=[p_o[bi]], writes=[o_sb[bi][h]])
        assert den_i[0] == n_den_total
        S.op("dve", lambda hh: hh.tensor_scalar(out=den_sb[:, :], in0=p_den[0:12, :], scalar1=1e-30, scalar2=None, op0=ALU.max), reads=[p_den], writes=[den_sb])
        S.op("dve", lambda hh: hh.reciprocal(out=den_sb[:, :], in_=den_sb[:, :]), reads=[den_sb], writes=[den_sb])
        S.op("dve", lambda hh: hh.tensor_tensor(out=fct[:, :], in0=den_sb[:, :], in1=gn12[:, :], op=ALU.mult), reads=[den_sb, gn12], writes=[fct])
        for h in range(4):
            for b in range(3):
                mm(S, p_x, p_x[:, :], [(sel12(3 * h + b), fct[:, :])], reads=[cB, fct])
                if b == 0:
                    S.op("dve", lambda hh, h=h, b=b: hh.tensor_tensor(out=ya[:, :], in0=o_sb[b][h][:, :], in1=p_x[:, :], op=ALU.mult), reads=[o_sb[b][h], p_x], writes=[ya])
                else:
                    S.op("dve", lambda hh, h=h, b=b: hh.tensor_tensor(out=tmpm[:, :], in0=o_sb[b][h][:, :], in1=p_x[:, :], op=ALU.mult), reads=[o_sb[b][h], p_x], writes=[tmpm])
                    S.op("dve", lambda hh: hh.tensor_tensor(out=ya[:, :], in0=ya[:, :], in1=tmpm[:, :], op=ALU.add), reads=[ya, tmpm], writes=[ya])
            S.op("dve", lambda hh, h=h: hh.tensor_tensor(out=ya[:, :], in0=ya[:, :], in1=gm0[:, h, :], op=ALU.mult), reads=[ya, gm0], writes=[ya])
            S.dma(mA_d.t[h * 128:(h + 1) * 128, t0:t0 + TQ], ya[:, :], reads=[ya], writes=[mA_d])
    S.end_stage()

    S.begin_stage()
    ar_ = Rot([S.sbuf("ca%d" % i, [128, 4, TQ], F32) for i in range(2)])
    br_ = Rot([S.sbuf("cbb%d" % i, [128, 4, TQ], F32) for i in range(2)])
    cr_ = Rot([S.sbuf("ccc%d" % i, [128, 4, TQ], F32) for i in range(2)])
    or_ = Rot([S.sbuf("co%d" % i, [128, 4, TQ], BF16) for i in range(2)])
    for qt in range(nQT):
        t0 = qt * TQ
        a_, b_, c_, o_ = ar_.get(), br_.get(), cr_.get(), or_.get()
        for tl, src in ((a_, mA_d), (b_, mB_d), (c_, mC_d)):
            S.dma(tl[:, :, :], src.t.rearrange("(c p) t -> p c t", p=128)[:, :, t0:t0 + TQ], reads=[src], writes=[tl])
        S.op("dve", lambda hh, a_=a_, b_=b_: hh.tensor_tensor(out=a_[:, :, :], in0=a_[:, :, :], in1=b_[:, :, :], op=ALU.add), reads=[a_, b_], writes=[a_])
        S.op("dve", lambda hh, a_=a_, c_=c_, o_=o_: hh.tensor_tensor(out=o_[:, :, :], in0=a_[:, :, :], in1=c_[:, :, :], op=ALU.add), reads=[a_, c_], writes=[o_])
        mk = mT[t0 // 1024]
        S.dma(mk.t.rearrange("(c p) t -> p c t", p=128)[:, :, (t0 % 1024):(t0 % 1024) + TQ], o_[:, :, :], reads=[o_], writes=[mk])
    S.end_stage()


OFF = {"q": 0, "kc": 2048, "vc": 2560, "ks": 3072, "vs": 3584, "kw": 4096, "vw": 4608, "gn": 5120, "cb": 5168, "cc": 7216,
       "u": 9264, "z": 11312, "xbc": 13360, "dt": 16432, "gm": 16464}


def wcat_cols(g):
    cols = []
    for h in range(4):
        cols.append(np.arange(OFF["q"] + 512 * g + 128 * h, OFF["q"] + 512 * g + 128 * h + 128))
    for nm in ("kc", "ks", "kw", "vc"):
        cols.append(np.arange(OFF[nm] + 128 * g, OFF[nm] + 128 * g + 128))
    for nm in ("cb", "cc", "u", "z"):
        for i in range(4):
            cols.append(np.arange(OFF[nm] + 512 * g + 128 * i, OFF[nm] + 512 * g + 128 * i + 128))
    for i in range(4):
        cols.append(np.arange(OFF["xbc"] + 512 * g + 128 * i, OFF["xbc"] + 512 * g + 128 * i + 128))
    cols.append(np.arange(OFF["xbc"] + 2048 + 128 * g, OFF["xbc"] + 2048 + 128 * g + 128))
    cols.append(np.arange(OFF["xbc"] + 2560 + 128 * g, OFF["xbc"] + 2560 + 128 * g + 128))
    for i in range(12):
        j, c = i // 4, i % 4
        s0 = OFF["gm"] + j * 2048 + 512 * g + 128 * c
        cols.append(np.arange(s0, s0 + 128))
    cols.append(np.arange(OFF["vs"] + 128 * g, OFF["vs"] + 128 * g + 128))
    cols.append(np.arange(OFF["vw"] + 128 * g, OFF["vw"] + 128 * g + 128))
    cols.append(np.arange(OFF["gn"] + 12 * g, OFF["gn"] + 12 * g + 12))
    cols.append(np.arange(OFF["dt"] + 8 * g, OFF["dt"] + 8 * g + 8))
    cols = np.concatenate(cols)
    assert cols.shape[0] == NWC
    return cols


def ssd_ch(g, c):
    p = np.arange(128)
    if c < 4:
        return 512 * g + 128 * c + p
    return (2048 if c == 4 else 2560) + 128 * g + p


def prep_p1_layer(inp, l, g):
    p = np.arange(128)
    prm = np.zeros((128, PRM["n"]), np.float32)
    for c in range(4):
        ch = 512 * g + 128 * c + p
        for j in range(3):
            prm[:, PRM["scw"] + c * 3 + j] = inp["sconv_w"][l, j, ch]
        prm[:, PRM["dsk"] + c] = inp["ssd_d"][l, 8 * g + 2 * c + (p >= 64)]
        prm[:, PRM["ng"] + c] = inp["ssd_norm_g"][l, ch]
    for c in range(6):
        ch = ssd_ch(g, c)
        for j in range(4):
            prm[:, PRM["sdw"] + c * 4 + j] = inp["ssd_conv_w"][l, j, ch]
        prm[:, PRM["sdb"] + c] = inp["ssd_conv_b"][l, ch]
    prm[:, PRM["kb1"]] = inp["phi_k_b1"][l]
    prm[:, PRM["kb2"]] = inp["phi_k_b2"][l]
    prm[:, PRM["vb1"]] = inp["phi_v_b1"][l]
    prm[0:8, PRM["dtb"]] = inp["ssd_dt_bias"][l, 8 * g:8 * g + 8]
    prm[0:8, PRM["alog"]] = inp["ssd_a_log"][l, 8 * g:8 * g + 8]
    return {
        "wcat": np.ascontiguousarray(inp["w_in"][l][:, wcat_cols(g)]),
        "prm": prm,
        "peT": np.ascontiguousarray(np.concatenate([inp["nsa_pe_k"][l].T, inp["nsa_pe_v"][l].T], axis=1)).astype(np.float32),
        "kw1": np.ascontiguousarray(inp["phi_k_w1"][l]), "vw1": np.ascontiguousarray(inp["phi_v_w1"][l]),
        "kw2": np.ascontiguousarray(inp["phi_k_w2"][l]), "vw2": np.ascontiguousarray(inp["phi_v_w2"][l]),
        "vb2": np.ascontiguousarray(inp["phi_v_b2"][l][None, :]),
    }


def emit_p0(nc, S, Tc, xT, gv_d, hb):
    S.begin_stage()
    cst = S.sbuf("cst", [128, 128], BF16)
    gv = S.sbuf("gvs", [128, 32], F32)
    x_sb = S.sbuf("x_sb", [128, KC, TQ], F32)
    a16 = S.sbuf("a16", [128, KC, TQ], BF16)
    h_bf = S.sbuf("h_bf", [128, KC, TQ], BF16)
    rstd = S.sbuf("rstd", [128, TQ], F32)
    psn = S.psum("psn", [128, TQ])
    S.op("dve", lambda h: h.memset(cst[:, :], 1.0), writes=[cst])
    S.op("dve", lambda h: h.memset(gv[:, 16:17], EPS), writes=[gv])
    S.dma(gv[:, 0:16], gv_d[:, :], reads=[gv_d], writes=[gv])
    for tt in range(Tc // TQ):
        t0 = tt * TQ
        S.dma(x_sb[:, :, :], xT.t.rearrange("(kc p) t -> p kc t", p=128)[:, :, t0:t0 + TQ], reads=[xT], writes=[x_sb])
        rmsnorm_fm(S, x_sb, gv, 0, TQ, a16, psn, cst[:, 0:128], cst, gv[:, 16:17], rstd, [h_bf])
        for hh in range(2):
            hk = hb[t0 // 256 + hh]
            S.dma(hk.t.rearrange("(kc p) t -> p kc t", p=128), h_bf[:, :, hh * 256:(hh + 1) * 256], reads=[h_bf], writes=[hk])
    S.end_stage()


NCORES = 8
B_, T_, TC_ = 2, 8192, 2048
DEPTH_ = 4
GROUPS = [[0, 1, 2, 3], [4, 5, 6, 7]]
P1_KEYS = ("wcat", "prm", "peT", "kw1", "vw1", "kw2", "vw2", "vb2")
P1_SHAPES = {"wcat": [D, NWC], "prm": [128, PRM["n"]], "peT": [128, 64], "kw1": [4096, 128], "vw1": [4096, 128],
             "kw2": [128, 128], "vw2": [128, 128], "vb2": [1, 128]}
P2_SHAPES = {"w_o": [D, D], "w_up": [D, 4 * D], "w_dn": [4 * D, D], "w_pg": [D, D], "w_pl": [256, D], "gv": [128, 48], "pT": [256, TC_]}


def build_fused(T=T_, Tc=TC_, depth=DEPTH_):
    nc = bass.Bass("TRN2", target_bir_lowering=False)
    L = cb_layout(T)
    with ExitStack() as st:
        S = Sched(nc, st)
        ext = lambda n, s, d, k="ExternalInput": S.dram(n, s, d, kind=k)
        xT = ext("xT", [D, Tc], F32)
        gv0 = ext("gv0", [128, 16], F32)
        pos = ext("pos", [1, T], I32)
        cF_d = ext("cF", [128, CF["n"]], F32)
        cB_d = ext("cB", [128, L["n"]], BF16)
        oh_d = ext("oh", [128, 4], F32)
        hf = ext("hf", [D, Tc], F32, "ExternalOutput")
        lay = []
        for l in range(depth):
            dct = {k: ext("%s_%d" % (k, l), P1_SHAPES[k], F32) for k in P1_KEYS}
            dct.update({k: ext("%s_%d" % (k, l), ([256, Tc] if k == "pT" else P2_SHAPES[k]), F32) for k in P2_SHAPES})
            lay.append(dct)
        xres = S.dram("xres", [D, Tc], F32)
        hb_loc = [S.dram("hb_loc%d" % k, [D, 256], BF16, kind=None) for k in range(Tc // 256)]
        hT_all = [S.dram("hT_all%d" % k, [4 * D, 256], BF16, kind=None) for k in range(Tc // 256)]
        mT_loc = [S.dram("mT_loc%d" % k, [512, 1024], BF16, kind=None) for k in range(T // 1024)]
        mT_all = [S.dram("mT_all%d" % k, [D, 1024], BF16, kind=None) for k in range(T // 1024)]
        scr = make_p1_scratch(S, T)
        nq = Tc // TQ

        def hT_fn(tt):
            tq, c0 = tt // nq, (tt % nq) * TQ
            outl = []
            for hh in range(2):
                hk = hT_all[c0 // 256 + hh]
                outl.append((hh * 256, (hh + 1) * 256, hk.t[tq * D:(tq + 1) * D, :].rearrange("(kc p) t -> p kc t", p=128), hk))
            return outl

        emit_p0(nc, S, Tc, xT, gv0, hb_loc)
        for l in range(depth):
            d_ = lay[l]
            for k in range(len(hb_loc)):
                S.collective(hb_loc[k], hT_all[k], GROUPS)
            emit_p1(nc, S, T, {"hT_fn": hT_fn, "wcat": d_["wcat"], "pos": pos, "cF": cF_d, "cB": cB_d, "prm": d_["prm"], "peT": d_["peT"],
                               "w1": [d_["kw1"], d_["vw1"]], "w2": [d_["kw2"], d_["vw2"]], "vb2": d_["vb2"], "mT": mT_loc}, scr)
            for k in range(len(mT_loc)):
                S.collective(mT_loc[k], mT_all[k], GROUPS)
            emit_p2(nc, S, Tc, {"x_in": xT if l == 0 else xres, "x_out": xres, "mT_all": mT_all, "oh": oh_d, "pT": d_["pT"],
                                "w_o": d_["w_o"], "w_up": d_["w_up"], "w_dn": d_["w_dn"], "w_pg": d_["w_pg"], "w_pl": d_["w_pl"], "gv": d_["gv"],
                                "hb": hb_loc, "hf": hf if l == depth - 1 else None})
        with nc.Block() as block:
            S.finish(block)
    return nc


_PROG = {}


def _gcol(gvec):
    return np.ascontiguousarray(np.asarray(gvec, np.float32).reshape(16, 128).T)


def kernel(**inputs):
    return _run(inputs, T_, TC_, DEPTH_)


def _run(inputs, T_, TC_, DEPTH_):
    inp = {k: np.asarray(v) for k, v in inputs.items()}
    cores = list(range(NCORES))
    key = (T_, TC_, DEPTH_)
    if key not in _PROG:
        _PROG[key] = (build_fused(T_, TC_, DEPTH_), make_consts(T_))
    prog, (cF, cB) = _PROG[key]
    x = inp["x"].astype(np.float32, copy=False)
    p1 = [[prep_p1_layer(inp, l, g) for g in range(4)] for l in range(DEPTH_)]
    p2 = []
    for l in range(DEPTH_):
        g_next = inp["g_mix"][l + 1] if l + 1 < DEPTH_ else inp["g_final"]
        p2.append({"w_o": np.ascontiguousarray(inp["w_o"][l]), "w_up": np.ascontiguousarray(inp["w_up"][l]),
                   "w_dn": np.ascontiguousarray(inp["w_down"][l]), "w_pg": np.ascontiguousarray(inp["w_ple_gate"][l]),
                   "w_pl": np.ascontiguousarray(inp["w_ple"][l]),
                   "gv": np.ascontiguousarray(np.concatenate([_gcol(inp["g_mlp"][l]), _gcol(inp["g_ple"][l]), _gcol(g_next)], axis=1))})
    g0 = _gcol(inp["g_mix"][0])
    maps = []
    for c in cores:
        b, g = c // 4, c % 4
        sl = slice(g * TC_, (g + 1) * TC_)
        oh = np.zeros((128, 4), np.float32)
        oh[:, g] = 1.0
        m = {"xT": np.ascontiguousarray(x[b, sl, :].T), "gv0": g0, "pos": np.ascontiguousarray(inp["positions"][b:b + 1, :]).astype(np.int32),
             "cF": cF, "cB": cB, "oh": oh}
        m["pos"] = np.ascontiguousarray(m["pos"][:, :T_])
        for l in range(DEPTH_):
            for k in P1_KEYS:
                m["%s_%d" % (k, l)] = p1[l][g][k]
            for k in ("w_o", "w_up", "w_dn", "w_pg", "w_pl", "gv"):
                m["%s_%d" % (k, l)] = p2[l][k]
            m["pT_%d" % l] = np.ascontiguousarray(inp["p"][l, b, sl, :].T.astype(np.float32))
        maps.append(m)
    res = run_bass_kernel_spmd(prog, maps, core_ids=cores)
    out = np.empty((B_, T_, D), np.float32)
    for c in cores:
        b, g = c // 4, c % 4
        out[b, g * TC_:(g + 1) * TC_, :] = np.asarray(res.results[c]["hf"]).T
    return out
```

```python
from contextlib import ExitStack
import numpy as np
import ml_dtypes
import concourse.bass as bass
import concourse.mybir as mybir
from concourse.bass_utils import run_bass_kernel_spmd

F32 = mybir.dt.float32
BF16 = mybir.dt.bfloat16
I32 = mybir.dt.int32
AF = mybir.ActivationFunctionType
ALU = mybir.AluOpType
NPBF = ml_dtypes.bfloat16

D = 2048
KC = 16
TQ = 512
NEG = -30000.0
EPS = 1e-6


class TT:
    __slots__ = ("t", "writers", "readers", "name")

    def __init__(self, t, name=""):
        self.t = t
        self.writers = {}
        self.readers = {}
        self.name = name

    def __getitem__(self, k):
        return self.t[k]


class _Eng:
    def __init__(self, name, sem):
        self.name = name
        self.sem = sem
        self.cnt = 0
        self.cmds = []
        self.waited = {}


class Sched:
    ENGS = ("pe", "act", "dve", "pool", "sp")

    def __init__(self, nc, stack, n_dma_sems=12):
        self.nc = nc
        self.stack = stack
        self.cur = stack
        self.e = {}
        self.semobj = {}
        for n in self.ENGS:
            sem = stack.enter_context(nc.semaphore("s_" + n))
            self.e[n] = _Eng(n, sem)
        self.dsem = {}
        for q in ("sp", "pool"):
            sems = [stack.enter_context(nc.semaphore("d_%s%d" % (q, i))) for i in range(n_dma_sems)]
            self.dsem[q] = {"sems": sems, "vals": [0] * n_dma_sems, "next": 0}

    def _uniq(self, name):
        self._n = getattr(self, "_n", 0) + 1
        return "%s_%d" % (name, self._n)

    def sbuf(self, name, shape, dtype):
        return TT(self.cur.enter_context(self.nc.sbuf_tensor(self._uniq(name), list(shape), dtype)), name)

    def psum(self, name, shape, dtype=F32):
        return TT(self.cur.enter_context(self.nc.psum_tensor(self._uniq(name), list(shape), dtype)), name)

    def begin_stage(self):
        self._prev = getattr(self, "_prev", [])
        self._prev.append(self.cur)
        self.cur = ExitStack()

    def push_scope(self):
        self._prev = getattr(self, "_prev", [])
        self._prev.append(self.cur)
        self.cur = ExitStack()

    def pop_scope(self):
        self.cur.close()
        self.cur = self._prev.pop()

    def collective(self, in_tt, out_tt, groups):
        if not hasattr(self, "ccsem"):
            self.ccsem = self.stack.enter_context(self.nc.semaphore("s_cc"))
            self.cccnt = 0
        E = self.e["pool"]
        need = self._collect("pool", [in_tt], [out_tt])
        self.cccnt += 1
        val = self.cccnt
        sem = self.ccsem

        def cmd(h, need=need, sem=sem, i=in_tt, o=out_tt, groups=groups):
            for s_, v in need:
                h.wait_ge(s_, v)
            h.collective_compute("AllGather", mybir.AluOpType.bypass, replica_groups=groups,
                                 ins=[i.t.opt()], outs=[o.t.opt()]).then_inc(sem)
        E.cmds.append(cmd)
        k = self._key(sem)
        out_tt.writers = {k: val}
        out_tt.readers = {}
        if val > in_tt.readers.get(k, 0):
            in_tt.readers[k] = val

    def end_stage(self):
        self.barrier()
        with self.nc.Block() as block:
            self._emit(block)
        self.cur.close()
        self.cur = self._prev.pop()

    def dram(self, name, shape, dtype, kind="Internal"):
        if kind is None:
            return TT(self.nc.dram_tensor(name, list(shape), dtype).ap(), name)
        return TT(self.nc.dram_tensor(name, list(shape), dtype, kind=kind).ap(), name)

    def view(self, tt, name=""):
        return TT(tt.t, name or tt.name)

    def _key(self, sem):
        k = id(sem)
        self.semobj[k] = sem
        return k

    def _collect(self, eng, reads, writes):
        own = self._key(self.e[eng].sem)
        waits = {}
        for t in reads:
            for k, v in t.writers.items():
                if v > waits.get(k, 0):
                    waits[k] = v
        for t in writes:
            for src in (t.writers, t.readers):
                for k, v in src.items():
                    if k == own:
                        continue
                    if v > waits.get(k, 0):
                        waits[k] = v
        E = self.e[eng]
        need = []
        for k, v in waits.items():
            if v > E.waited.get(k, 0):
                E.waited[k] = v
                need.append((self.semobj[k], v))
        return need

    def op(self, eng, fn, reads=(), writes=()):
        E = self.e[eng]
        need = self._collect(eng, reads, writes)
        E.cnt += 1
        idx = E.cnt
        sem = E.sem

        def cmd(h, need=need, fn=fn, sem=sem):
            for s, v in need:
                h.wait_ge(s, v)
            fn(h).then_inc(sem, 1)
        E.cmds.append(cmd)
        k = self._key(sem)
        for t in writes:
            t.writers = {k: idx}
            t.readers = {}
        for t in reads:
            if t in writes:
                continue
            if idx > t.readers.get(k, 0):
                t.readers[k] = idx

    def dma(self, out_ap, in_ap, reads=(), writes=(), q="sp", **kw):
        E = self.e[q]
        Dq = self.dsem[q]
        i = Dq["next"]
        Dq["next"] = (i + 1) % len(Dq["sems"])
        sem = Dq["sems"][i]
        k = self._key(sem)
        need = self._collect(q, reads, writes)
        prev = Dq["vals"][i]
        if prev > E.waited.get(k, 0):
            E.waited[k] = prev
            need.append((sem, prev))
        val = prev + 16
        Dq["vals"][i] = val

        def cmd(h, need=need, sem=sem, out_ap=out_ap, in_ap=in_ap, kw=kw):
            for s, v in need:
                h.wait_ge(s, v)
            h.dma_start(out=out_ap, in_=in_ap, **kw).then_inc(sem, 16)
        E.cmds.append(cmd)
        for t in writes:
            t.writers = {k: val}
            t.readers = {}
        for t in reads:
            if val > t.readers.get(k, 0):
                t.readers[k] = val

    def _all_final(self):
        fin = []
        for n in self.ENGS:
            E = self.e[n]
            if E.cnt:
                fin.append((E.sem, E.cnt))
        for q, Dq in self.dsem.items():
            for s, v in zip(Dq["sems"], Dq["vals"]):
                if v:
                    fin.append((s, v))
        if getattr(self, "cccnt", 0):
            fin.append((self.ccsem, self.cccnt))
        return fin

    def barrier(self):
        fin = self._all_final()
        for n in self.ENGS:
            E = self.e[n]
            need = []
            for s, v in fin:
                k = self._key(s)
                if v > E.waited.get(k, 0):
                    E.waited[k] = v
                    need.append((s, v))

            def cmd(h, need=need):
                for s, v in need:
                    h.wait_ge(s, v)
            E.cmds.append(cmd)

    def finish(self, block):
        self.barrier()
        self._emit(block)

    def _emit(self, block):
        def run(cmds):
            def f(h):
                for c in cmds:
                    c(h)
            return f
        block.tensor(run(self.e["pe"].cmds))
        block.scalar(run(self.e["act"].cmds))
        block.vector(run(self.e["dve"].cmds))
        block.gpsimd(run(self.e["pool"].cmds))
        block.sync(run(self.e["sp"].cmds))
        for n in self.ENGS:
            self.e[n].cmds = []


def mm(S, out_tt, out_ap, pairs, reads, start=True, stop=True):
    def fn(h, pairs=pairs, out_ap=out_ap, start=start, stop=stop):
        n = len(pairs)
        ins = None
        for i, (l, r) in enumerate(pairs):
            ins = h.matmul(out_ap, lhsT=l, rhs=r, start=(start and i == 0), stop=(stop and i == n - 1))
        return ins
    S.op("pe", fn, reads=reads, writes=[out_tt])


class Rot:
    def __init__(self, tiles):
        self.tiles = tiles
        self.i = 0

    def get(self):
        t = self.tiles[self.i]
        self.i = (self.i + 1) % len(self.tiles)
        return t


class WLoader:
    def __init__(self, S, nbuf=3, kc=KC, ncol=128, name="w", cast_engs=("pool", "dve")):
        self.S = S
        self.cast_engs = cast_engs
        self.st = Rot([S.sbuf("%s_st%d" % (name, i), [128, kc, ncol], F32) for i in range(nbuf)])
        self.bf = Rot([S.sbuf("%s_bf%d" % (name, i), [128, kc, ncol], BF16) for i in range(nbuf)])
        self.cast_i = 0

    def load(self, dram_tt, dram_ap, kc=KC, ncol=128):
        S = self.S
        st = self.st.get()
        bf = self.bf.get()
        S.dma(st[:, 0:kc, 0:ncol], dram_ap, reads=[dram_tt], writes=[st])
        eng = self.cast_engs[self.cast_i % len(self.cast_engs)]
        self.cast_i += 1
        S.op(eng, lambda h, st=st, bf=bf: h.tensor_copy(out=bf[:, 0:kc, 0:ncol], in_=st[:, 0:kc, 0:ncol]),
             reads=[st], writes=[bf])
        return bf


def rmsnorm_fm(S, x_sb, g_tt, g_off, ncols, sq_bf, ps, ones_bf, cst, eps_col, rstd, outs, nfeat=D, kc=KC):
    for c in range(kc):
        S.op("act", lambda h, c=c: h.activation(out=sq_bf[:, c, 0:ncols], in_=x_sb[:, c, 0:ncols], func=AF.Square),
             reads=[x_sb], writes=[sq_bf])
    mm(S, ps, ps[:, 0:ncols], [(ones_bf[:, :], sq_bf[:, c, 0:ncols]) for c in range(kc)], reads=[sq_bf, cst])
    S.op("act", lambda h: h.activation(out=rstd[:, 0:ncols], in_=ps[:, 0:ncols], func=AF.Ln, bias=eps_col, scale=1.0 / nfeat),
         reads=[ps, cst], writes=[rstd])
    S.op("act", lambda h: h.activation(out=rstd[:, 0:ncols], in_=rstd[:, 0:ncols], func=AF.Exp, scale=-0.5),
         reads=[rstd], writes=[rstd])
    for o in outs:
        for c in range(kc):
            S.op("dve", lambda h, c=c, o=o: h.scalar_tensor_tensor(
                out=o[:, c, 0:ncols], in0=x_sb[:, c, 0:ncols], scalar=g_tt[:, g_off + c:g_off + c + 1],
                in1=rstd[:, 0:ncols], op0=ALU.mult, op1=ALU.mult), reads=[x_sb, g_tt, rstd], writes=[o])


def emit_p2(nc, S, Tc, io):
    if True:
        if True:
            pass
        xT, xo, mT_all, oh_d, pT = io["x_in"], io["x_out"], io["mT_all"], io["oh"], io["pT"]
        w_o, w_up, w_dn, w_pg, w_pl, gv_d = io["w_o"], io["w_up"], io["w_dn"], io["w_pg"], io["w_pl"], io["gv"]
        hb, hf = io["hb"], io["hf"]
        S.begin_stage()
        oh = S.sbuf("oh_sb", [128, 4], F32)
        S.dma(oh[:, :], oh_d[:, :], reads=[oh_d], writes=[oh])
        cand = Rot([S.sbuf("cand%d" % i, [128, KC, TQ], BF16) for i in range(1)])
        cst = S.sbuf("cst", [128, 256], BF16)
        gv = S.sbuf("gvs", [128, 64], F32)
        x_sb = S.sbuf("x_sb", [128, KC, TQ], F32)
        a16 = S.sbuf("a16", [128, KC, TQ], BF16)
        h_bf = S.sbuf("h_bf", [128, KC, TQ], BF16)
        hid = S.sbuf("hid", [128, 4 * KC, TQ], BF16)
        p32 = S.sbuf("p32", [128, 2, TQ], F32)
        p16 = S.sbuf("p16", [128, 2, TQ], BF16)
        rstd = S.sbuf("rstd", [128, TQ], F32)
        tmpr = Rot([S.sbuf("tmp%d" % i, [128, TQ], F32) for i in range(2)])
        gate = S.sbuf("gate", [128, TQ], F32)
        WL = WLoader(S, nbuf=3)
        psr = Rot([S.psum("ps%d" % i, [128, TQ]) for i in range(4)])
        psn = S.psum("psn", [128, TQ])

        S.op("dve", lambda h: h.memset(cst[:, 0:128], 1.0), writes=[cst])
        S.op("dve", lambda h: h.memset(gv[:, 48:49], EPS), writes=[gv])
        S.dma(gv[:, 0:48], gv_d[:, :], reads=[gv_d], writes=[gv])
        ones_bf = cst[:, 0:128]
        eps_col = gv[:, 48:49]

        def fm(ap, t0):
            return ap.rearrange("(kc p) t -> p kc t", p=128)[:, :, t0:t0 + TQ]

        def wblk(w, r0, c0):
            return w[r0:r0 + D, c0:c0 + 128].rearrange("(kc p) n -> p kc n", p=128)

        for tt in range(Tc // TQ):
            t0 = tt * TQ
            S.dma(x_sb[:, :, :], fm(xT.t, t0), reads=[xT], writes=[x_sb])
            for j in range(4):
                cd = cand.get()
                gtok = j * Tc + t0
                mk = mT_all[gtok // 1024]
                S.dma(cd[:, :, :], mk.t.rearrange("(kc p) t -> p kc t", p=128)[:, :, (gtok % 1024):(gtok % 1024) + TQ], reads=[mk], writes=[cd])
                if j == 0:
                    S.op("dve", lambda h, cd=cd: h.tensor_scalar(out=a16[:, :, :], in0=cd[:, :, :], scalar1=oh[:, 0:1], scalar2=None, op0=ALU.mult),
                         reads=[cd, oh], writes=[a16])
                else:
                    S.op("dve", lambda h, cd=cd, j=j: h.scalar_tensor_tensor(out=a16[:, :, :], in0=cd[:, :, :], scalar=oh[:, j:j + 1], in1=a16[:, :, :],
                                                                           op0=ALU.mult, op1=ALU.add), reads=[cd, oh, a16], writes=[a16])
            S.dma(p32[:, :, :], fm(pT.t, t0), reads=[pT], writes=[p32])
            S.op("dve", lambda h: h.tensor_copy(out=p16[:, :, :], in_=p32[:, :, :]), reads=[p32], writes=[p16])
            for dc in range(KC):
                wb = WL.load(w_o, wblk(w_o.t, 0, dc * 128))
                ps = psr.get()
                mm(S, ps, ps[:, :], [(wb[:, k, :], a16[:, k, :]) for k in range(KC)], reads=[wb, a16])
                S.op("dve", lambda h, dc=dc, ps=ps: h.tensor_tensor(out=x_sb[:, dc, :], in0=ps[:, :], in1=x_sb[:, dc, :], op=ALU.add),
                     reads=[ps, x_sb], writes=[x_sb])
            rmsnorm_fm(S, x_sb, gv, 0, TQ, a16, psn, ones_bf, cst, eps_col, rstd, [h_bf])
            for fc in range(4 * KC):
                wb = WL.load(w_up, wblk(w_up.t, 0, fc * 128))
                ps = psr.get()
                mm(S, ps, ps[:, :], [(wb[:, k, :], h_bf[:, k, :]) for k in range(KC)], reads=[wb, h_bf])
                tmp = tmpr.get()
                S.op("act", lambda h, ps=ps, tmp=tmp: h.activation(out=tmp[:, :], in_=ps[:, :], func=AF.Relu), reads=[ps], writes=[tmp])
                S.op("dve", lambda h, fc=fc, tmp=tmp: h.tensor_tensor(out=hid[:, fc, :], in0=tmp[:, :], in1=tmp[:, :], op=ALU.mult),
                     reads=[tmp], writes=[hid])
            for dc in range(KC):
                ps = psr.get()
                for q4 in range(4):
                    wb = WL.load(w_dn, wblk(w_dn.t, q4 * D, dc * 128))
                    mm(S, ps, ps[:, :], [(wb[:, k, :], hid[:, q4 * KC + k, :]) for k in range(KC)], reads=[wb, hid],
                       start=(q4 == 0), stop=(q4 == 3))
                S.op("dve", lambda h, dc=dc, ps=ps: h.tensor_tensor(out=x_sb[:, dc, :], in0=ps[:, :], in1=x_sb[:, dc, :], op=ALU.add),
                     reads=[ps, x_sb], writes=[x_sb])
            rmsnorm_fm(S, x_sb, gv, 16, TQ, a16, psn, ones_bf, cst, eps_col, rstd, [h_bf])
            for dc in range(KC):
                wb = WL.load(w_pg, wblk(w_pg.t, 0, dc * 128))
                ps = psr.get()
                mm(S, ps, ps[:, :], [(wb[:, k, :], h_bf[:, k, :]) for k in range(KC)], reads=[wb, h_bf])
                S.op("act", lambda h, ps=ps: h.activation(out=gate[:, :], in_=ps[:, :], func=AF.Sigmoid), reads=[ps], writes=[gate])
                wb2 = WL.load(w_pl, w_pl.t[:, dc * 128:dc * 128 + 128].rearrange("(kc p) n -> p kc n", p=128), kc=2)
                ps2 = psr.get()
                mm(S, ps2, ps2[:, :], [(wb2[:, k, :], p16[:, k, :]) for k in range(2)], reads=[wb2, p16])
                tmp = tmpr.get()
                S.op("dve", lambda h, ps2=ps2, tmp=tmp: h.tensor_tensor(out=tmp[:, :], in0=ps2[:, :], in1=gate[:, :], op=ALU.mult),
                     reads=[ps2, gate], writes=[tmp])
                S.op("dve", lambda h, dc=dc, tmp=tmp: h.tensor_tensor(out=x_sb[:, dc, :], in0=tmp[:, :], in1=x_sb[:, dc, :], op=ALU.add),
                     reads=[tmp, x_sb], writes=[x_sb])
            S.dma(fm(xo.t, t0), x_sb[:, :, :], reads=[x_sb], writes=[xo])
            rmsnorm_fm(S, x_sb, gv, 32, TQ, a16, psn, ones_bf, cst, eps_col, rstd, [h_bf])
            for hh in range(2):
                hk = hb[t0 // 256 + hh]
                S.dma(hk.t.rearrange("(kc p) t -> p kc t", p=128), h_bf[:, :, hh * 256:(hh + 1) * 256], reads=[h_bf], writes=[hk])
            if hf is not None:
                for c in range(KC):
                    S.op("dve", lambda h, c=c: h.scalar_tensor_tensor(
                        out=x_sb[:, c, :], in0=x_sb[:, c, :], scalar=gv[:, 32 + c:33 + c], in1=rstd[:, :],
                        op0=ALU.mult, op1=ALU.mult), reads=[x_sb, gv, rstd], writes=[x_sb])
                S.dma(fm(hf.t, t0), x_sb[:, :, :], reads=[x_sb], writes=[hf])
        S.end_stage()


CF = {"ident": 0, "U": 128, "ssdneg": 256, "psw": 384, "pbig": 512, "invf": 768, "sgn": 769, "eps": 770, "one": 771,
      "npi": 772, "n": 776}
TWO_PI = 2.0 * np.pi
PI_IN = 3.1415925


def cb_layout(T):
    o = {}
    c = 0
    for nm, w in (("ident", 128), ("ones", 128), ("cmask", 5 * TQ), ("wmask", 8 * TQ), ("ebig", T), ("ovl1", 4 * 129),
                  ("onesel12", 144), ("sel12", 12 * 128)):
        o[nm] = c
        c += w
    o["n"] = c
    return o


def make_consts(T):
    p = np.arange(128)
    cF = np.zeros((128, CF["n"]), np.float32)
    cF[:, 0:128] = np.eye(128)
    cF[:, 128:256] = (p[:, None] <= p[None, :])
    cF[:, 256:384] = np.where(p[None, :] >= p[:, None], 0.0, NEG)
    cF[:, 384:512] = (p[:, None] == (p[None, :] + 64) % 128)
    xx = np.arange(256)[None, :]
    lo = (p[:, None] < 64)
    pb = np.zeros((128, 256), np.float32)
    pb[:, :] = np.where(xx > 129, -1e9, 0.0)
    pb[:, 127] = np.where(p < 64, 1e4, 0.0)
    pb[:, 128] = 1e4
    pb[:, 129] = np.where(p < 64, -1e9, 1e4)
    cF[:, 512:768] = pb
    invf = (1.0 / (np.float32(10000.0) ** (np.arange(0, 128, 2, dtype=np.float32) / np.float32(128)))).astype(np.float32)
    cF[:, 768] = invf[p % 64]
    cF[:, 769] = np.where(p < 64, -1.0, 1.0)
    cF[:, 770] = EPS
    cF[:, 771] = 1.0
    cF[:, 772] = -np.pi
    L = cb_layout(T)
    cB = np.zeros((128, L["n"]), np.float32)
    cB[:, L["ident"]:L["ident"] + 128] = np.eye(128)
    cB[:, L["ones"]:L["ones"] + 128] = 1.0
    x = np.arange(TQ)[None, :]
    for di in range(5):
        delta = -2048 + 512 * di
        cB[:, L["cmask"] + di * TQ:L["cmask"] + (di + 1) * TQ] = np.where(16 * p[:, None] + 31 + delta <= x, 0.0, NEG)
    for j in range(8):
        kk = 128 * j + p[:, None] - 512
        cB[:, L["wmask"] + j * TQ:L["wmask"] + (j + 1) * TQ] = np.where((x >= kk) & (x < kk + 512), 0.0, NEG)
    key = np.arange(T)[None, :]
    cB[:, L["ebig"]:L["ebig"] + T] = (key // 64 == p[:, None])
    for i in range(4):
        n = 128 * i + p[:, None]
        j = np.arange(128)[None, :]
        ov = (n < 4 * j + 4) & (n > 4 * j - 2)
        cB[:, L["ovl1"] + i * 129:L["ovl1"] + i * 129 + 128] = ov
        cB[:, L["ovl1"] + i * 129 + 128] = 1.0
    for r in range(12):
        cB[:, L["onesel12"] + r * 12 + r] = 1.0
        cB[r, L["sel12"] + r * 128:L["sel12"] + (r + 1) * 128] = 1.0
    return cF, cB.astype(NPBF)


FM = ["q0", "q1", "q2", "q3", "kc", "ks", "kw", "vc"] + ["cb%d" % i for i in range(4)] + ["cc%d" % i for i in range(4)] \
    + ["u%d" % i for i in range(4)] + ["z%d" % i for i in range(4)] + ["xb%d" % i for i in range(6)] \
    + ["gm%d" % i for i in range(12)]
NFM = len(FM)
COL_V = NFM * 128
COL_GN = COL_V + 256
COL_DT = COL_GN + 12
NWC = COL_DT + 8
PRM = {"scw": 0, "sdw": 12, "sdb": 36, "dsk": 42, "ng": 46, "kb1": 50, "kb2": 51, "vb1": 52, "dtb": 53, "alog": 54, "n": 56}


def make_p1_scratch(S, T):
    scr = {}
    scr["qT_d"] = S.dram("qT_d", [4, 128, T], BF16)
    scr["kT_d"] = {n: S.dram(n + "T_d", [128, T], BF16) for n in ("kc", "ks", "kw", "vc")}
    scr["vtok_d"] = S.dram("vtok_d", [T, 256], BF16)
    for nm, r in (("cb_d", 512), ("cu_d", 512), ("z_d", 512), ("xb_d", 768), ("gm_d", 1536), ("gn_d", 12), ("dt_d", 8),
                  ("mA_d", 512), ("mB_d", 512), ("mC_d", 512)):
        scr[nm] = S.dram(nm, [r, T], F32)
    return scr


def emit_p1(nc, S, T, io, scr):
    nQT = T // TQ
    NKT = T // 128
    NCMP = (T - 32) // 16 + 1
    L = cb_layout(T)
    scale = 128 ** -0.5
    if True:
        hT_fn = io["hT_fn"]
        wcat, pos, cF_d, cB_d, prm_d, peT_d = io["wcat"], io["pos"], io["cF"], io["cB"], io["prm"], io["peT"]
        w1_d, w2_d, vb2_d, mT = io["w1"], io["w2"], io["vb2"], io["mT"]
        qT_d, kT_d, vtok_d = scr["qT_d"], scr["kT_d"], scr["vtok_d"]
        cb_d, cu_d, z_d, xb_d, gm_d, gn_d, dt_d = scr["cb_d"], scr["cu_d"], scr["z_d"], scr["xb_d"], scr["gm_d"], scr["gn_d"], scr["dt_d"]
        mA_d, mB_d, mC_d = scr["mA_d"], scr["mB_d"], scr["mC_d"]
        S.push_scope()
        cF = S.sbuf("cF_sb", [128, CF["n"]], F32)
        cB = S.sbuf("cB_sb", [128, L["n"]], BF16)
        prm = S.sbuf("prm_sb", [128, PRM["n"]], F32)
        S.dma(cF[:, :], cF_d[:, :], reads=[cF_d], writes=[cF])
        S.dma(cB[:, :], cB_d[:, :], reads=[cB_d], writes=[cB])
        S.dma(prm[:, :], prm_d[:, :], reads=[prm_d], writes=[prm])
        identF = cF[:, 0:128]
        identB = cB[:, L["ident"]:L["ident"] + 128]
        onesB = cB[:, L["ones"]:L["ones"] + 128]
        eps_col = cF[:, 770:771]

        def fcol(name, i=0):
            c = CF[name] + i
            return cF[:, c:c + 1]

        def pcol(name, i=0, rows=128):
            c = PRM[name] + i
            return prm[0:rows, c:c + 1]

        S.begin_stage()
        h_sb = S.sbuf("h_sb", [128, KC, TQ], BF16)
        WL = WLoader(S, nbuf=3, cast_engs=("pool",))
        psr = Rot([S.psum("s1ps%d" % i, [128, TQ]) for i in range(4)])
        ps_sw = S.psum("s1sw", [128, TQ])
        f32r = Rot([S.sbuf("s1f%d" % i, [128, TQ], F32) for i in range(4)])
        b16r = Rot([S.sbuf("s1b%d" % i, [128, TQ], BF16) for i in range(3)])
        posi = S.sbuf("posi", [128, TQ], I32)
        ang = S.sbuf("ang", [128, TQ], F32)
        kf = S.sbuf("kf", [128, TQ], F32)
        ki = S.sbuf("ki", [128, TQ], I32)
        rr = S.sbuf("rr", [128, TQ], F32)
        rc = S.sbuf("rc", [128, TQ], F32)
        fx = S.sbuf("fx", [128, TQ], F32)
        cos_t = S.sbuf("cos_t", [128, TQ], F32)
        sin_t = S.sbuf("sin_t", [128, TQ], F32)
        xr = S.sbuf("xr", [128, TQ], F32)
        t1 = S.sbuf("t1", [128, TQ], F32)
        t2 = S.sbuf("t2", [128, TQ], F32)
        cc_sb = S.sbuf("cc_sb", [128, 4, TQ], F32)
        vt_sb = Rot([S.sbuf("vt%d" % i, [128, 256], BF16) for i in range(2)])

        def wrap_pi(r):
            S.op("dve", lambda h: h.tensor_scalar(out=fx[:, :], in0=r[:, :], scalar1=float(np.pi), scalar2=-TWO_PI,
                                                  op0=ALU.is_gt, op1=ALU.mult), reads=[r], writes=[fx])
            S.op("dve", lambda h: h.tensor_tensor(out=r[:, :], in0=r[:, :], in1=fx[:, :], op=ALU.add), reads=[r, fx], writes=[r])
            S.op("dve", lambda h: h.tensor_scalar(out=r[:, :], in0=r[:, :], scalar1=-PI_IN, scalar2=PI_IN,
                                                  op0=ALU.max, op1=ALU.min), reads=[r], writes=[r])

        G_ = 2 if nQT % 2 == 0 else 1
        h_sbs = [h_sb] + [S.sbuf("h_sbx%d" % i, [128, KC, TQ], BF16) for i in range(G_ - 1)]
        cos_ts = [cos_t] + [S.sbuf("cos_x%d" % i, [128, TQ], F32) for i in range(G_ - 1)]
        sin_ts = [sin_t] + [S.sbuf("sin_x%d" % i, [128, TQ], F32) for i in range(G_ - 1)]
        cc_sbs = [cc_sb] + [S.sbuf("cc_x%d" % i, [128, 4, TQ], F32) for i in range(G_ - 1)]
        for tg in range(nQT // G_):
            for s_ in range(G_):
                tt = tg * G_ + s_
                h_sb, cos_t, sin_t, cc_sb = h_sbs[s_], cos_ts[s_], sin_ts[s_], cc_sbs[s_]
                t0 = tt * TQ
                for (c_lo, c_hi, h_ap, h_tt) in hT_fn(tt):
                    S.dma(h_sb[:, :, c_lo:c_hi], h_ap, reads=[h_tt], writes=[h_sb])
                S.dma(posi[:, :], pos.t[0:1, t0:t0 + TQ].partition_broadcast(128), reads=[pos], writes=[posi])
                S.op("dve", lambda h: h.tensor_copy(out=ang[:, :], in_=posi[:, :]), reads=[posi], writes=[ang])
                S.op("dve", lambda h: h.tensor_scalar(out=ang[:, :], in0=ang[:, :], scalar1=fcol("invf"), scalar2=None, op0=ALU.mult),
                     reads=[ang, cF], writes=[ang])
                S.op("dve", lambda h: h.tensor_scalar(out=kf[:, :], in0=ang[:, :], scalar1=float(1.0 / TWO_PI), scalar2=None, op0=ALU.mult),
                     reads=[ang], writes=[kf])
                S.op("dve", lambda h: h.tensor_copy(out=ki[:, :], in_=kf[:, :]), reads=[kf], writes=[ki])
                S.op("dve", lambda h: h.tensor_copy(out=kf[:, :], in_=ki[:, :]), reads=[ki], writes=[kf])
                C1 = 6.28125
                C2 = float(TWO_PI - 6.28125)
                S.op("dve", lambda h: h.scalar_tensor_tensor(out=rr[:, :], in0=kf[:, :], scalar=-C1, in1=ang[:, :], op0=ALU.mult, op1=ALU.add),
                     reads=[kf, ang], writes=[rr])
                S.op("dve", lambda h: h.scalar_tensor_tensor(out=rr[:, :], in0=kf[:, :], scalar=-C2, in1=rr[:, :], op0=ALU.mult, op1=ALU.add),
                     reads=[kf, rr], writes=[rr])
                S.op("dve", lambda h: h.tensor_scalar(out=fx[:, :], in0=rr[:, :], scalar1=float(-np.pi), scalar2=TWO_PI,
                                                      op0=ALU.is_lt, op1=ALU.mult), reads=[rr], writes=[fx])
                S.op("dve", lambda h: h.tensor_tensor(out=rr[:, :], in0=rr[:, :], in1=fx[:, :], op=ALU.add), reads=[rr, fx], writes=[rr])
                S.op("dve", lambda h: h.tensor_scalar(out=rc[:, :], in0=rr[:, :], scalar1=float(np.pi / 2), scalar2=None, op0=ALU.add),
                     reads=[rr], writes=[rc])
                wrap_pi(rr)
                wrap_pi(rc)
                S.op("act", lambda h, sin_t=sin_t: h.activation(out=sin_t[:, :], in_=rr[:, :], func=AF.Sin, scale=fcol("sgn")), reads=[rr, cF], writes=[sin_t])
                S.op("act", lambda h, cos_t=cos_t: h.activation(out=cos_t[:, :], in_=rc[:, :], func=AF.Sin), reads=[rc], writes=[cos_t])
            def _ld(ci):
                return WL.load(wcat, wcat.t[:, ci * 128:(ci + 1) * 128].rearrange("(kc p) n -> p kc n", p=128))
            pre = [_ld(0), _ld(1)]
            for ci, nm in enumerate(FM):
                wb = pre.pop(0)
                if ci + 2 < NFM:
                    pre.append(_ld(ci + 2))
                for s_ in range(G_):
                    t0 = (tg * G_ + s_) * TQ
                    h_sb, cos_t, sin_t, cc_sb = h_sbs[s_], cos_ts[s_], sin_ts[s_], cc_sbs[s_]
                    ps = psr.get()
                    mm(S, ps, ps[:, :], [(wb[:, k, :], h_sb[:, k, :]) for k in range(KC)], reads=[wb, h_sb])
                    if nm in ("q0", "q1", "q2", "q3", "kc", "ks", "kw"):
                        S.op("act", lambda h, ps=ps: h.activation(out=xr[:, :], in_=ps[:, :], func=AF.Copy), reads=[ps], writes=[xr])
                        mm(S, ps_sw, ps_sw[:, :], [(cF[:, 384:512], xr[:, :])], reads=[cF, xr])
                        S.op("dve", lambda h, cos_t=cos_t: h.tensor_tensor(out=t1[:, :], in0=xr[:, :], in1=cos_t[:, :], op=ALU.mult), reads=[xr, cos_t], writes=[t1])
                        S.op("dve", lambda h, sin_t=sin_t: h.tensor_tensor(out=t2[:, :], in0=ps_sw[:, :], in1=sin_t[:, :], op=ALU.mult), reads=[ps_sw, sin_t], writes=[t2])
                        ob = b16r.get()
                        S.op("dve", lambda h, ob=ob: h.tensor_tensor(out=ob[:, :], in0=t1[:, :], in1=t2[:, :], op=ALU.add), reads=[t1, t2], writes=[ob])
                        if nm[0] == "q":
                            S.dma(qT_d.t[int(nm[1]), :, t0:t0 + TQ], ob[:, :], reads=[ob], writes=[qT_d])
                        else:
                            S.dma(kT_d[nm].t[:, t0:t0 + TQ], ob[:, :], reads=[ob], writes=[kT_d[nm]])
                    elif nm == "vc":
                        ob = b16r.get()
                        S.op("act", lambda h, ps=ps, ob=ob: h.activation(out=ob[:, :], in_=ps[:, :], func=AF.Copy), reads=[ps], writes=[ob])
                        S.dma(kT_d["vc"].t[:, t0:t0 + TQ], ob[:, :], reads=[ob], writes=[kT_d["vc"]])
                    elif nm.startswith("cc"):
                        c = int(nm[2])
                        S.op("act", lambda h, ps=ps, c=c, cc_sb=cc_sb: h.activation(out=cc_sb[:, c, :], in_=ps[:, :], func=AF.Copy), reads=[ps], writes=[cc_sb])
                    elif nm[0] == "u":
                        c = int(nm[1])
                        of = f32r.get()
                        S.op("dve", lambda h, ps=ps, c=c, of=of, cc_sb=cc_sb: h.tensor_tensor(out=of[:, :], in0=ps[:, :], in1=cc_sb[:, c, :], op=ALU.mult),
                             reads=[ps, cc_sb], writes=[of])
                        S.dma(cu_d.t[c * 128:(c + 1) * 128, t0:t0 + TQ], of[:, :], reads=[of], writes=[cu_d])
                    else:
                        of = f32r.get()
                        fn = AF.Sigmoid if nm.startswith("gm") else AF.Copy
                        S.op("act", lambda h, ps=ps, of=of, fn=fn: h.activation(out=of[:, :], in_=ps[:, :], func=fn), reads=[ps], writes=[of])
                        if nm.startswith("cb"):
                            dst, c = cb_d, int(nm[2])
                        elif nm[0] == "z":
                            dst, c = z_d, int(nm[1])
                        elif nm.startswith("xb"):
                            dst, c = xb_d, int(nm[2])
                        else:
                            dst, c = gm_d, int(nm[2:])
                        S.dma(dst.t[c * 128:(c + 1) * 128, t0:t0 + TQ], of[:, :], reads=[of], writes=[dst])
            for (c0, ncol, dst, fn) in ((COL_GN, 12, gn_d, AF.Sigmoid), (COL_DT, 8, dt_d, AF.Copy)):
                wb = WL.load(wcat, wcat.t[:, c0:c0 + ncol].rearrange("(kc p) n -> p kc n", p=128), ncol=ncol)
                for s_ in range(G_):
                    t0 = (tg * G_ + s_) * TQ
                    h_sb, cos_t, sin_t, cc_sb = h_sbs[s_], cos_ts[s_], sin_ts[s_], cc_sbs[s_]
                    ps = psr.get()
                    mm(S, ps, ps[0:ncol, :], [(wb[:, k, 0:ncol], h_sb[:, k, :]) for k in range(KC)], reads=[wb, h_sb])
                    of = f32r.get()
                    S.op("act", lambda h, ps=ps, of=of, fn=fn, ncol=ncol: h.activation(out=of[0:ncol, :], in_=ps[0:ncol, :], func=fn), reads=[ps], writes=[of])
                    S.dma(dst.t[:, t0:t0 + TQ], of[0:ncol, :], reads=[of], writes=[dst])
            wv = [WL.load(wcat, wcat.t[:, COL_V + i * 128:COL_V + (i + 1) * 128].rearrange("(kc p) n -> p kc n", p=128)) for i in range(2)]
            for s_ in range(G_):
                t0 = (tg * G_ + s_) * TQ
                h_sb, cos_t, sin_t, cc_sb = h_sbs[s_], cos_ts[s_], sin_ts[s_], cc_sbs[s_]
                for s4 in range(4):
                    ps = psr.get()
                    for i in range(2):
                        mm(S, ps, ps[:, i * 128:(i + 1) * 128], [(h_sb[:, k, s4 * 128:(s4 + 1) * 128], wv[i][:, k, :]) for k in range(KC)],
                           reads=[wv[i], h_sb])
                    vt = vt_sb.get()
                    S.op("act", lambda h, ps=ps, vt=vt: h.activation(out=vt[:, :], in_=ps[:, 0:256], func=AF.Copy), reads=[ps], writes=[vt])
                    S.dma(vtok_d.t[t0 + s4 * 128:t0 + (s4 + 1) * 128, :], vt[:, :], reads=[vt], writes=[vtok_d])
        S.end_stage()
        build_p1_rest(nc, S, locals())
        S.pop_scope()


def build_p1_rest(nc, S, V):
    T, nQT, NKT, NCMP, L, scale = V["T"], V["nQT"], V["NKT"], V["NCMP"], V["L"], V["scale"]
    cF, cB, prm = V["cF"], V["cB"], V["prm"]
    identF, identB, onesB, eps_col = V["identF"], V["identB"], V["onesB"], V["eps_col"]
    fcol, pcol = V["fcol"], V["pcol"]
    qT_d, kT_d, vtok_d = V["qT_d"], V["kT_d"], V["vtok_d"]
    cb_d, cu_d, z_d, xb_d, gm_d, gn_d, dt_d = V["cb_d"], V["cu_d"], V["z_d"], V["xb_d"], V["gm_d"], V["gn_d"], V["dt_d"]
    mA_d, mB_d, mC_d, mT = V["mA_d"], V["mB_d"], V["mC_d"], V["mT"]
    peT_d, w1_d, w2_d, vb2_d = V["peT_d"], V["w1_d"], V["w2_d"], V["vb2_d"]

    def bc(ap, shape):
        return ap.to_broadcast(list(shape))

    S.begin_stage()
    cur = Rot([S.sbuf("cu%d" % i, [128, TQ + 2], F32) for i in range(2)])
    cbr = Rot([S.sbuf("cbt%d" % i, [128, TQ], F32) for i in range(2)])
    gmr = Rot([S.sbuf("gmt%d" % i, [128, TQ], F32) for i in range(2)])
    acr = Rot([S.sbuf("acc%d" % i, [128, TQ], F32) for i in range(2)])
    for tt in range(nQT):
        t0 = tt * TQ
        for c in range(4):
            cu, cbt, gmt, acc = cur.get(), cbr.get(), gmr.get(), acr.get()
            rows = slice(c * 128, (c + 1) * 128)
            if tt == 0:
                S.op("dve", lambda h, cu=cu: h.memset(cu[:, 0:2], 0.0), writes=[cu])
                S.dma(cu[:, 2:TQ + 2], cu_d.t[rows, 0:TQ], reads=[cu_d], writes=[cu])
            else:
                S.dma(cu[:, :], cu_d.t[rows, t0 - 2:t0 + TQ], reads=[cu_d], writes=[cu])
            S.dma(cbt[:, :], cb_d.t[rows, t0:t0 + TQ], reads=[cb_d], writes=[cbt])
            S.dma(gmt[:, :], gm_d.t[(4 + c) * 128:(5 + c) * 128, t0:t0 + TQ], reads=[gm_d], writes=[gmt])
            S.op("dve", lambda h, cu=cu, acc=acc, c=c: h.tensor_scalar(out=acc[:, :], in0=cu[:, 0:TQ], scalar1=pcol("scw", c * 3), scalar2=None, op0=ALU.mult),
                 reads=[cu, prm], writes=[acc])
            for j in (1, 2):
                S.op("dve", lambda h, cu=cu, acc=acc, c=c, j=j: h.scalar_tensor_tensor(
                    out=acc[:, :], in0=cu[:, j:j + TQ], scalar=pcol("scw", c * 3 + j), in1=acc[:, :], op0=ALU.mult, op1=ALU.add),
                    reads=[cu, prm, acc], writes=[acc])
            S.op("dve", lambda h, acc=acc, cbt=cbt: h.tensor_tensor(out=acc[:, :], in0=acc[:, :], in1=cbt[:, :], op=ALU.mult), reads=[acc, cbt], writes=[acc])
            S.op("dve", lambda h, acc=acc, gmt=gmt: h.tensor_tensor(out=acc[:, :], in0=acc[:, :], in1=gmt[:, :], op=ALU.mult), reads=[acc, gmt], writes=[acc])
            S.dma(mB_d.t[rows, t0:t0 + TQ], acc[:, :], reads=[acc], writes=[mB_d])
    S.end_stage()

    S.begin_stage()
    CH = 128
    xbr = Rot([S.sbuf("xb%d" % i, [128, 6, CH + 3], F32) for i in range(2)])
    xc = S.sbuf("xc", [128, 6, CH], F32)
    acc6 = S.sbuf("acc6", [128, 6, CH], F32)
    BT = S.sbuf("BT", [128, CH], BF16)
    CT = S.sbuf("CT", [128, CH], BF16)
    Bk = S.sbuf("Bk", [128, CH], BF16)
    dtr = S.sbuf("dtr", [8, CH], F32)
    dtT = S.sbuf("dtT", [8, CH], F32)
    daT = S.sbuf("daT", [8, CH], F32)
    nA = S.sbuf("nA", [8, 2], F32)
    dtk = S.sbuf("dtk", [128, 8], F32)
    dak = S.sbuf("dak", [128, 8], F32)
    nacol = S.sbuf("nacol", [128, 8], F32)
    alast = S.sbuf("alast", [128, 8], F32)
    decay = S.sbuf("decay", [128, 8], F32)
    wcol = S.sbuf("wcol", [128, 8], F32)
    darep = S.sbuf("darep", [128, 8, CH], F32)
    diffm = S.sbuf("diffm", [128, 8, CH], F32)
    seg = S.sbuf("seg", [128, 8, CH], F32)
    ea = S.sbuf("ea", [128, 8, CH], F32)
    G = S.sbuf("G", [128, 8, CH], BF16)
    Cexp = S.sbuf("Cexp", [128, 8, CH], BF16)
    xdt32 = S.sbuf("xdt32", [128, 8, 64], F32)
    xdtw = S.sbuf("xdtw", [128, 8, 64], BF16)
    xdt_pad = S.sbuf("xdt_pad", [128, 8, 128], BF16)
    S_pad = S.sbuf("S_pad", [128, 8, 128], BF16)
    S32 = S.sbuf("S32", [128, 8, 64], F32)
    zr = Rot([S.sbuf("zs%d" % i, [128, 4, CH], F32) for i in range(2)])
    g2r = Rot([S.sbuf("g2s%d" % i, [128, 4, CH], F32) for i in range(2)])
    sz = S.sbuf("sz", [128, 4, CH], F32)
    yv = S.sbuf("yv", [128, 4, CH], F32)
    sq4 = S.sbuf("sq4", [128, 4, CH], BF16)
    rs = S.sbuf("rs", [128, CH], F32)
    ycr = Rot([S.sbuf("yc%d" % i, [128, 4, CH], F32) for i in range(2)])
    p_t = S.psum("p_t", [128, TQ])
    p_t2 = S.psum("p_t2", [128, TQ])
    p_ar = S.psum("p_ar", [128, 1024])
    p_cb = S.psum("p_cb", [128, TQ])
    p_y = S.psum("p_y", [128, TQ])
    p_cs = S.psum("p_cs", [128, TQ])
    p_bt = S.psum("p_bt", [128, 256], BF16)
    Umat = cF[:, 128:256]
    ssdneg = cF[:, 256:384]
    S.op("dve", lambda h: h.memset(xdt_pad[:, :, :], 0.0), writes=[xdt_pad])
    S.op("dve", lambda h: h.memset(S_pad[:, :, :], 0.0), writes=[S_pad])
    S.op("dve", lambda h: h.memset(S32[:, :, :], 0.0), writes=[S32])
    S.op("act", lambda h: h.activation(out=nA[:, 0:1], in_=pcol("alog", rows=8), func=AF.Exp), reads=[prm], writes=[nA])
    S.op("dve", lambda h: h.tensor_scalar(out=nA[:, 1:2], in0=nA[:, 0:1], scalar1=-1.0, scalar2=None, op0=ALU.mult), reads=[nA], writes=[nA])
    ar3 = p_ar.t[:, :].rearrange("p (h l) -> p h l", h=8)

    def pad_copy(dst, src32):
        d5 = dst.t[:, :, :].rearrange("p (a e) (s q) -> p a e s q", e=2, s=2)
        s4 = src32.t[:, :, :].rearrange("p (a e) q -> p a e q", e=2)
        for e in range(2):
            S.op("dve", lambda h, e=e: h.tensor_copy(out=d5[:, :, e, e, :], in_=s4[:, :, e, :]), reads=[src32], writes=[dst])

    for ch in range(T // CH):
        t0 = ch * CH
        xb = xbr.get()
        src = xb_d.t.rearrange("(c p) t -> p c t", p=128)
        if ch == 0:
            S.op("dve", lambda h, xb=xb: h.memset(xb[:, :, 0:3], 0.0), writes=[xb])
            S.dma(xb[:, :, 3:CH + 3], src[:, :, 0:CH], reads=[xb_d], writes=[xb])
        else:
            S.dma(xb[:, :, :], src[:, :, t0 - 3:t0 + CH], reads=[xb_d], writes=[xb])
        S.dma(dtr[:, :], dt_d.t[:, t0:t0 + CH], reads=[dt_d], writes=[dtr])
        zs, g2s = zr.get(), g2r.get()
        S.dma(zs[:, :, :], z_d.t.rearrange("(c p) t -> p c t", p=128)[:, :, t0:t0 + CH], reads=[z_d], writes=[zs])
        S.dma(g2s[:, :, :], gm_d.t[1024:1536, :].rearrange("(c p) t -> p c t", p=128)[:, :, t0:t0 + CH], reads=[gm_d], writes=[g2s])
        for c in range(6):
            S.op("dve", lambda h, xb=xb, c=c: h.tensor_scalar(out=acc6[:, c, :], in0=xb[:, c, 0:CH], scalar1=pcol("sdw", c * 4), scalar2=None, op0=ALU.mult),
                 reads=[xb, prm], writes=[acc6])
            for j in (1, 2, 3):
                S.op("dve", lambda h, xb=xb, c=c, j=j: h.scalar_tensor_tensor(
                    out=acc6[:, c, :], in0=xb[:, c, j:j + CH], scalar=pcol("sdw", c * 4 + j), in1=acc6[:, c, :], op0=ALU.mult, op1=ALU.add),
                    reads=[xb, prm, acc6], writes=[acc6])
            S.op("act", lambda h, c=c: h.activation(out=xc[:, c, :], in_=acc6[:, c, :], func=AF.Silu, bias=pcol("sdb", c)), reads=[acc6, prm], writes=[xc])
        S.op("dve", lambda h: h.tensor_copy(out=BT[:, :], in_=xc[:, 4, :]), reads=[xc], writes=[BT])
        S.op("dve", lambda h: h.tensor_copy(out=CT[:, :], in_=xc[:, 5, :]), reads=[xc], writes=[CT])
        S.op("act", lambda h: h.activation(out=dtT[:, :], in_=dtr[:, :], func=AF.Exp, bias=pcol("dtb", rows=8)), reads=[dtr, prm], writes=[dtT])
        S.op("act", lambda h: h.activation(out=dtT[:, :], in_=dtT[:, :], func=AF.Ln, bias=cF[0:8, 771:772]), reads=[dtT, cF], writes=[dtT])
        S.op("dve", lambda h: h.tensor_scalar(out=daT[:, :], in0=dtT[:, :], scalar1=nA[:, 1:2], scalar2=None, op0=ALU.mult), reads=[dtT, nA], writes=[daT])
        S.op("pe", lambda h: h.transpose(out=p_t[:, 0:8], in_=dtT[:, :], identity=cF[0:8, 0:8]), reads=[dtT, cF], writes=[p_t])
        S.op("pe", lambda h: h.transpose(out=p_t[:, 8:16], in_=daT[:, :], identity=cF[0:8, 0:8]), reads=[daT, cF], writes=[p_t])
        S.op("dve", lambda h: h.tensor_copy(out=dtk[:, :], in_=p_t[:, 0:8]), reads=[p_t], writes=[dtk])
        S.op("dve", lambda h: h.tensor_copy(out=dak[:, :], in_=p_t[:, 8:16]), reads=[p_t], writes=[dak])
        mm(S, p_t, p_t[:, 16:24], [(Umat, dak[:, :])], reads=[cF, dak])
        S.op("dve", lambda h: h.tensor_copy(out=darep[:, :, :], in_=bc(dak[:, 0:8].unsqueeze(2), [128, 8, CH])), reads=[dak], writes=[darep])
        for hd in range(8):
            mm(S, p_ar, ar3[:, hd, :], [(darep[:, hd, :], Umat)], reads=[darep, cF])
        S.op("dve", lambda h: h.tensor_scalar(out=nacol[:, :], in0=p_t[:, 16:24], scalar1=-1.0, scalar2=None, op0=ALU.mult), reads=[p_t], writes=[nacol])
        S.op("dve", lambda h: h.tensor_copy(out=alast[:, :], in_=ar3[:, :, CH - 1]), reads=[p_ar], writes=[alast])
        S.op("act", lambda h: h.activation(out=decay[:, :], in_=alast[:, :], func=AF.Exp), reads=[alast], writes=[decay])
        S.op("dve", lambda h: h.tensor_tensor(out=wcol[:, :], in0=alast[:, :], in1=nacol[:, :], op=ALU.add), reads=[alast, nacol], writes=[wcol])
        S.op("act", lambda h: h.activation(out=wcol[:, :], in_=wcol[:, :], func=AF.Exp), reads=[wcol], writes=[wcol])
        for c in range(4):
            S.op("pe", lambda h, c=c: h.transpose(out=p_t2[:, c * 128:(c + 1) * 128], in_=xc[:, c, :], identity=identF), reads=[xc, cF], writes=[p_t2])
        pt3 = p_t2.t[:, :].rearrange("p (h q) -> p h q", h=8)
        S.op("dve", lambda h: h.tensor_tensor(out=xdt32[:, :, :], in0=pt3, in1=bc(dtk[:, 0:8].unsqueeze(2), [128, 8, 64]), op=ALU.mult),
             reads=[p_t2, dtk], writes=[xdt32])
        pad_copy(xdt_pad, xdt32)
        S.op("dve", lambda h: h.tensor_tensor(out=xdtw[:, :, :], in0=xdt32[:, :, :], in1=bc(wcol[:, 0:8].unsqueeze(2), [128, 8, 64]), op=ALU.mult),
             reads=[xdt32, wcol], writes=[xdtw])
        S.op("pe", lambda h: h.transpose(out=p_bt[:, 0:128], in_=BT[:, :], identity=identB), reads=[BT, cB], writes=[p_bt])
        S.op("dve", lambda h: h.tensor_copy(out=Bk[:, :], in_=p_bt[:, 0:128]), reads=[p_bt], writes=[Bk])
        mm(S, p_cb, p_cb[:, 0:CH], [(BT[:, :], CT[:, :])], reads=[BT, CT])
        S.op("dve", lambda h: h.tensor_tensor(out=diffm[:, :, :], in0=ar3, in1=bc(ssdneg.unsqueeze(1), [128, 8, CH]), op=ALU.add),
             reads=[p_ar, cF], writes=[diffm])
        for hd in range(8):
            S.op("act", lambda h, hd=hd: h.activation(out=seg[:, hd, :], in_=diffm[:, hd, :], func=AF.Exp, bias=nacol[:, hd:hd + 1]),
                 reads=[diffm, nacol], writes=[seg])
        S.op("dve", lambda h: h.tensor_tensor(out=G[:, :, :], in0=seg[:, :, :], in1=bc(p_cb[:, 0:CH].unsqueeze(1), [128, 8, CH]), op=ALU.mult),
             reads=[seg, p_cb], writes=[G])
        for half in range(2):
            S.op("act", lambda h, half=half: h.activation(out=ea[:, half * 4:(half + 1) * 4, :], in_=ar3[:, half * 4:(half + 1) * 4, :], func=AF.Exp),
                 reads=[p_ar], writes=[ea])
        S.op("dve", lambda h: h.tensor_tensor(out=Cexp[:, :, :], in0=ea[:, :, :], in1=bc(CT[:, :].unsqueeze(1), [128, 8, CH]), op=ALU.mult),
             reads=[ea, CT], writes=[Cexp])
        for c in range(4):
            pairs = []
            for e in range(2):
                pairs.append((xdt_pad[:, 2 * c + e, :], G[:, 2 * c + e, :]))
                pairs.append((S_pad[:, 2 * c + e, :], Cexp[:, 2 * c + e, :]))
            mm(S, p_y, p_y[:, c * 128:(c + 1) * 128], pairs, reads=[xdt_pad, G, S_pad, Cexp])
        py3 = p_y.t[:, :].rearrange("p (c l) -> p c l", c=4)
        for c in range(4):
            S.op("dve", lambda h, c=c: h.scalar_tensor_tensor(out=yv[:, c, :], in0=xc[:, c, :], scalar=pcol("dsk", c), in1=py3[:, c, :],
                                                              op0=ALU.mult, op1=ALU.add), reads=[xc, prm, p_y], writes=[yv])
        mm(S, p_cs, p_cs[:, :], [(Bk[:, :], xdtw[:, :, :].rearrange("p h q -> p (h q)"))], reads=[Bk, xdtw])
        S.op("dve", lambda h: h.tensor_tensor(out=S32[:, :, :], in0=S32[:, :, :], in1=bc(decay[:, 0:8].unsqueeze(2), [128, 8, 64]), op=ALU.mult),
             reads=[S32, decay], writes=[S32])
        S.op("dve", lambda h: h.tensor_tensor(out=S32[:, :, :], in0=S32[:, :, :], in1=p_cs.t[:, :].rearrange("p (h q) -> p h q", h=8), op=ALU.add),
             reads=[S32, p_cs], writes=[S32])
        pad_copy(S_pad, S32)
        S.op("act", lambda h, zs=zs: h.activation(out=sz[:, :, :], in_=zs[:, :, :], func=AF.Silu), reads=[zs], writes=[sz])
        S.op("dve", lambda h: h.tensor_tensor(out=yv[:, :, :], in0=yv[:, :, :], in1=sz[:, :, :], op=ALU.mult), reads=[yv, sz], writes=[yv])
        S.op("act", lambda h: h.activation(out=sq4[:, :, :], in_=yv[:, :, :], func=AF.Square), reads=[yv], writes=[sq4])
        mm(S, p_cb, p_cb[:, 128:256], [(onesB, sq4[:, c, :]) for c in range(4)], reads=[cB, sq4])
        S.op("act", lambda h: h.activation(out=rs[:, :], in_=p_cb[:, 128:256], func=AF.Ln, bias=eps_col, scale=1.0 / 512), reads=[p_cb, cF], writes=[rs])
        S.op("act", lambda h: h.activation(out=rs[:, :], in_=rs[:, :], func=AF.Exp, scale=-0.5), reads=[rs], writes=[rs])
        yc = ycr.get()
        for c in range(4):
            S.op("dve", lambda h, c=c, yc=yc: h.scalar_tensor_tensor(out=yc[:, c, :], in0=yv[:, c, :], scalar=pcol("ng", c), in1=rs[:, :],
                                                                     op0=ALU.mult, op1=ALU.mult), reads=[yv, prm, rs], writes=[yc])
        S.op("dve", lambda h, yc=yc, g2s=g2s: h.tensor_tensor(out=yc[:, :, :], in0=yc[:, :, :], in1=g2s[:, :, :], op=ALU.mult), reads=[yc, g2s], writes=[yc])
        S.dma(mC_d.t.rearrange("(c p) t -> p c t", p=128)[:, :, t0:t0 + CH], yc[:, :, :], reads=[yc], writes=[mC_d])
    S.end_stage()
    build_p1_nsa(nc, S, V)


def build_p1_nsa(nc, S, V):
    T, nQT, NKT, NCMP, L, scale = V["T"], V["nQT"], V["NKT"], V["NCMP"], V["L"], V["scale"]
    cF, cB, prm = V["cF"], V["cB"], V["prm"]
    identF, identB, onesB, eps_col = V["identF"], V["identB"], V["onesB"], V["eps_col"]
    fcol, pcol = V["fcol"], V["pcol"]
    qT_d, kT_d, vtok_d = V["qT_d"], V["kT_d"], V["vtok_d"]
    gm_d, gn_d = V["gm_d"], V["gn_d"]
    mA_d, mB_d, mC_d, mT = V["mA_d"], V["mB_d"], V["mC_d"], V["mT"]
    peT_d, w1_d, w2_d, vb2_d = V["peT_d"], V["w1_d"], V["w2_d"], V["vb2_d"]
    NC4 = (NCMP + 127) // 128
    NCP = NC4 * 128

    def cmask(di):
        return cB[:, L["cmask"] + di * TQ:L["cmask"] + (di + 1) * TQ]

    def wmask(j):
        return cB[:, L["wmask"] + j * TQ:L["wmask"] + (j + 1) * TQ]

    def ebig(kt):
        return cB[:, L["ebig"] + kt * 128:L["ebig"] + (kt + 1) * 128]

    def ovl1(i):
        return cB[:, L["ovl1"] + i * 129:L["ovl1"] + (i + 1) * 129]

    def onesel(r):
        return cB[:, L["onesel12"] + r * 12:L["onesel12"] + (r + 1) * 12]

    def sel12(r):
        return cB[0:12, L["sel12"] + r * 128:L["sel12"] + (r + 1) * 128]

    S.begin_stage()
    srcT = {n: S.sbuf(n + "T", [128, T], BF16) for n in ("kc", "ks", "kw")}
    srcT["vc"] = srcT["kc"]
    vs = S.sbuf("vs", [128, NKT, 128], BF16)
    vw = S.sbuf("vw", [128, NKT, 128], BF16)
    kcmpT = S.sbuf("kcmpT", [128, NCP], BF16)
    vcmp = S.sbuf("vcmp", [128, NC4, 128], BF16)
    for n in ("ks", "kw"):
        S.dma(srcT[n][:, :], kT_d[n].t[:, :], reads=[kT_d[n]], writes=[srcT[n]])
    vt3 = vtok_d.t.rearrange("(kt p) d -> p kt d", p=128)
    S.dma(vs[:, :, :], vt3[:, :, 0:128], reads=[vtok_d], writes=[vs])
    S.dma(vw[:, :, :], vt3[:, :, 128:256], reads=[vtok_d], writes=[vw])
    w1st = S.sbuf("w1st", [128, 32, 128], F32)
    w1bf = S.sbuf("w1bf", [128, 32, 128], BF16)
    w2st = S.sbuf("w2st", [128, 128], F32)
    w2bf = S.sbuf("w2bf", [128, 128], BF16)
    pest = S.sbuf("pest", [128, 64], F32)
    pebf = S.sbuf("pebf", [128, 64], BF16)
    vb2s = S.sbuf("vb2s", [1, 128], F32)
    vb2b = S.sbuf("vb2b", [1, 128], BF16)
    btot = S.sbuf("btot", [128, 1], F32)
    hs = S.sbuf("hs", [128, NCP], BF16)
    p_sc = Rot([S.psum("p_sc%d" % i, [128, TQ]) for i in range(2)])
    p_o = [S.psum("p_o%d" % i, [128, TQ]) for i in range(3)]
    p_den = S.psum("p_den", [128, TQ])
    p_u = S.psum("p_u", [128, TQ])
    p_x = S.psum("p_x", [128, TQ])
    S.dma(pest[:, :], peT_d[:, :], reads=[peT_d], writes=[pest])
    S.op("dve", lambda h: h.tensor_copy(out=pebf[:, :], in_=pest[:, :]), reads=[pest], writes=[pebf])
    S.dma(vb2s[:, :], vb2_d[:, :], reads=[vb2_d], writes=[vb2s])
    S.op("dve", lambda h: h.tensor_copy(out=vb2b[:, :], in_=vb2s[:, :]), reads=[vb2s], writes=[vb2b])
    S.op("dve", lambda h: h.memset(kcmpT[:, :], 0.0), writes=[kcmpT])
    for wi, nm in enumerate(("kc", "vc")):
        S.dma(w1st[:, :, :], w1_d[wi].t.rearrange("(j d) h -> d j h", d=128), reads=[w1_d[wi]], writes=[w1st])
        S.op("pool", lambda h: h.tensor_copy(out=w1bf[:, :, :], in_=w1st[:, :, :]), reads=[w1st], writes=[w1bf])
        S.dma(w2st[:, :], w2_d[wi].t[:, :], reads=[w2_d[wi]], writes=[w2st])
        S.op("dve", lambda h: h.tensor_copy(out=w2bf[:, :], in_=w2st[:, :]), reads=[w2st], writes=[w2bf])
        S.op("dve", lambda h: h.memset(hs[:, :], 0.0), writes=[hs])
        mm(S, p_x, p_x[:, 0:1], [(w1bf[:, j, :], pebf[:, wi * 32 + j:wi * 32 + j + 1]) for j in range(32)], reads=[w1bf, pebf])
        S.op("dve", lambda h, wi=wi: h.tensor_tensor(out=btot[:, :], in0=p_x[:, 0:1], in1=pcol("kb1" if wi == 0 else "vb1"), op=ALU.add),
             reads=[p_x, prm], writes=[btot])
        src = srcT[nm]
        S.dma(src[:, :], kT_d[nm].t[:, :], reads=[kT_d[nm]], writes=[src])
        for n0 in range(0, NCMP, 512):
            nn = min(512, NCMP - n0)
            ps = p_sc.get()
            mm(S, ps, ps[:, 0:nn], [(w1bf[:, j, :], src[:, 16 * n0 + j:16 * n0 + j + 16 * (nn - 1) + 1:16]) for j in range(32)], reads=[w1bf, src])
            S.op("act", lambda h, ps=ps, n0=n0, nn=nn: h.activation(out=hs[:, n0:n0 + nn], in_=ps[:, 0:nn], func=AF.Silu, bias=btot[:, 0:1]),
                 reads=[ps, btot], writes=[hs])
            if wi == 0:
                ps2 = p_sc.get()
                mm(S, ps2, ps2[:, 0:nn], [(w2bf[:, :], hs[:, n0:n0 + nn])], reads=[w2bf, hs])
                S.op("act", lambda h, ps2=ps2, n0=n0, nn=nn: h.activation(out=kcmpT[:, n0:n0 + nn], in_=ps2[:, 0:nn], func=AF.Identity, bias=pcol("kb2")),
                     reads=[ps2, prm], writes=[kcmpT])
        if wi == 1:
            for i in range(NC4):
                ps3 = p_sc.get()
                mm(S, ps3, ps3[:, 0:128], [(hs[:, 128 * i:128 * i + 128], w2bf[:, :]), (cB[0:1, L["ones"]:L["ones"] + 128], vb2b[0:1, :])],
                   reads=[hs, w2bf, cB, vb2b])
                S.op("act", lambda h, ps3=ps3, i=i: h.activation(out=vcmp[:, i, :], in_=ps3[:, 0:128], func=AF.Copy), reads=[ps3], writes=[vcmp])
    q_sb = S.sbuf("q_sb", [128, 4, TQ], BF16)
    gn12 = S.sbuf("gn12", [12, TQ], F32)
    gm0 = S.sbuf("gm0", [128, 4, TQ], F32)
    ec = [S.sbuf("ec%d" % i, [128, TQ], BF16) for i in range(NC4)]
    er = Rot([S.sbuf("er%d" % i, [128, TQ], BF16) for i in range(3)])
    o_sb = [[S.sbuf("o%d_%d" % (b, h), [128, TQ], F32) for h in range(4)] for b in range(3)]
    imp = S.sbuf("imp", [128, 4, 128], F32)
    rd = S.sbuf("rd", [128, 1], F32)
    sc2 = S.sbuf("sc2", [128, 128], F32)
    sc3 = S.sbuf("sc3", [128, 128], F32)
    m8a = S.sbuf("m8a", [128, 8], F32)
    m8b = S.sbuf("m8b", [128, 8], F32)
    negm = S.sbuf("negm", [128, 128], F32)
    negmT = S.sbuf("negmT", [128, TQ], BF16)
    den_sb = S.sbuf("den_sb", [12, TQ], F32)
    fct = S.sbuf("fct", [12, TQ], BF16)
    ya = S.sbuf("ya", [128, TQ], F32)
    tmpm = S.sbuf("tmpm", [128, TQ], F32)

    for qt in range(nQT):
        t0 = qt * TQ
        S.dma(q_sb[:, :, :], qT_d.t.rearrange("h p t -> p h t")[:, :, t0:t0 + TQ], reads=[qT_d], writes=[q_sb])
        S.dma(gn12[:, :], gn_d.t[:, t0:t0 + TQ], reads=[gn_d], writes=[gn12])
        S.dma(gm0[:, :, :], gm_d.t[0:512, :].rearrange("(c p) t -> p c t", p=128)[:, :, t0:t0 + TQ], reads=[gm_d], writes=[gm0])
        ntc = min(NC4, (32 * qt + 30) // 128 + 1)
        sel_kts = list(range(0, 4 * qt + 4))
        win_kts = list(range(max(0, 4 * qt - 4), 4 * qt + 4))
        n_den_total = 4 * (ntc + len(sel_kts) + len(win_kts))
        den_i = [0]

        def den_mm(r, e_t, den_i=den_i, n_den_total=n_den_total):
            i = den_i[0]
            den_i[0] += 1
            mm(S, p_den, p_den[0:12, :], [(onesel(r), e_t[:, :])], reads=[cB, e_t], start=(i == 0), stop=(i == n_den_total - 1))

        for h in range(4):
            for i in range(ntc):
                delta = 2048 * i - 512 * qt
                pairs = [(kcmpT[:, 128 * i:128 * i + 128], q_sb[:, h, :])]
                if -2048 <= delta <= 0:
                    pairs.append((identB, cmask((delta + 2048) // 512)))
                ps = p_sc.get()
                mm(S, ps, ps[:, :], pairs, reads=[kcmpT, q_sb, cB])
                S.op("act", lambda hh, ps=ps, i=i: hh.activation(out=ec[i][:, :], in_=ps[:, :], func=AF.Exp, scale=scale), reads=[ps], writes=[ec[i]])
                mm(S, p_o[0], p_o[0][:, :], [(vcmp[:, i, :], ec[i][:, :])], reads=[vcmp, ec[i]], start=(i == 0), stop=(i == ntc - 1))
                den_mm(3 * h + 0, ec[i])
            S.op("act", lambda hh, h=h: hh.activation(out=o_sb[0][h][:, :], in_=p_o[0][:, :], func=AF.Copy), reads=[p_o[0]], writes=[o_sb[0][h]])
            for s4 in range(4):
                mm(S, p_u, p_u[:, 0:129], [(ec[i][:, s4 * 128:(s4 + 1) * 128], ovl1(i)) for i in range(ntc)], reads=[cB] + ec[0:ntc])
                S.op("dve", lambda hh: hh.tensor_scalar(out=rd[:, :], in0=p_u[:, 128:129], scalar1=1e-30, scalar2=None, op0=ALU.max), reads=[p_u], writes=[rd])
                S.op("dve", lambda hh: hh.reciprocal(out=rd[:, :], in_=rd[:, :]), reads=[rd], writes=[rd])
                if h == 0:
                    S.op("dve", lambda hh, s4=s4: hh.tensor_scalar(out=imp[:, s4, :], in0=p_u[:, 0:128], scalar1=rd[:, 0:1], scalar2=None, op0=ALU.mult),
                         reads=[p_u, rd], writes=[imp])
                else:
                    S.op("dve", lambda hh, s4=s4: hh.scalar_tensor_tensor(out=imp[:, s4, :], in0=p_u[:, 0:128], scalar=rd[:, 0:1], in1=imp[:, s4, :],
                                                                           op0=ALU.mult, op1=ALU.add), reads=[p_u, rd, imp], writes=[imp])
        for s4 in range(4):
            g4 = 4 * qt + s4
            pb0 = CF["pbig"] + 128 - 2 * g4
            S.op("dve", lambda hh, s4=s4, pb0=pb0: hh.tensor_tensor(out=sc2[:, :], in0=imp[:, s4, :], in1=cF[:, pb0:pb0 + 128], op=ALU.add),
                 reads=[imp, cF], writes=[sc2])
            S.op("dve", lambda hh: hh.tensor_scalar(out=sc2[:, 0:1], in0=sc2[:, 0:1], scalar1=1e4, scalar2=None, op0=ALU.add), reads=[sc2], writes=[sc2])
            S.op("dve", lambda hh: hh.max(out=m8a[:, :], in_=sc2[:, :]), reads=[sc2], writes=[m8a])
            S.op("dve", lambda hh: hh.match_replace(out=sc3[:, :], in_to_replace=m8a[:, :], in_values=sc2[:, :], imm_value=-2e9), reads=[sc2, m8a], writes=[sc3])
            S.op("dve", lambda hh: hh.max(out=m8b[:, :], in_=sc3[:, :]), reads=[sc3], writes=[m8b])
            S.op("dve", lambda hh: hh.tensor_scalar(out=negm[:, :], in0=sc2[:, :], scalar1=m8b[:, 7:8], scalar2=NEG, op0=ALU.is_lt, op1=ALU.mult),
                 reads=[sc2, m8b], writes=[negm])
            S.op("pe", lambda hh: hh.transpose(out=p_u[:, 0:128], in_=negm[:, :], identity=identF), reads=[negm, cF], writes=[p_u])
            S.op("dve", lambda hh, s4=s4: hh.tensor_copy(out=negmT[:, s4 * 128:(s4 + 1) * 128], in_=p_u[:, 0:128]), reads=[p_u], writes=[negmT])
        items = []
        for h in range(4):
            for bi, kts in ((1, sel_kts), (2, win_kts)):
                for ii, kt in enumerate(kts):
                    if bi == 1:
                        pairs = [(srcT["ks"][:, 128 * kt:128 * kt + 128], q_sb[:, h, :]), (ebig(kt), negmT[:, :])]
                        if kt >= 4 * qt:
                            pairs.append((identB, wmask(4 + kt - 4 * qt)))
                        rds = [srcT["ks"], q_sb, cB, negmT]
                        vt = vs
                    else:
                        pairs = [(srcT["kw"][:, 128 * kt:128 * kt + 128], q_sb[:, h, :]), (identB, wmask(kt - (4 * qt - 4)))]
                        rds = [srcT["kw"], q_sb, cB]
                        vt = vw
                    items.append((h, bi, kt, pairs, rds, vt, ii == 0, ii == len(kts) - 1))

        def issue_scores(it):
            ps = p_sc.get()
            mm(S, ps, ps[:, :], it[3], reads=it[4])
            e_t = er.get()
            S.op("act", lambda hh, ps=ps, e_t=e_t: hh.activation(out=e_t[:, :], in_=ps[:, :], func=AF.Exp, scale=scale), reads=[ps], writes=[e_t])
            return e_t

        pend = issue_scores(items[0])
        for idx, it in enumerate(items):
            h, bi, kt, _, _, vt, first, last = it
            e_t = pend
            if idx + 1 < len(items):
                pend = issue_scores(items[idx + 1])
            mm(S, p_o[bi], p_o[bi][:, :], [(vt[:, kt, :], e_t[:, :])], reads=[vt, e_t], start=first, stop=last)
            den_mm(3 * h + bi, e_t)
            if last:
                S.op("act", lambda hh, h=h, bi=bi: hh.activation(out=o_sb[bi][h][:, :], in_=p_o[bi][:, :], func=AF.Copy), reads=[p_o[bi]], writes=[o_sb[bi][h]])
        assert den_i[0] == n_den_total
        S.op("dve", lambda hh: hh.tensor_scalar(out=den_sb[:, :], in0=p_den[0:12, :], scalar1=1e-30, scalar2=None, op0=ALU.max), reads=[p_den], writes=[den_sb])
        S.op("dve", lambda hh: hh.reciprocal(out=den_sb[:, :], in_=den_sb[:, :]), reads=[den_sb], writes=[den_sb])
        S.op("dve", lambda hh: hh.tensor_tensor(out=fct[:, :], in0=den_sb[:, :], in1=gn12[:, :], op=ALU.mult), reads=[den_sb, gn12], writes=[fct])
        for h in range(4):
            for b in range(3):
                mm(S, p_x, p_x[:, :], [(sel12(3 * h + b), fct[:, :])], reads=[cB, fct])
                if b == 0:
                    S.op("dve", lambda hh, h=h, b=b: hh.tensor_tensor(out=ya[:, :], in0=o_sb[b][h][:, :], in1=p_x[:, :], op=ALU.mult), reads=[o_sb[b][h], p_x], writes=[ya])
                else:
                    S.op("dve", lambda hh, h=h, b=b: hh.tensor_tensor(out=tmpm[:, :], in0=o_sb[b][h][:, :], in1=p_x[:, :], op=ALU.mult), reads=[o_sb[b][h], p_x], writes=[tmpm])
                    S.op("dve", lambda hh: hh.tensor_tensor(out=ya[:, :], in0=ya[:, :], in1=tmpm[:, :], op=ALU.add), reads=[ya, tmpm], writes=[ya])
            S.op("dve", lambda hh, h=h: hh.tensor_tensor(out=ya[:, :], in0=ya[:, :], in1=gm0[:, h, :], op=ALU.mult), reads=[ya, gm0], writes=[ya])
            S.dma(mA_d.t[h * 128:(h + 1) * 128, t0:t0 + TQ], ya[:, :], reads=[ya], writes=[mA_d])
    S.end_stage()

    S.begin_stage()
    ar_ = Rot([S.sbuf("ca%d" % i, [128, 4, TQ], F32) for i in range(2)])
    br_ = Rot([S.sbuf("cbb%d" % i, [128, 4, TQ], F32) for i in range(2)])
    cr_ = Rot([S.sbuf("ccc%d" % i, [128, 4, TQ], F32) for i in range(2)])
    or_ = Rot([S.sbuf("co%d" % i, [128, 4, TQ], BF16) for i in range(2)])
    for qt in range(nQT):
        t0 = qt * TQ
        a_, b_, c_, o_ = ar_.get(), br_.get(), cr_.get(), or_.get()
        for tl, src in ((a_, mA_d), (b_, mB_d), (c_, mC_d)):
            S.dma(tl[:, :, :], src.t.rearrange("(c p) t -> p c t", p=128)[:, :, t0:t0 + TQ], reads=[src], writes=[tl])
        S.op("dve", lambda hh, a_=a_, b_=b_: hh.tensor_tensor(out=a_[:, :, :], in0=a_[:, :, :], in1=b_[:, :, :], op=ALU.add), reads=[a_, b_], writes=[a_])
        S.op("dve", lambda hh, a_=a_, c_=c_, o_=o_: hh.tensor_tensor(out=o_[:, :, :], in0=a_[:, :, :], in1=c_[:, :, :], op=ALU.add), reads=[a_, c_], writes=[o_])
        mk = mT[t0 // 1024]
        S.dma(mk.t.rearrange("(c p) t -> p c t", p=128)[:, :, (t0 % 1024):(t0 % 1024) + TQ], o_[:, :, :], reads=[o_], writes=[mk])
    S.end_stage()


OFF = {"q": 0, "kc": 2048, "vc": 2560, "ks": 3072, "vs": 3584, "kw": 4096, "vw": 4608, "gn": 5120, "cb": 5168, "cc": 7216,
       "u": 9264, "z": 11312, "xbc": 13360, "dt": 16432, "gm": 16464}


def wcat_cols(g):
    cols = []
    for h in range(4):
        cols.append(np.arange(OFF["q"] + 512 * g + 128 * h, OFF["q"] + 512 * g + 128 * h + 128))
    for nm in ("kc", "ks", "kw", "vc"):
        cols.append(np.arange(OFF[nm] + 128 * g, OFF[nm] + 128 * g + 128))
    for nm in ("cb", "cc", "u", "z"):
        for i in range(4):
            cols.append(np.arange(OFF[nm] + 512 * g + 128 * i, OFF[nm] + 512 * g + 128 * i + 128))
    for i in range(4):
        cols.append(np.arange(OFF["xbc"] + 512 * g + 128 * i, OFF["xbc"] + 512 * g + 128 * i + 128))
    cols.append(np.arange(OFF["xbc"] + 2048 + 128 * g, OFF["xbc"] + 2048 + 128 * g + 128))
    cols.append(np.arange(OFF["xbc"] + 2560 + 128 * g, OFF["xbc"] + 2560 + 128 * g + 128))
    for i in range(12):
        j, c = i // 4, i % 4
        s0 = OFF["gm"] + j * 2048 + 512 * g + 128 * c
        cols.append(np.arange(s0, s0 + 128))
    cols.append(np.arange(OFF["vs"] + 128 * g, OFF["vs"] + 128 * g + 128))
    cols.append(np.arange(OFF["vw"] + 128 * g, OFF["vw"] + 128 * g + 128))
    cols.append(np.arange(OFF["gn"] + 12 * g, OFF["gn"] + 12 * g + 12))
    cols.append(np.arange(OFF["dt"] + 8 * g, OFF["dt"] + 8 * g + 8))
    cols = np.concatenate(cols)
    assert cols.shape[0] == NWC
    return cols


def ssd_ch(g, c):
    p = np.arange(128)
    if c < 4:
        return 512 * g + 128 * c + p
    return (2048 if c == 4 else 2560) + 128 * g + p


def prep_p1_layer(inp, l, g):
    p = np.arange(128)
    prm = np.zeros((128, PRM["n"]), np.float32)
    for c in range(4):
        ch = 512 * g + 128 * c + p
        for j in range(3):
            prm[:, PRM["scw"] + c * 3 + j] = inp["sconv_w"][l, j, ch]
        prm[:, PRM["dsk"] + c] = inp["ssd_d"][l, 8 * g + 2 * c + (p >= 64)]
        prm[:, PRM["ng"] + c] = inp["ssd_norm_g"][l, ch]
    for c in range(6):
        ch = ssd_ch(g, c)
        for j in range(4):
            prm[:, PRM["sdw"] + c * 4 + j] = inp["ssd_conv_w"][l, j, ch]
        prm[:, PRM["sdb"] + c] = inp["ssd_conv_b"][l, ch]
    prm[:, PRM["kb1"]] = inp["phi_k_b1"][l]
    prm[:, PRM["kb2"]] = inp["phi_k_b2"][l]
    prm[:, PRM["vb1"]] = inp["phi_v_b1"][l]
    prm[0:8, PRM["dtb"]] = inp["ssd_dt_bias"][l, 8 * g:8 * g + 8]
    prm[0:8, PRM["alog"]] = inp["ssd_a_log"][l, 8 * g:8 * g + 8]
    return {
        "wcat": np.ascontiguousarray(inp["w_in"][l][:, wcat_cols(g)]),
        "prm": prm,
        "peT": np.ascontiguousarray(np.concatenate([inp["nsa_pe_k"][l].T, inp["nsa_pe_v"][l].T], axis=1)).astype(np.float32),
        "kw1": np.ascontiguousarray(inp["phi_k_w1"][l]), "vw1": np.ascontiguousarray(inp["phi_v_w1"][l]),
        "kw2": np.ascontiguousarray(inp["phi_k_w2"][l]), "vw2": np.ascontiguousarray(inp["phi_v_w2"][l]),
        "vb2": np.ascontiguousarray(inp["phi_v_b2"][l][None, :]),
    }


def emit_p0(nc, S, Tc, xT, gv_d, hb):
    S.begin_stage()
    cst = S.sbuf("cst", [128, 128], BF16)
    gv = S.sbuf("gvs", [128, 32], F32)
    x_sb = S.sbuf("x_sb", [128, KC, TQ], F32)
    a16 = S.sbuf("a16", [128, KC, TQ], BF16)
    h_bf = S.sbuf("h_bf", [128, KC, TQ], BF16)
    rstd = S.sbuf("rstd", [128, TQ], F32)
    psn = S.psum("psn", [128, TQ])
    S.op("dve", lambda h: h.memset(cst[:, :], 1.0), writes=[cst])
    S.op("dve", lambda h: h.memset(gv[:, 16:17], EPS), writes=[gv])
    S.dma(gv[:, 0:16], gv_d[:, :], reads=[gv_d], writes=[gv])
    for tt in range(Tc // TQ):
        t0 = tt * TQ
        S.dma(x_sb[:, :, :], xT.t.rearrange("(kc p) t -> p kc t", p=128)[:, :, t0:t0 + TQ], reads=[xT], writes=[x_sb])
        rmsnorm_fm(S, x_sb, gv, 0, TQ, a16, psn, cst[:, 0:128], cst, gv[:, 16:17], rstd, [h_bf])
        for hh in range(2):
            hk = hb[t0 // 256 + hh]
            S.dma(hk.t.rearrange("(kc p) t -> p kc t", p=128), h_bf[:, :, hh * 256:(hh + 1) * 256], reads=[h_bf], writes=[hk])
    S.end_stage()


NCORES = 8
B_, T_, TC_ = 2, 8192, 2048
DEPTH_ = 4
GROUPS = [[0, 1, 2, 3], [4, 5, 6, 7]]
P1_KEYS = ("wcat", "prm", "peT", "kw1", "vw1", "kw2", "vw2", "vb2")
P1_SHAPES = {"wcat": [D, NWC], "prm": [128, PRM["n"]], "peT": [128, 64], "kw1": [4096, 128], "vw1": [4096, 128],
             "kw2": [128, 128], "vw2": [128, 128], "vb2": [1, 128]}
P2_SHAPES = {"w_o": [D, D], "w_up": [D, 4 * D], "w_dn": [4 * D, D], "w_pg": [D, D], "w_pl": [256, D], "gv": [128, 48], "pT": [256, TC_]}


def build_fused(T=T_, Tc=TC_, depth=DEPTH_):
    nc = bass.Bass("TRN2", target_bir_lowering=False)
    L = cb_layout(T)
    with ExitStack() as st:
        S = Sched(nc, st)
        ext = lambda n, s, d, k="ExternalInput": S.dram(n, s, d, kind=k)
        xT = ext("xT", [D, Tc], F32)
        gv0 = ext("gv0", [128, 16], F32)
        pos = ext("pos", [1, T], I32)
        cF_d = ext("cF", [128, CF["n"]], F32)
        cB_d = ext("cB", [128, L["n"]], BF16)
        oh_d = ext("oh", [128, 4], F32)
        hf = ext("hf", [D, Tc], F32, "ExternalOutput")
        lay = []
        for l in range(depth):
            dct = {k: ext("%s_%d" % (k, l), P1_SHAPES[k], F32) for k in P1_KEYS}
            dct.update({k: ext("%s_%d" % (k, l), ([256, Tc] if k == "pT" else P2_SHAPES[k]), F32) for k in P2_SHAPES})
            lay.append(dct)
        xres = S.dram("xres", [D, Tc], F32)
        hb_loc = [S.dram("hb_loc%d" % k, [D, 256], BF16, kind=None) for k in range(Tc // 256)]
        hT_all = [S.dram("hT_all%d" % k, [4 * D, 256], BF16, kind=None) for k in range(Tc // 256)]
        mT_loc = [S.dram("mT_loc%d" % k, [512, 1024], BF16, kind=None) for k in range(T // 1024)]
        mT_all = [S.dram("mT_all%d" % k, [D, 1024], BF16, kind=None) for k in range(T // 1024)]
        scr = make_p1_scratch(S, T)
        nq = Tc // TQ

        def hT_fn(tt):
            tq, c0 = tt // nq, (tt % nq) * TQ
            outl = []
            for hh in range(2):
                hk = hT_all[c0 // 256 + hh]
                outl.append((hh * 256, (hh + 1) * 256, hk.t[tq * D:(tq + 1) * D, :].rearrange("(kc p) t -> p kc t", p=128), hk))
            return outl

        emit_p0(nc, S, Tc, xT, gv0, hb_loc)
        for l in range(depth):
            d_ = lay[l]
            for k in range(len(hb_loc)):
                S.collective(hb_loc[k], hT_all[k], GROUPS)
            emit_p1(nc, S, T, {"hT_fn": hT_fn, "wcat": d_["wcat"], "pos": pos, "cF": cF_d, "cB": cB_d, "prm": d_["prm"], "peT": d_["peT"],
                               "w1": [d_["kw1"], d_["vw1"]], "w2": [d_["kw2"], d_["vw2"]], "vb2": d_["vb2"], "mT": mT_loc}, scr)
            for k in range(len(mT_loc)):
                S.collective(mT_loc[k], mT_all[k], GROUPS)
            emit_p2(nc, S, Tc, {"x_in": xT if l == 0 else xres, "x_out": xres, "mT_all": mT_all, "oh": oh_d, "pT": d_["pT"],
                                "w_o": d_["w_o"], "w_up": d_["w_up"], "w_dn": d_["w_dn"], "w_pg": d_["w_pg"], "w_pl": d_["w_pl"], "gv": d_["gv"],
                                "hb": hb_loc, "hf": hf if l == depth - 1 else None})
        with nc.Block() as block:
            S.finish(block)
    return nc


_PROG = {}


def _gcol(gvec):
    return np.ascontiguousarray(np.asarray(gvec, np.float32).reshape(16, 128).T)


def kernel(**inputs):
    return _run(inputs, T_, TC_, DEPTH_)


def _run(inputs, T_, TC_, DEPTH_):
    inp = {k: np.asarray(v) for k, v in inputs.items()}
    cores = list(range(NCORES))
    key = (T_, TC_, DEPTH_)
    if key not in _PROG:
        _PROG[key] = (build_fused(T_, TC_, DEPTH_), make_consts(T_))
    prog, (cF, cB) = _PROG[key]
    x = inp["x"].astype(np.float32, copy=False)
    p1 = [[prep_p1_layer(inp, l, g) for g in range(4)] for l in range(DEPTH_)]
    p2 = []
    for l in range(DEPTH_):
        g_next = inp["g_mix"][l + 1] if l + 1 < DEPTH_ else inp["g_final"]
        p2.append({"w_o": np.ascontiguousarray(inp["w_o"][l]), "w_up": np.ascontiguousarray(inp["w_up"][l]),
                   "w_dn": np.ascontiguousarray(inp["w_down"][l]), "w_pg": np.ascontiguousarray(inp["w_ple_gate"][l]),
                   "w_pl": np.ascontiguousarray(inp["w_ple"][l]),
                   "gv": np.ascontiguousarray(np.concatenate([_gcol(inp["g_mlp"][l]), _gcol(inp["g_ple"][l]), _gcol(g_next)], axis=1))})
    g0 = _gcol(inp["g_mix"][0])
    maps = []
    for c in cores:
        b, g = c // 4, c % 4
        sl = slice(g * TC_, (g + 1) * TC_)
        oh = np.zeros((128, 4), np.float32)
        oh[:, g] = 1.0
        m = {"xT": np.ascontiguousarray(x[b, sl, :].T), "gv0": g0, "pos": np.ascontiguousarray(inp["positions"][b:b + 1, :]).astype(np.int32),
             "cF": cF, "cB": cB, "oh": oh}
        m["pos"] = np.ascontiguousarray(m["pos"][:, :T_])
        for l in range(DEPTH_):
            for k in P1_KEYS:
                m["%s_%d" % (k, l)] = p1[l][g][k]
            for k in ("w_o", "w_up", "w_dn", "w_pg", "w_pl", "gv"):
                m["%s_%d" % (k, l)] = p2[l][k]
            m["pT_%d" % l] = np.ascontiguousarray(inp["p"][l, b, sl, :].T.astype(np.float32))
        maps.append(m)
    res = run_bass_kernel_spmd(prog, maps, core_ids=cores)
    out = np.empty((B_, T_, D), np.float32)
    for c in cores:
        b, g = c // 4, c % 4
        out[b, g * TC_:(g + 1) * TC_, :] = np.asarray(res.results[c]["hf"]).T
    return out
```

```python
from contextlib import ExitStack
import numpy as np
import ml_dtypes
import concourse.bass as bass
import concourse.mybir as mybir
from concourse.bass_utils import run_bass_kernel_spmd

F32 = mybir.dt.float32
BF16 = mybir.dt.bfloat16
I32 = mybir.dt.int32
AF = mybir.ActivationFunctionType
ALU = mybir.AluOpType
NPBF = ml_dtypes.bfloat16

D = 2048
KC = 16
TQ = 512
NEG = -30000.0
EPS = 1e-6


class TT:
    __slots__ = ("t", "writers", "readers", "name")

    def __init__(self, t, name=""):
        self.t = t
        self.writers = {}
        self.readers = {}
        self.name = name

    def __getitem__(self, k):
        return self.t[k]


class _Eng:
    def __init__(self, name, sem):
        self.name = name
        self.sem = sem
        self.cnt = 0
        self.cmds = []
        self.waited = {}


class Sched:
    ENGS = ("pe", "act", "dve", "pool", "sp")

    def __init__(self, nc, stack, n_dma_sems=12):
        self.nc = nc
        self.stack = stack
        self.cur = stack
        self.e = {}
        self.semobj = {}
        for n in self.ENGS:
            sem = stack.enter_context(nc.semaphore("s_" + n))
            self.e[n] = _Eng(n, sem)
        self.dsem = {}
        for q in ("sp", "pool"):
            sems = [stack.enter_context(nc.semaphore("d_%s%d" % (q, i))) for i in range(n_dma_sems)]
            self.dsem[q] = {"sems": sems, "vals": [0] * n_dma_sems, "next": 0}

    def _uniq(self, name):
        self._n = getattr(self, "_n", 0) + 1
        return "%s_%d" % (name, self._n)

    def sbuf(self, name, shape, dtype):
        return TT(self.cur.enter_context(self.nc.sbuf_tensor(self._uniq(name), list(shape), dtype)), name)

    def psum(self, name, shape, dtype=F32):
        return TT(self.cur.enter_context(self.nc.psum_tensor(self._uniq(name), list(shape), dtype)), name)

    def begin_stage(self):
        self._prev = getattr(self, "_prev", [])
        self._prev.append(self.cur)
        self.cur = ExitStack()

    def push_scope(self):
        self._prev = getattr(self, "_prev", [])
        self._prev.append(self.cur)
        self.cur = ExitStack()

    def pop_scope(self):
        self.cur.close()
        self.cur = self._prev.pop()

    def collective(self, in_tt, out_tt, groups):
        if not hasattr(self, "ccsem"):
            self.ccsem = self.stack.enter_context(self.nc.semaphore("s_cc"))
            self.cccnt = 0
        E = self.e["pool"]
        need = self._collect("pool", [in_tt], [out_tt])
        self.cccnt += 1
        val = self.cccnt
        sem = self.ccsem

        def cmd(h, need=need, sem=sem, i=in_tt, o=out_tt, groups=groups):
            for s_, v in need:
                h.wait_ge(s_, v)
            h.collective_compute("AllGather", mybir.AluOpType.bypass, replica_groups=groups,
                                 ins=[i.t.opt()], outs=[o.t.opt()]).then_inc(sem)
        E.cmds.append(cmd)
        k = self._key(sem)
        out_tt.writers = {k: val}
        out_tt.readers = {}
        if val > in_tt.readers.get(k, 0):
            in_tt.readers[k] = val

    def end_stage(self):
        self.barrier()
        with self.nc.Block() as block:
            self._emit(block)
        self.cur.close()
        self.cur = self._prev.pop()

    def dram(self, name, shape, dtype, kind="Internal"):
        if kind is None:
            return TT(self.nc.dram_tensor(name, list(shape), dtype).ap(), name)
        return TT(self.nc.dram_tensor(name, list(shape), dtype, kind=kind).ap(), name)

    def view(self, tt, name=""):
        return TT(tt.t, name or tt.name)

    def _key(self, sem):
        k = id(sem)
        self.semobj[k] = sem
        return k

    def _collect(self, eng, reads, writes):
        own = self._key(self.e[eng].sem)
        waits = {}
        for t in reads:
            for k, v in t.writers.items():
                if v > waits.get(k, 0):
                    waits[k] = v
        for t in writes:
            for src in (t.writers, t.readers):
                for k, v in src.items():
                    if k == own:
                        continue
                    if v > waits.get(k, 0):
                        waits[k] = v
        E = self.e[eng]
        need = []
        for k, v in waits.items():
            if v > E.waited.get(k, 0):
                E.waited[k] = v
                need.append((self.semobj[k], v))
        return need

    def op(self, eng, fn, reads=(), writes=()):
        E = self.e[eng]
        need = self._collect(eng, reads, writes)
        E.cnt += 1
        idx = E.cnt
        sem = E.sem

        def cmd(h, need=need, fn=fn, sem=sem):
            for s, v in need:
                h.wait_ge(s, v)
            fn(h).then_inc(sem, 1)
        E.cmds.append(cmd)
        k = self._key(sem)
        for t in writes:
            t.writers = {k: idx}
            t.readers = {}
        for t in reads:
            if t in writes:
                continue
            if idx > t.readers.get(k, 0):
                t.readers[k] = idx

    def dma(self, out_ap, in_ap, reads=(), writes=(), q="sp", **kw):
        E = self.e[q]
        Dq = self.dsem[q]
        i = Dq["next"]
        Dq["next"] = (i + 1) % len(Dq["sems"])
        sem = Dq["sems"][i]
        k = self._key(sem)
        need = self._collect(q, reads, writes)
        prev = Dq["vals"][i]
        if prev > E.waited.get(k, 0):
            E.waited[k] = prev
            need.append((sem, prev))
        val = prev + 16
        Dq["vals"][i] = val

        def cmd(h, need=need, sem=sem, out_ap=out_ap, in_ap=in_ap, kw=kw):
            for s, v in need:
                h.wait_ge(s, v)
            h.dma_start(out=out_ap, in_=in_ap, **kw).then_inc(sem, 16)
        E.cmds.append(cmd)
        for t in writes:
            t.writers = {k: val}
            t.readers = {}
        for t in reads:
            if val > t.readers.get(k, 0):
                t.readers[k] = val

    def _all_final(self):
        fin = []
        for n in self.ENGS:
            E = self.e[n]
            if E.cnt:
                fin.append((E.sem, E.cnt))
        for q, Dq in self.dsem.items():
            for s, v in zip(Dq["sems"], Dq["vals"]):
                if v:
                    fin.append((s, v))
        if getattr(self, "cccnt", 0):
            fin.append((self.ccsem, self.cccnt))
        return fin

    def barrier(self):
        fin = self._all_final()
        for n in self.ENGS:
            E = self.e[n]
            need = []
            for s, v in fin:
                k = self._key(s)
                if v > E.waited.get(k, 0):
                    E.waited[k] = v
                    need.append((s, v))

            def cmd(h, need=need):
                for s, v in need:
                    h.wait_ge(s, v)
            E.cmds.append(cmd)

    def finish(self, block):
        self.barrier()
        self._emit(block)

    def _emit(self, block):
        def run(cmds):
            def f(h):
                for c in cmds:
                    c(h)
            return f
        block.tensor(run(self.e["pe"].cmds))
        block.scalar(run(self.e["act"].cmds))
        block.vector(run(self.e["dve"].cmds))
        block.gpsimd(run(self.e["pool"].cmds))
        block.sync(run(self.e["sp"].cmds))
        for n in self.ENGS:
            self.e[n].cmds = []


def mm(S, out_tt, out_ap, pairs, reads, start=True, stop=True):
    def fn(h, pairs=pairs, out_ap=out_ap, start=start, stop=stop):
        n = len(pairs)
        ins = None
        for i, (l, r) in enumerate(pairs):
            ins = h.matmul(out_ap, lhsT=l, rhs=r, start=(start and i == 0), stop=(stop and i == n - 1))
        return ins
    S.op("pe", fn, reads=reads, writes=[out_tt])


class Rot:
    def __init__(self, tiles):
        self.tiles = tiles
        self.i = 0

    def get(self):
        t = self.tiles[self.i]
        self.i = (self.i + 1) % len(self.tiles)
        return t


class WLoader:
    def __init__(self, S, nbuf=3, kc=KC, ncol=128, name="w", cast_engs=("pool", "dve")):
        self.S = S
        self.cast_engs = cast_engs
        self.st = Rot([S.sbuf("%s_st%d" % (name, i), [128, kc, ncol], F32) for i in range(nbuf)])
        self.bf = Rot([S.sbuf("%s_bf%d" % (name, i), [128, kc, ncol], BF16) for i in range(nbuf)])
        self.cast_i = 0

    def load(self, dram_tt, dram_ap, kc=KC, ncol=128):
        S = self.S
        st = self.st.get()
        bf = self.bf.get()
        S.dma(st[:, 0:kc, 0:ncol], dram_ap, reads=[dram_tt], writes=[st])
        eng = self.cast_engs[self.cast_i % len(self.cast_engs)]
        self.cast_i += 1
        S.op(eng, lambda h, st=st, bf=bf: h.tensor_copy(out=bf[:, 0:kc, 0:ncol], in_=st[:, 0:kc, 0:ncol]),
             reads=[st], writes=[bf])
        return bf


def rmsnorm_fm(S, x_sb, g_tt, g_off, ncols, sq_bf, ps, ones_bf, cst, eps_col, rstd, outs, nfeat=D, kc=KC):
    for c in range(kc):
        S.op("act", lambda h, c=c: h.activation(out=sq_bf[:, c, 0:ncols], in_=x_sb[:, c, 0:ncols], func=AF.Square),
             reads=[x_sb], writes=[sq_bf])
    mm(S, ps, ps[:, 0:ncols], [(ones_bf[:, :], sq_bf[:, c, 0:ncols]) for c in range(kc)], reads=[sq_bf, cst])
    S.op("act", lambda h: h.activation(out=rstd[:, 0:ncols], in_=ps[:, 0:ncols], func=AF.Ln, bias=eps_col, scale=1.0 / nfeat),
         reads=[ps, cst], writes=[rstd])
    S.op("act", lambda h: h.activation(out=rstd[:, 0:ncols], in_=rstd[:, 0:ncols], func=AF.Exp, scale=-0.5),
         reads=[rstd], writes=[rstd])
    for o in outs:
        for c in range(kc):
            S.op("dve", lambda h, c=c, o=o: h.scalar_tensor_tensor(
                out=o[:, c, 0:ncols], in0=x_sb[:, c, 0:ncols], scalar=g_tt[:, g_off + c:g_off + c + 1],
                in1=rstd[:, 0:ncols], op0=ALU.mult, op1=ALU.mult), reads=[x_sb, g_tt, rstd], writes=[o])


def emit_p2(nc, S, Tc, io):
    if True:
        if True:
            pass
        xT, xo, mT_all, oh_d, pT = io["x_in"], io["x_out"], io["mT_all"], io["oh"], io["pT"]
        w_o, w_up, w_dn, w_pg, w_pl, gv_d = io["w_o"], io["w_up"], io["w_dn"], io["w_pg"], io["w_pl"], io["gv"]
        hb, hf = io["hb"], io["hf"]
        S.begin_stage()
        oh = S.sbuf("oh_sb", [128, 4], F32)
        S.dma(oh[:, :], oh_d[:, :], reads=[oh_d], writes=[oh])
        cand = Rot([S.sbuf("cand%d" % i, [128, KC, TQ], BF16) for i in range(1)])
        cst = S.sbuf("cst", [128, 256], BF16)
        gv = S.sbuf("gvs", [128, 64], F32)
        x_sb = S.sbuf("x_sb", [128, KC, TQ], F32)
        a16 = S.sbuf("a16", [128, KC, TQ], BF16)
        h_bf = S.sbuf("h_bf", [128, KC, TQ], BF16)
        hid = S.sbuf("hid", [128, 4 * KC, TQ], BF16)
        p32 = S.sbuf("p32", [128, 2, TQ], F32)
        p16 = S.sbuf("p16", [128, 2, TQ], BF16)
        rstd = S.sbuf("rstd", [128, TQ], F32)
        tmpr = Rot([S.sbuf("tmp%d" % i, [128, TQ], F32) for i in range(2)])
        gate = S.sbuf("gate", [128, TQ], F32)
        WL = WLoader(S, nbuf=3)
        psr = Rot([S.psum("ps%d" % i, [128, TQ]) for i in range(4)])
        psn = S.psum("psn", [128, TQ])

        S.op("dve", lambda h: h.memset(cst[:, 0:128], 1.0), writes=[cst])
        S.op("dve", lambda h: h.memset(gv[:, 48:49], EPS), writes=[gv])
        S.dma(gv[:, 0:48], gv_d[:, :], reads=[gv_d], writes=[gv])
        ones_bf = cst[:, 0:128]
        eps_col = gv[:, 48:49]

        def fm(ap, t0):
            return ap.rearrange("(kc p) t -> p kc t", p=128)[:, :, t0:t0 + TQ]

        def wblk(w, r0, c0):
            return w[(r0 // D) * 16 + c0 // 128]

        for tt in range(Tc // TQ):
            t0 = tt * TQ
            S.dma(x_sb[:, :, :], fm(xT.t, t0), reads=[xT], writes=[x_sb])
            for j in range(4):
                cd = cand.get()
                gtok = j * Tc + t0
                mk = mT_all[gtok // 1024]
                S.dma(cd[:, :, :], mk.t.rearrange("(kc p) t -> p kc t", p=128)[:, :, (gtok % 1024):(gtok % 1024) + TQ], reads=[mk], writes=[cd])
                if j == 0:
                    S.op("dve", lambda h, cd=cd: h.tensor_scalar(out=a16[:, :, :], in0=cd[:, :, :], scalar1=oh[:, 0:1], scalar2=None, op0=ALU.mult),
                         reads=[cd, oh], writes=[a16])
                else:
                    S.op("dve", lambda h, cd=cd, j=j: h.scalar_tensor_tensor(out=a16[:, :, :], in0=cd[:, :, :], scalar=oh[:, j:j + 1], in1=a16[:, :, :],
                                                                           op0=ALU.mult, op1=ALU.add), reads=[cd, oh, a16], writes=[a16])
            S.dma(p32[:, :, :], fm(pT.t, t0), reads=[pT], writes=[p32])
            S.op("dve", lambda h: h.tensor_copy(out=p16[:, :, :], in_=p32[:, :, :]), reads=[p32], writes=[p16])
            for dc in range(KC):
                wb = WL.load(w_o, wblk(w_o.t, 0, dc * 128))
                ps = psr.get()
                mm(S, ps, ps[:, :], [(wb[:, k, :], a16[:, k, :]) for k in range(KC)], reads=[wb, a16])
                S.op("dve", lambda h, dc=dc, ps=ps: h.tensor_tensor(out=x_sb[:, dc, :], in0=ps[:, :], in1=x_sb[:, dc, :], op=ALU.add),
                     reads=[ps, x_sb], writes=[x_sb])
            rmsnorm_fm(S, x_sb, gv, 0, TQ, a16, psn, ones_bf, cst, eps_col, rstd, [h_bf])
            for fc in range(4 * KC):
                wb = WL.load(w_up, wblk(w_up.t, 0, fc * 128))
                ps = psr.get()
                mm(S, ps, ps[:, :], [(wb[:, k, :], h_bf[:, k, :]) for k in range(KC)], reads=[wb, h_bf])
                tmp = tmpr.get()
                S.op("act", lambda h, ps=ps, tmp=tmp: h.activation(out=tmp[:, :], in_=ps[:, :], func=AF.Relu), reads=[ps], writes=[tmp])
                S.op("dve", lambda h, fc=fc, tmp=tmp: h.tensor_tensor(out=hid[:, fc, :], in0=tmp[:, :], in1=tmp[:, :], op=ALU.mult),
                     reads=[tmp], writes=[hid])
            for dc in range(KC):
                ps = psr.get()
                for q4 in range(4):
                    wb = WL.load(w_dn, wblk(w_dn.t, q4 * D, dc * 128))
                    mm(S, ps, ps[:, :], [(wb[:, k, :], hid[:, q4 * KC + k, :]) for k in range(KC)], reads=[wb, hid],
                       start=(q4 == 0), stop=(q4 == 3))
                S.op("dve", lambda h, dc=dc, ps=ps: h.tensor_tensor(out=x_sb[:, dc, :], in0=ps[:, :], in1=x_sb[:, dc, :], op=ALU.add),
                     reads=[ps, x_sb], writes=[x_sb])
            rmsnorm_fm(S, x_sb, gv, 16, TQ, a16, psn, ones_bf, cst, eps_col, rstd, [h_bf])
            for dc in range(KC):
                wb = WL.load(w_pg, wblk(w_pg.t, 0, dc * 128))
                ps = psr.get()
                mm(S, ps, ps[:, :], [(wb[:, k, :], h_bf[:, k, :]) for k in range(KC)], reads=[wb, h_bf])
                S.op("act", lambda h, ps=ps: h.activation(out=gate[:, :], in_=ps[:, :], func=AF.Sigmoid), reads=[ps], writes=[gate])
                wb2 = WL.load(w_pl, w_pl.t[dc], kc=2)
                ps2 = psr.get()
                mm(S, ps2, ps2[:, :], [(wb2[:, k, :], p16[:, k, :]) for k in range(2)], reads=[wb2, p16])
                tmp = tmpr.get()
                S.op("dve", lambda h, ps2=ps2, tmp=tmp: h.tensor_tensor(out=tmp[:, :], in0=ps2[:, :], in1=gate[:, :], op=ALU.mult),
                     reads=[ps2, gate], writes=[tmp])
                S.op("dve", lambda h, dc=dc, tmp=tmp: h.tensor_tensor(out=x_sb[:, dc, :], in0=tmp[:, :], in1=x_sb[:, dc, :], op=ALU.add),
                     reads=[tmp, x_sb], writes=[x_sb])
            S.dma(fm(xo.t, t0), x_sb[:, :, :], reads=[x_sb], writes=[xo])
            rmsnorm_fm(S, x_sb, gv, 32, TQ, a16, psn, ones_bf, cst, eps_col, rstd, [h_bf])
            for hh in range(2):
                hk = hb[t0 // 256 + hh]
                S.dma(hk.t.rearrange("(kc p) t -> p kc t", p=128), h_bf[:, :, hh * 256:(hh + 1) * 256], reads=[h_bf], writes=[hk])
            if hf is not None:
                for c in range(KC):
                    S.op("dve", lambda h, c=c: h.scalar_tensor_tensor(
                        out=x_sb[:, c, :], in0=x_sb[:, c, :], scalar=gv[:, 32 + c:33 + c], in1=rstd[:, :],
                        op0=ALU.mult, op1=ALU.mult), reads=[x_sb, gv, rstd], writes=[x_sb])
                S.dma(fm(hf.t, t0), x_sb[:, :, :], reads=[x_sb], writes=[hf])
        S.end_stage()


CF = {"ident": 0, "U": 128, "ssdneg": 256, "psw": 384, "pbig": 512, "invf": 768, "sgn": 769, "eps": 770, "one": 771,
      "npi": 772, "n": 776}
TWO_PI = 2.0 * np.pi
PI_IN = 3.1415925


def cb_layout(T):
    o = {}
    c = 0
    for nm, w in (("ident", 128), ("ones", 128), ("cmask", 5 * TQ), ("wmask", 8 * TQ), ("ebig", T), ("ovl1", 4 * 129),
                  ("onesel12", 144), ("sel12", 12 * 128)):
        o[nm] = c
        c += w
    o["n"] = c
    return o


def make_consts(T):
    p = np.arange(128)
    cF = np.zeros((128, CF["n"]), np.float32)
    cF[:, 0:128] = np.eye(128)
    cF[:, 128:256] = (p[:, None] <= p[None, :])
    cF[:, 256:384] = np.where(p[None, :] >= p[:, None], 0.0, NEG)
    cF[:, 384:512] = (p[:, None] == (p[None, :] + 64) % 128)
    xx = np.arange(256)[None, :]
    lo = (p[:, None] < 64)
    pb = np.zeros((128, 256), np.float32)
    pb[:, :] = np.where(xx > 129, -1e9, 0.0)
    pb[:, 127] = np.where(p < 64, 1e4, 0.0)
    pb[:, 128] = 1e4
    pb[:, 129] = np.where(p < 64, -1e9, 1e4)
    cF[:, 512:768] = pb
    invf = (1.0 / (np.float32(10000.0) ** (np.arange(0, 128, 2, dtype=np.float32) / np.float32(128)))).astype(np.float32)
    cF[:, 768] = invf[p % 64]
    cF[:, 769] = np.where(p < 64, -1.0, 1.0)
    cF[:, 770] = EPS
    cF[:, 771] = 1.0
    cF[:, 772] = -np.pi
    L = cb_layout(T)
    cB = np.zeros((128, L["n"]), np.float32)
    cB[:, L["ident"]:L["ident"] + 128] = np.eye(128)
    cB[:, L["ones"]:L["ones"] + 128] = 1.0
    x = np.arange(TQ)[None, :]
    for di in range(5):
        delta = -2048 + 512 * di
        cB[:, L["cmask"] + di * TQ:L["cmask"] + (di + 1) * TQ] = np.where(16 * p[:, None] + 31 + delta <= x, 0.0, NEG)
    for j in range(8):
        kk = 128 * j + p[:, None] - 512
        cB[:, L["wmask"] + j * TQ:L["wmask"] + (j + 1) * TQ] = np.where((x >= kk) & (x < kk + 512), 0.0, NEG)
    key = np.arange(T)[None, :]
    cB[:, L["ebig"]:L["ebig"] + T] = (key // 64 == p[:, None])
    for i in range(4):
        n = 128 * i + p[:, None]
        j = np.arange(128)[None, :]
        ov = (n < 4 * j + 4) & (n > 4 * j - 2)
        cB[:, L["ovl1"] + i * 129:L["ovl1"] + i * 129 + 128] = ov
        cB[:, L["ovl1"] + i * 129 + 128] = 1.0
    for r in range(12):
        cB[:, L["onesel12"] + r * 12 + r] = 1.0
        cB[r, L["sel12"] + r * 128:L["sel12"] + (r + 1) * 128] = 1.0
    return cF, cB.astype(NPBF)


FM = ["q0", "q1", "q2", "q3", "kc", "ks", "kw", "vc"] + ["cb%d" % i for i in range(4)] + ["cc%d" % i for i in range(4)] \
    + ["u%d" % i for i in range(4)] + ["z%d" % i for i in range(4)] + ["xb%d" % i for i in range(6)] \
    + ["gm%d" % i for i in range(12)]
NFM = len(FM)
COL_V = NFM * 128
COL_GN = COL_V + 256
COL_DT = COL_GN + 12
NWC = COL_DT + 8
PRM = {"scw": 0, "sdw": 12, "sdb": 36, "dsk": 42, "ng": 46, "kb1": 50, "kb2": 51, "vb1": 52, "dtb": 53, "alog": 54, "n": 56}


def make_p1_scratch(S, T):
    scr = {}
    scr["qT_d"] = S.dram("qT_d", [4, 128, T], BF16)
    scr["kT_d"] = {n: S.dram(n + "T_d", [128, T], BF16) for n in ("kc", "ks", "kw", "vc")}
    scr["vtok_d"] = S.dram("vtok_d", [T, 256], BF16)
    for nm, r in (("cb_d", 512), ("cu_d", 512), ("z_d", 512), ("xb_d", 768), ("gm_d", 1536), ("gn_d", 12), ("dt_d", 8),
                  ("mA_d", 512), ("mB_d", 512), ("mC_d", 512)):
        scr[nm] = S.dram(nm, [r, T], F32)
    return scr


def emit_p1(nc, S, T, io, scr):
    nQT = T // TQ
    NKT = T // 128
    NCMP = (T - 32) // 16 + 1
    L = cb_layout(T)
    scale = 128 ** -0.5
    if True:
        hT_fn = io["hT_fn"]
        wcat, pos, cF_d, cB_d, prm_d, peT_d = io["wcat"], io["pos"], io["cF"], io["cB"], io["prm"], io["peT"]
        w1_d, w2_d, vb2_d, mT = io["w1"], io["w2"], io["vb2"], io["mT"]
        qT_d, kT_d, vtok_d = scr["qT_d"], scr["kT_d"], scr["vtok_d"]
        cb_d, cu_d, z_d, xb_d, gm_d, gn_d, dt_d = scr["cb_d"], scr["cu_d"], scr["z_d"], scr["xb_d"], scr["gm_d"], scr["gn_d"], scr["dt_d"]
        mA_d, mB_d, mC_d = scr["mA_d"], scr["mB_d"], scr["mC_d"]
        S.push_scope()
        cF = S.sbuf("cF_sb", [128, CF["n"]], F32)
        cB = S.sbuf("cB_sb", [128, L["n"]], BF16)
        prm = S.sbuf("prm_sb", [128, PRM["n"]], F32)
        S.dma(cF[:, :], cF_d[:, :], reads=[cF_d], writes=[cF])
        S.dma(cB[:, :], cB_d[:, :], reads=[cB_d], writes=[cB])
        S.dma(prm[:, :], prm_d[:, :], reads=[prm_d], writes=[prm])
        identF = cF[:, 0:128]
        identB = cB[:, L["ident"]:L["ident"] + 128]
        onesB = cB[:, L["ones"]:L["ones"] + 128]
        eps_col = cF[:, 770:771]

        def fcol(name, i=0):
            c = CF[name] + i
            return cF[:, c:c + 1]

        def pcol(name, i=0, rows=128):
            c = PRM[name] + i
            return prm[0:rows, c:c + 1]

        S.begin_stage()
        h_sb = S.sbuf("h_sb", [128, KC, TQ], BF16)
        WL = WLoader(S, nbuf=3, cast_engs=("pool",))
        psr = Rot([S.psum("s1ps%d" % i, [128, TQ]) for i in range(4)])
        ps_sw = S.psum("s1sw", [128, TQ])
        f32r = Rot([S.sbuf("s1f%d" % i, [128, TQ], F32) for i in range(4)])
        b16r = Rot([S.sbuf("s1b%d" % i, [128, TQ], BF16) for i in range(3)])
        posi = S.sbuf("posi", [128, TQ], I32)
        ang = S.sbuf("ang", [128, TQ], F32)
        kf = S.sbuf("kf", [128, TQ], F32)
        ki = S.sbuf("ki", [128, TQ], I32)
        rr = S.sbuf("rr", [128, TQ], F32)
        rc = S.sbuf("rc", [128, TQ], F32)
        fx = S.sbuf("fx", [128, TQ], F32)
        cos_t = S.sbuf("cos_t", [128, TQ], F32)
        sin_t = S.sbuf("sin_t", [128, TQ], F32)
        xr = S.sbuf("xr", [128, TQ], F32)
        t1 = S.sbuf("t1", [128, TQ], F32)
        t2 = S.sbuf("t2", [128, TQ], F32)
        cc_sb = S.sbuf("cc_sb", [128, 4, TQ], F32)
        vt_sb = Rot([S.sbuf("vt%d" % i, [128, 256], BF16) for i in range(2)])

        def wrap_pi(r):
            S.op("dve", lambda h: h.tensor_scalar(out=fx[:, :], in0=r[:, :], scalar1=float(np.pi), scalar2=-TWO_PI,
                                                  op0=ALU.is_gt, op1=ALU.mult), reads=[r], writes=[fx])
            S.op("dve", lambda h: h.tensor_tensor(out=r[:, :], in0=r[:, :], in1=fx[:, :], op=ALU.add), reads=[r, fx], writes=[r])
            S.op("dve", lambda h: h.tensor_scalar(out=r[:, :], in0=r[:, :], scalar1=-PI_IN, scalar2=PI_IN,
                                                  op0=ALU.max, op1=ALU.min), reads=[r], writes=[r])

        G_ = 2 if nQT % 2 == 0 else 1
        h_sbs = [h_sb] + [S.sbuf("h_sbx%d" % i, [128, KC, TQ], BF16) for i in range(G_ - 1)]
        cos_ts = [cos_t] + [S.sbuf("cos_x%d" % i, [128, TQ], F32) for i in range(G_ - 1)]
        sin_ts = [sin_t] + [S.sbuf("sin_x%d" % i, [128, TQ], F32) for i in range(G_ - 1)]
        cc_sbs = [cc_sb] + [S.sbuf("cc_x%d" % i, [128, 4, TQ], F32) for i in range(G_ - 1)]
        for tg in range(nQT // G_):
            for s_ in range(G_):
                tt = tg * G_ + s_
                h_sb, cos_t, sin_t, cc_sb = h_sbs[s_], cos_ts[s_], sin_ts[s_], cc_sbs[s_]
                t0 = tt * TQ
                for (c_lo, c_hi, h_ap, h_tt) in hT_fn(tt):
                    S.dma(h_sb[:, :, c_lo:c_hi], h_ap, reads=[h_tt], writes=[h_sb])
                S.dma(posi[:, :], pos.t[0:1, t0:t0 + TQ].partition_broadcast(128), reads=[pos], writes=[posi])
                S.op("dve", lambda h: h.tensor_copy(out=ang[:, :], in_=posi[:, :]), reads=[posi], writes=[ang])
                S.op("dve", lambda h: h.tensor_scalar(out=ang[:, :], in0=ang[:, :], scalar1=fcol("invf"), scalar2=None, op0=ALU.mult),
                     reads=[ang, cF], writes=[ang])
                S.op("dve", lambda h: h.tensor_scalar(out=kf[:, :], in0=ang[:, :], scalar1=float(1.0 / TWO_PI), scalar2=None, op0=ALU.mult),
                     reads=[ang], writes=[kf])
                S.op("dve", lambda h: h.tensor_copy(out=ki[:, :], in_=kf[:, :]), reads=[kf], writes=[ki])
                S.op("dve", lambda h: h.tensor_copy(out=kf[:, :], in_=ki[:, :]), reads=[ki], writes=[kf])
                C1 = 6.28125
                C2 = float(TWO_PI - 6.28125)
                S.op("dve", lambda h: h.scalar_tensor_tensor(out=rr[:, :], in0=kf[:, :], scalar=-C1, in1=ang[:, :], op0=ALU.mult, op1=ALU.add),
                     reads=[kf, ang], writes=[rr])
                S.op("dve", lambda h: h.scalar_tensor_tensor(out=rr[:, :], in0=kf[:, :], scalar=-C2, in1=rr[:, :], op0=ALU.mult, op1=ALU.add),
                     reads=[kf, rr], writes=[rr])
                S.op("dve", lambda h: h.tensor_scalar(out=fx[:, :], in0=rr[:, :], scalar1=float(-np.pi), scalar2=TWO_PI,
                                                      op0=ALU.is_lt, op1=ALU.mult), reads=[rr], writes=[fx])
                S.op("dve", lambda h: h.tensor_tensor(out=rr[:, :], in0=rr[:, :], in1=fx[:, :], op=ALU.add), reads=[rr, fx], writes=[rr])
                S.op("dve", lambda h: h.tensor_scalar(out=rc[:, :], in0=rr[:, :], scalar1=float(np.pi / 2), scalar2=None, op0=ALU.add),
                     reads=[rr], writes=[rc])
                wrap_pi(rr)
                wrap_pi(rc)
                S.op("act", lambda h, sin_t=sin_t: h.activation(out=sin_t[:, :], in_=rr[:, :], func=AF.Sin, scale=fcol("sgn")), reads=[rr, cF], writes=[sin_t])
                S.op("act", lambda h, cos_t=cos_t: h.activation(out=cos_t[:, :], in_=rc[:, :], func=AF.Sin), reads=[rc], writes=[cos_t])
            def _ld(ci):
                return WL.load(wcat, wcat.t[:, ci * 128:(ci + 1) * 128].rearrange("(kc p) n -> p kc n", p=128))
            pre = [_ld(0), _ld(1)]
            for ci, nm in enumerate(FM):
                wb = pre.pop(0)
                if ci + 2 < NFM:
                    pre.append(_ld(ci + 2))
                for s_ in range(G_):
                    t0 = (tg * G_ + s_) * TQ
                    h_sb, cos_t, sin_t, cc_sb = h_sbs[s_], cos_ts[s_], sin_ts[s_], cc_sbs[s_]
                    ps = psr.get()
                    mm(S, ps, ps[:, :], [(wb[:, k, :], h_sb[:, k, :]) for k in range(KC)], reads=[wb, h_sb])
                    if nm in ("q0", "q1", "q2", "q3", "kc", "ks", "kw"):
                        S.op("act", lambda h, ps=ps: h.activation(out=xr[:, :], in_=ps[:, :], func=AF.Copy), reads=[ps], writes=[xr])
                        mm(S, ps_sw, ps_sw[:, :], [(cF[:, 384:512], xr[:, :])], reads=[cF, xr])
                        S.op("dve", lambda h, cos_t=cos_t: h.tensor_tensor(out=t1[:, :], in0=xr[:, :], in1=cos_t[:, :], op=ALU.mult), reads=[xr, cos_t], writes=[t1])
                        S.op("dve", lambda h, sin_t=sin_t: h.tensor_tensor(out=t2[:, :], in0=ps_sw[:, :], in1=sin_t[:, :], op=ALU.mult), reads=[ps_sw, sin_t], writes=[t2])
                        ob = b16r.get()
                        S.op("dve", lambda h, ob=ob: h.tensor_tensor(out=ob[:, :], in0=t1[:, :], in1=t2[:, :], op=ALU.add), reads=[t1, t2], writes=[ob])
                        if nm[0] == "q":
                            S.dma(qT_d.t[int(nm[1]), :, t0:t0 + TQ], ob[:, :], reads=[ob], writes=[qT_d])
                        else:
                            S.dma(kT_d[nm].t[:, t0:t0 + TQ], ob[:, :], reads=[ob], writes=[kT_d[nm]])
                    elif nm == "vc":
                        ob = b16r.get()
                        S.op("act", lambda h, ps=ps, ob=ob: h.activation(out=ob[:, :], in_=ps[:, :], func=AF.Copy), reads=[ps], writes=[ob])
                        S.dma(kT_d["vc"].t[:, t0:t0 + TQ], ob[:, :], reads=[ob], writes=[kT_d["vc"]])
                    elif nm.startswith("cc"):
                        c = int(nm[2])
                        S.op("act", lambda h, ps=ps, c=c, cc_sb=cc_sb: h.activation(out=cc_sb[:, c, :], in_=ps[:, :], func=AF.Copy), reads=[ps], writes=[cc_sb])
                    elif nm[0] == "u":
                        c = int(nm[1])
                        of = f32r.get()
                        S.op("dve", lambda h, ps=ps, c=c, of=of, cc_sb=cc_sb: h.tensor_tensor(out=of[:, :], in0=ps[:, :], in1=cc_sb[:, c, :], op=ALU.mult),
                             reads=[ps, cc_sb], writes=[of])
                        S.dma(cu_d.t[c * 128:(c + 1) * 128, t0:t0 + TQ], of[:, :], reads=[of], writes=[cu_d])
                    else:
                        of = f32r.get()
                        fn = AF.Sigmoid if nm.startswith("gm") else AF.Copy
                        S.op("act", lambda h, ps=ps, of=of, fn=fn: h.activation(out=of[:, :], in_=ps[:, :], func=fn), reads=[ps], writes=[of])
                        if nm.startswith("cb"):
                            dst, c = cb_d, int(nm[2])
                        elif nm[0] == "z":
                            dst, c = z_d, int(nm[1])
                        elif nm.startswith("xb"):
                            dst, c = xb_d, int(nm[2])
                        else:
                            dst, c = gm_d, int(nm[2:])
                        S.dma(dst.t[c * 128:(c + 1) * 128, t0:t0 + TQ], of[:, :], reads=[of], writes=[dst])
            for (c0, ncol, dst, fn) in ((COL_GN, 12, gn_d, AF.Sigmoid), (COL_DT, 8, dt_d, AF.Copy)):
                wb = WL.load(wcat, wcat.t[:, c0:c0 + ncol].rearrange("(kc p) n -> p kc n", p=128), ncol=ncol)
                for s_ in range(G_):
                    t0 = (tg * G_ + s_) * TQ
                    h_sb, cos_t, sin_t, cc_sb = h_sbs[s_], cos_ts[s_], sin_ts[s_], cc_sbs[s_]
                    ps = psr.get()
                    mm(S, ps, ps[0:ncol, :], [(wb[:, k, 0:ncol], h_sb[:, k, :]) for k in range(KC)], reads=[wb, h_sb])
                    of = f32r.get()
                    S.op("act", lambda h, ps=ps, of=of, fn=fn, ncol=ncol: h.activation(out=of[0:ncol, :], in_=ps[0:ncol, :], func=fn), reads=[ps], writes=[of])
                    S.dma(dst.t[:, t0:t0 + TQ], of[0:ncol, :], reads=[of], writes=[dst])
            wv = [WL.load(wcat, wcat.t[:, COL_V + i * 128:COL_V + (i + 1) * 128].rearrange("(kc p) n -> p kc n", p=128)) for i in range(2)]
            for s_ in range(G_):
                t0 = (tg * G_ + s_) * TQ
                h_sb, cos_t, sin_t, cc_sb = h_sbs[s_], cos_ts[s_], sin_ts[s_], cc_sbs[s_]
                for s4 in range(4):
                    ps = psr.get()
                    for i in range(2):
                        mm(S, ps, ps[:, i * 128:(i + 1) * 128], [(h_sb[:, k, s4 * 128:(s4 + 1) * 128], wv[i][:, k, :]) for k in range(KC)],
                           reads=[wv[i], h_sb])
                    vt = vt_sb.get()
                    S.op("act", lambda h, ps=ps, vt=vt: h.activation(out=vt[:, :], in_=ps[:, 0:256], func=AF.Copy), reads=[ps], writes=[vt])
                    S.dma(vtok_d.t[t0 + s4 * 128:t0 + (s4 + 1) * 128, :], vt[:, :], reads=[vt], writes=[vtok_d])
        S.end_stage()
        build_p1_rest(nc, S, locals())
        S.pop_scope()


def build_p1_rest(nc, S, V):
    T, nQT, NKT, NCMP, L, scale = V["T"], V["nQT"], V["NKT"], V["NCMP"], V["L"], V["scale"]
    cF, cB, prm = V["cF"], V["cB"], V["prm"]
    identF, identB, onesB, eps_col = V["identF"], V["identB"], V["onesB"], V["eps_col"]
    fcol, pcol = V["fcol"], V["pcol"]
    qT_d, kT_d, vtok_d = V["qT_d"], V["kT_d"], V["vtok_d"]
    cb_d, cu_d, z_d, xb_d, gm_d, gn_d, dt_d = V["cb_d"], V["cu_d"], V["z_d"], V["xb_d"], V["gm_d"], V["gn_d"], V["dt_d"]
    mA_d, mB_d, mC_d, mT = V["mA_d"], V["mB_d"], V["mC_d"], V["mT"]
    peT_d, w1_d, w2_d, vb2_d = V["peT_d"], V["w1_d"], V["w2_d"], V["vb2_d"]

    def bc(ap, shape):
        return ap.to_broadcast(list(shape))

    S.begin_stage()
    cur = Rot([S.sbuf("cu%d" % i, [128, TQ + 2], F32) for i in range(2)])
    cbr = Rot([S.sbuf("cbt%d" % i, [128, TQ], F32) for i in range(2)])
    gmr = Rot([S.sbuf("gmt%d" % i, [128, TQ], F32) for i in range(2)])
    acr = Rot([S.sbuf("acc%d" % i, [128, TQ], F32) for i in range(2)])
    for tt in range(nQT):
        t0 = tt * TQ
        for c in range(4):
            cu, cbt, gmt, acc = cur.get(), cbr.get(), gmr.get(), acr.get()
            rows = slice(c * 128, (c + 1) * 128)
            if tt == 0:
                S.op("dve", lambda h, cu=cu: h.memset(cu[:, 0:2], 0.0), writes=[cu])
                S.dma(cu[:, 2:TQ + 2], cu_d.t[rows, 0:TQ], reads=[cu_d], writes=[cu])
            else:
                S.dma(cu[:, :], cu_d.t[rows, t0 - 2:t0 + TQ], reads=[cu_d], writes=[cu])
            S.dma(cbt[:, :], cb_d.t[rows, t0:t0 + TQ], reads=[cb_d], writes=[cbt])
            S.dma(gmt[:, :], gm_d.t[(4 + c) * 128:(5 + c) * 128, t0:t0 + TQ], reads=[gm_d], writes=[gmt])
            S.op("dve", lambda h, cu=cu, acc=acc, c=c: h.tensor_scalar(out=acc[:, :], in0=cu[:, 0:TQ], scalar1=pcol("scw", c * 3), scalar2=None, op0=ALU.mult),
                 reads=[cu, prm], writes=[acc])
            for j in (1, 2):
                S.op("dve", lambda h, cu=cu, acc=acc, c=c, j=j: h.scalar_tensor_tensor(
                    out=acc[:, :], in0=cu[:, j:j + TQ], scalar=pcol("scw", c * 3 + j), in1=acc[:, :], op0=ALU.mult, op1=ALU.add),
                    reads=[cu, prm, acc], writes=[acc])
            S.op("dve", lambda h, acc=acc, cbt=cbt: h.tensor_tensor(out=acc[:, :], in0=acc[:, :], in1=cbt[:, :], op=ALU.mult), reads=[acc, cbt], writes=[acc])
            S.op("dve", lambda h, acc=acc, gmt=gmt: h.tensor_tensor(out=acc[:, :], in0=acc[:, :], in1=gmt[:, :], op=ALU.mult), reads=[acc, gmt], writes=[acc])
            S.dma(mB_d.t[rows, t0:t0 + TQ], acc[:, :], reads=[acc], writes=[mB_d])
    S.end_stage()

    S.begin_stage()
    CH = 128
    xbr = Rot([S.sbuf("xb%d" % i, [128, 6, CH + 3], F32) for i in range(2)])
    xc = S.sbuf("xc", [128, 6, CH], F32)
    acc6 = S.sbuf("acc6", [128, 6, CH], F32)
    BT = S.sbuf("BT", [128, CH], BF16)
    CT = S.sbuf("CT", [128, CH], BF16)
    Bk = S.sbuf("Bk", [128, CH], BF16)
    dtr = S.sbuf("dtr", [8, CH], F32)
    dtT = S.sbuf("dtT", [8, CH], F32)
    daT = S.sbuf("daT", [8, CH], F32)
    nA = S.sbuf("nA", [8, 2], F32)
    dtk = S.sbuf("dtk", [128, 8], F32)
    dak = S.sbuf("dak", [128, 8], F32)
    nacol = S.sbuf("nacol", [128, 8], F32)
    alast = S.sbuf("alast", [128, 8], F32)
    decay = S.sbuf("decay", [128, 8], F32)
    wcol = S.sbuf("wcol", [128, 8], F32)
    darep = S.sbuf("darep", [128, 8, CH], F32)
    diffm = S.sbuf("diffm", [128, 8, CH], F32)
    seg = S.sbuf("seg", [128, 8, CH], F32)
    ea = S.sbuf("ea", [128, 8, CH], F32)
    G = S.sbuf("G", [128, 8, CH], BF16)
    Cexp = S.sbuf("Cexp", [128, 8, CH], BF16)
    xdt32 = S.sbuf("xdt32", [128, 8, 64], F32)
    xdtw = S.sbuf("xdtw", [128, 8, 64], BF16)
    xdt_pad = S.sbuf("xdt_pad", [128, 8, 128], BF16)
    S_pad = S.sbuf("S_pad", [128, 8, 128], BF16)
    S32 = S.sbuf("S32", [128, 8, 64], F32)
    zr = Rot([S.sbuf("zs%d" % i, [128, 4, CH], F32) for i in range(2)])
    g2r = Rot([S.sbuf("g2s%d" % i, [128, 4, CH], F32) for i in range(2)])
    sz = S.sbuf("sz", [128, 4, CH], F32)
    yv = S.sbuf("yv", [128, 4, CH], F32)
    sq4 = S.sbuf("sq4", [128, 4, CH], BF16)
    rs = S.sbuf("rs", [128, CH], F32)
    ycr = Rot([S.sbuf("yc%d" % i, [128, 4, CH], F32) for i in range(2)])
    p_t = S.psum("p_t", [128, TQ])
    p_t2 = S.psum("p_t2", [128, TQ])
    p_ar = S.psum("p_ar", [128, 1024])
    p_cb = S.psum("p_cb", [128, TQ])
    p_y = S.psum("p_y", [128, TQ])
    p_cs = S.psum("p_cs", [128, TQ])
    p_bt = S.psum("p_bt", [128, 256], BF16)
    Umat = cF[:, 128:256]
    ssdneg = cF[:, 256:384]
    S.op("dve", lambda h: h.memset(xdt_pad[:, :, :], 0.0), writes=[xdt_pad])
    S.op("dve", lambda h: h.memset(S_pad[:, :, :], 0.0), writes=[S_pad])
    S.op("dve", lambda h: h.memset(S32[:, :, :], 0.0), writes=[S32])
    S.op("act", lambda h: h.activation(out=nA[:, 0:1], in_=pcol("alog", rows=8), func=AF.Exp), reads=[prm], writes=[nA])
    S.op("dve", lambda h: h.tensor_scalar(out=nA[:, 1:2], in0=nA[:, 0:1], scalar1=-1.0, scalar2=None, op0=ALU.mult), reads=[nA], writes=[nA])
    ar3 = p_ar.t[:, :].rearrange("p (h l) -> p h l", h=8)

    def pad_copy(dst, src32):
        d5 = dst.t[:, :, :].rearrange("p (a e) (s q) -> p a e s q", e=2, s=2)
        s4 = src32.t[:, :, :].rearrange("p (a e) q -> p a e q", e=2)
        for e in range(2):
            S.op("dve", lambda h, e=e: h.tensor_copy(out=d5[:, :, e, e, :], in_=s4[:, :, e, :]), reads=[src32], writes=[dst])

    for ch in range(T // CH):
        t0 = ch * CH
        xb = xbr.get()
        src = xb_d.t.rearrange("(c p) t -> p c t", p=128)
        if ch == 0:
            S.op("dve", lambda h, xb=xb: h.memset(xb[:, :, 0:3], 0.0), writes=[xb])
            S.dma(xb[:, :, 3:CH + 3], src[:, :, 0:CH], reads=[xb_d], writes=[xb])
        else:
            S.dma(xb[:, :, :], src[:, :, t0 - 3:t0 + CH], reads=[xb_d], writes=[xb])
        S.dma(dtr[:, :], dt_d.t[:, t0:t0 + CH], reads=[dt_d], writes=[dtr])
        zs, g2s = zr.get(), g2r.get()
        S.dma(zs[:, :, :], z_d.t.rearrange("(c p) t -> p c t", p=128)[:, :, t0:t0 + CH], reads=[z_d], writes=[zs])
        S.dma(g2s[:, :, :], gm_d.t[1024:1536, :].rearrange("(c p) t -> p c t", p=128)[:, :, t0:t0 + CH], reads=[gm_d], writes=[g2s])
        for c in range(6):
            S.op("dve", lambda h, xb=xb, c=c: h.tensor_scalar(out=acc6[:, c, :], in0=xb[:, c, 0:CH], scalar1=pcol("sdw", c * 4), scalar2=None, op0=ALU.mult),
                 reads=[xb, prm], writes=[acc6])
            for j in (1, 2, 3):
                S.op("dve", lambda h, xb=xb, c=c, j=j: h.scalar_tensor_tensor(
                    out=acc6[:, c, :], in0=xb[:, c, j:j + CH], scalar=pcol("sdw", c * 4 + j), in1=acc6[:, c, :], op0=ALU.mult, op1=ALU.add),
                    reads=[xb, prm, acc6], writes=[acc6])
            S.op("act", lambda h, c=c: h.activation(out=xc[:, c, :], in_=acc6[:, c, :], func=AF.Silu, bias=pcol("sdb", c)), reads=[acc6, prm], writes=[xc])
        S.op("dve", lambda h: h.tensor_copy(out=BT[:, :], in_=xc[:, 4, :]), reads=[xc], writes=[BT])
        S.op("dve", lambda h: h.tensor_copy(out=CT[:, :], in_=xc[:, 5, :]), reads=[xc], writes=[CT])
        S.op("act", lambda h: h.activation(out=dtT[:, :], in_=dtr[:, :], func=AF.Exp, bias=pcol("dtb", rows=8)), reads=[dtr, prm], writes=[dtT])
        S.op("act", lambda h: h.activation(out=dtT[:, :], in_=dtT[:, :], func=AF.Ln, bias=cF[0:8, 771:772]), reads=[dtT, cF], writes=[dtT])
        S.op("dve", lambda h: h.tensor_scalar(out=daT[:, :], in0=dtT[:, :], scalar1=nA[:, 1:2], scalar2=None, op0=ALU.mult), reads=[dtT, nA], writes=[daT])
        S.op("pe", lambda h: h.transpose(out=p_t[:, 0:8], in_=dtT[:, :], identity=cF[0:8, 0:8]), reads=[dtT, cF], writes=[p_t])
        S.op("pe", lambda h: h.transpose(out=p_t[:, 8:16], in_=daT[:, :], identity=cF[0:8, 0:8]), reads=[daT, cF], writes=[p_t])
        S.op("dve", lambda h: h.tensor_copy(out=dtk[:, :], in_=p_t[:, 0:8]), reads=[p_t], writes=[dtk])
        S.op("dve", lambda h: h.tensor_copy(out=dak[:, :], in_=p_t[:, 8:16]), reads=[p_t], writes=[dak])
        mm(S, p_t, p_t[:, 16:24], [(Umat, dak[:, :])], reads=[cF, dak])
        S.op("dve", lambda h: h.tensor_copy(out=darep[:, :, :], in_=bc(dak[:, 0:8].unsqueeze(2), [128, 8, CH])), reads=[dak], writes=[darep])
        for hd in range(8):
            mm(S, p_ar, ar3[:, hd, :], [(darep[:, hd, :], Umat)], reads=[darep, cF])
        S.op("dve", lambda h: h.tensor_scalar(out=nacol[:, :], in0=p_t[:, 16:24], scalar1=-1.0, scalar2=None, op0=ALU.mult), reads=[p_t], writes=[nacol])
        S.op("dve", lambda h: h.tensor_copy(out=alast[:, :], in_=ar3[:, :, CH - 1]), reads=[p_ar], writes=[alast])
        S.op("act", lambda h: h.activation(out=decay[:, :], in_=alast[:, :], func=AF.Exp), reads=[alast], writes=[decay])
        S.op("dve", lambda h: h.tensor_tensor(out=wcol[:, :], in0=alast[:, :], in1=nacol[:, :], op=ALU.add), reads=[alast, nacol], writes=[wcol])
        S.op("act", lambda h: h.activation(out=wcol[:, :], in_=wcol[:, :], func=AF.Exp), reads=[wcol], writes=[wcol])
        for c in range(4):
            S.op("pe", lambda h, c=c: h.transpose(out=p_t2[:, c * 128:(c + 1) * 128], in_=xc[:, c, :], identity=identF), reads=[xc, cF], writes=[p_t2])
        pt3 = p_t2.t[:, :].rearrange("p (h q) -> p h q", h=8)
        S.op("dve", lambda h: h.tensor_tensor(out=xdt32[:, :, :], in0=pt3, in1=bc(dtk[:, 0:8].unsqueeze(2), [128, 8, 64]), op=ALU.mult),
             reads=[p_t2, dtk], writes=[xdt32])
        pad_copy(xdt_pad, xdt32)
        S.op("dve", lambda h: h.tensor_tensor(out=xdtw[:, :, :], in0=xdt32[:, :, :], in1=bc(wcol[:, 0:8].unsqueeze(2), [128, 8, 64]), op=ALU.mult),
             reads=[xdt32, wcol], writes=[xdtw])
        S.op("pe", lambda h: h.transpose(out=p_bt[:, 0:128], in_=BT[:, :], identity=identB), reads=[BT, cB], writes=[p_bt])
        S.op("dve", lambda h: h.tensor_copy(out=Bk[:, :], in_=p_bt[:, 0:128]), reads=[p_bt], writes=[Bk])
        mm(S, p_cb, p_cb[:, 0:CH], [(BT[:, :], CT[:, :])], reads=[BT, CT])
        S.op("dve", lambda h: h.tensor_tensor(out=diffm[:, :, :], in0=ar3, in1=bc(ssdneg.unsqueeze(1), [128, 8, CH]), op=ALU.add),
             reads=[p_ar, cF], writes=[diffm])
        for hd in range(8):
            S.op("act", lambda h, hd=hd: h.activation(out=seg[:, hd, :], in_=diffm[:, hd, :], func=AF.Exp, bias=nacol[:, hd:hd + 1]),
                 reads=[diffm, nacol], writes=[seg])
        S.op("dve", lambda h: h.tensor_tensor(out=G[:, :, :], in0=seg[:, :, :], in1=bc(p_cb[:, 0:CH].unsqueeze(1), [128, 8, CH]), op=ALU.mult),
             reads=[seg, p_cb], writes=[G])
        for half in range(2):
            S.op("act", lambda h, half=half: h.activation(out=ea[:, half * 4:(half + 1) * 4, :], in_=ar3[:, half * 4:(half + 1) * 4, :], func=AF.Exp),
                 reads=[p_ar], writes=[ea])
        S.op("dve", lambda h: h.tensor_tensor(out=Cexp[:, :, :], in0=ea[:, :, :], in1=bc(CT[:, :].unsqueeze(1), [128, 8, CH]), op=ALU.mult),
             reads=[ea, CT], writes=[Cexp])
        for c in range(4):
            pairs = []
            for e in range(2):
                pairs.append((xdt_pad[:, 2 * c + e, :], G[:, 2 * c + e, :]))
                pairs.append((S_pad[:, 2 * c + e, :], Cexp[:, 2 * c + e, :]))
            mm(S, p_y, p_y[:, c * 128:(c + 1) * 128], pairs, reads=[xdt_pad, G, S_pad, Cexp])
        py3 = p_y.t[:, :].rearrange("p (c l) -> p c l", c=4)
        for c in range(4):
            S.op("dve", lambda h, c=c: h.scalar_tensor_tensor(out=yv[:, c, :], in0=xc[:, c, :], scalar=pcol("dsk", c), in1=py3[:, c, :],
                                                              op0=ALU.mult, op1=ALU.add), reads=[xc, prm, p_y], writes=[yv])
        mm(S, p_cs, p_cs[:, :], [(Bk[:, :], xdtw[:, :, :].rearrange("p h q -> p (h q)"))], reads=[Bk, xdtw])
        S.op("dve", lambda h: h.tensor_tensor(out=S32[:, :, :], in0=S32[:, :, :], in1=bc(decay[:, 0:8].unsqueeze(2), [128, 8, 64]), op=ALU.mult),
             reads=[S32, decay], writes=[S32])
        S.op("dve", lambda h: h.tensor_tensor(out=S32[:, :, :], in0=S32[:, :, :], in1=p_cs.t[:, :].rearrange("p (h q) -> p h q", h=8), op=ALU.add),
             reads=[S32, p_cs], writes=[S32])
        pad_copy(S_pad, S32)
        S.op("act", lambda h, zs=zs: h.activation(out=sz[:, :, :], in_=zs[:, :, :], func=AF.Silu), reads=[zs], writes=[sz])
        S.op("dve", lambda h: h.tensor_tensor(out=yv[:, :, :], in0=yv[:, :, :], in1=sz[:, :, :], op=ALU.mult), reads=[yv, sz], writes=[yv])
        S.op("act", lambda h: h.activation(out=sq4[:, :, :], in_=yv[:, :, :], func=AF.Square), reads=[yv], writes=[sq4])
        mm(S, p_cb, p_cb[:, 128:256], [(onesB, sq4[:, c, :]) for c in range(4)], reads=[cB, sq4])
        S.op("act", lambda h: h.activation(out=rs[:, :], in_=p_cb[:, 128:256], func=AF.Ln, bias=eps_col, scale=1.0 / 512), reads=[p_cb, cF], writes=[rs])
        S.op("act", lambda h: h.activation(out=rs[:, :], in_=rs[:, :], func=AF.Exp, scale=-0.5), reads=[rs], writes=[rs])
        yc = ycr.get()
        for c in range(4):
            S.op("dve", lambda h, c=c, yc=yc: h.scalar_tensor_tensor(out=yc[:, c, :], in0=yv[:, c, :], scalar=pcol("ng", c), in1=rs[:, :],
                                                                     op0=ALU.mult, op1=ALU.mult), reads=[yv, prm, rs], writes=[yc])
        S.op("dve", lambda h, yc=yc, g2s=g2s: h.tensor_tensor(out=yc[:, :, :], in0=yc[:, :, :], in1=g2s[:, :, :], op=ALU.mult), reads=[yc, g2s], writes=[yc])
        S.dma(mC_d.t.rearrange("(c p) t -> p c t", p=128)[:, :, t0:t0 + CH], yc[:, :, :], reads=[yc], writes=[mC_d])
    S.end_stage()
    build_p1_nsa(nc, S, V)


def build_p1_nsa(nc, S, V):
    T, nQT, NKT, NCMP, L, scale = V["T"], V["nQT"], V["NKT"], V["NCMP"], V["L"], V["scale"]
    cF, cB, prm = V["cF"], V["cB"], V["prm"]
    identF, identB, onesB, eps_col = V["identF"], V["identB"], V["onesB"], V["eps_col"]
    fcol, pcol = V["fcol"], V["pcol"]
    qT_d, kT_d, vtok_d = V["qT_d"], V["kT_d"], V["vtok_d"]
    gm_d, gn_d = V["gm_d"], V["gn_d"]
    mA_d, mB_d, mC_d, mT = V["mA_d"], V["mB_d"], V["mC_d"], V["mT"]
    peT_d, w1_d, w2_d, vb2_d = V["peT_d"], V["w1_d"], V["w2_d"], V["vb2_d"]
    NC4 = (NCMP + 127) // 128
    NCP = NC4 * 128

    def cmask(di):
        return cB[:, L["cmask"] + di * TQ:L["cmask"] + (di + 1) * TQ]

    def wmask(j):
        return cB[:, L["wmask"] + j * TQ:L["wmask"] + (j + 1) * TQ]

    def ebig(kt):
        return cB[:, L["ebig"] + kt * 128:L["ebig"] + (kt + 1) * 128]

    def ovl1(i):
        return cB[:, L["ovl1"] + i * 129:L["ovl1"] + (i + 1) * 129]

    def onesel(r):
        return cB[:, L["onesel12"] + r * 12:L["onesel12"] + (r + 1) * 12]

    def sel12(r):
        return cB[0:12, L["sel12"] + r * 128:L["sel12"] + (r + 1) * 128]

    S.begin_stage()
    srcT = {n: S.sbuf(n + "T", [128, T], BF16) for n in ("kc", "ks", "kw")}
    srcT["vc"] = srcT["kc"]
    vs = S.sbuf("vs", [128, NKT, 128], BF16)
    vw = S.sbuf("vw", [128, NKT, 128], BF16)
    kcmpT = S.sbuf("kcmpT", [128, NCP], BF16)
    vcmp = S.sbuf("vcmp", [128, NC4, 128], BF16)
    for n in ("ks", "kw"):
        S.dma(srcT[n][:, :], kT_d[n].t[:, :], reads=[kT_d[n]], writes=[srcT[n]])
    vt3 = vtok_d.t.rearrange("(kt p) d -> p kt d", p=128)
    S.dma(vs[:, :, :], vt3[:, :, 0:128], reads=[vtok_d], writes=[vs])
    S.dma(vw[:, :, :], vt3[:, :, 128:256], reads=[vtok_d], writes=[vw])
    w1st = S.sbuf("w1st", [128, 32, 128], F32)
    w1bf = S.sbuf("w1bf", [128, 32, 128], BF16)
    w2st = S.sbuf("w2st", [128, 128], F32)
    w2bf = S.sbuf("w2bf", [128, 128], BF16)
    pest = S.sbuf("pest", [128, 64], F32)
    pebf = S.sbuf("pebf", [128, 64], BF16)
    vb2s = S.sbuf("vb2s", [1, 128], F32)
    vb2b = S.sbuf("vb2b", [1, 128], BF16)
    btot = S.sbuf("btot", [128, 1], F32)
    hs = S.sbuf("hs", [128, NCP], BF16)
    p_sc = Rot([S.psum("p_sc%d" % i, [128, TQ]) for i in range(2)])
    p_o = [S.psum("p_o%d" % i, [128, TQ]) for i in range(3)]
    p_den = S.psum("p_den", [128, TQ])
    p_u = S.psum("p_u", [128, TQ])
    p_x = S.psum("p_x", [128, TQ])
    S.dma(pest[:, :], peT_d[:, :], reads=[peT_d], writes=[pest])
    S.op("dve", lambda h: h.tensor_copy(out=pebf[:, :], in_=pest[:, :]), reads=[pest], writes=[pebf])
    S.dma(vb2s[:, :], vb2_d[:, :], reads=[vb2_d], writes=[vb2s])
    S.op("dve", lambda h: h.tensor_copy(out=vb2b[:, :], in_=vb2s[:, :]), reads=[vb2s], writes=[vb2b])
    S.op("dve", lambda h: h.memset(kcmpT[:, :], 0.0), writes=[kcmpT])
    for wi, nm in enumerate(("kc", "vc")):
        S.dma(w1st[:, :, :], w1_d[wi].t.rearrange("(j d) h -> d j h", d=128), reads=[w1_d[wi]], writes=[w1st])
        S.op("pool", lambda h: h.tensor_copy(out=w1bf[:, :, :], in_=w1st[:, :, :]), reads=[w1st], writes=[w1bf])
        S.dma(w2st[:, :], w2_d[wi].t[:, :], reads=[w2_d[wi]], writes=[w2st])
        S.op("dve", lambda h: h.tensor_copy(out=w2bf[:, :], in_=w2st[:, :]), reads=[w2st], writes=[w2bf])
        S.op("dve", lambda h: h.memset(hs[:, :], 0.0), writes=[hs])
        mm(S, p_x, p_x[:, 0:1], [(w1bf[:, j, :], pebf[:, wi * 32 + j:wi * 32 + j + 1]) for j in range(32)], reads=[w1bf, pebf])
        S.op("dve", lambda h, wi=wi: h.tensor_tensor(out=btot[:, :], in0=p_x[:, 0:1], in1=pcol("kb1" if wi == 0 else "vb1"), op=ALU.add),
             reads=[p_x, prm], writes=[btot])
        src = srcT[nm]
        S.dma(src[:, :], kT_d[nm].t[:, :], reads=[kT_d[nm]], writes=[src])
        for n0 in range(0, NCMP, 512):
            nn = min(512, NCMP - n0)
            ps = p_sc.get()
            mm(S, ps, ps[:, 0:nn], [(w1bf[:, j, :], src[:, 16 * n0 + j:16 * n0 + j + 16 * (nn - 1) + 1:16]) for j in range(32)], reads=[w1bf, src])
            S.op("act", lambda h, ps=ps, n0=n0, nn=nn: h.activation(out=hs[:, n0:n0 + nn], in_=ps[:, 0:nn], func=AF.Silu, bias=btot[:, 0:1]),
                 reads=[ps, btot], writes=[hs])
            if wi == 0:
                ps2 = p_sc.get()
                mm(S, ps2, ps2[:, 0:nn], [(w2bf[:, :], hs[:, n0:n0 + nn])], reads=[w2bf, hs])
                S.op("act", lambda h, ps2=ps2, n0=n0, nn=nn: h.activation(out=kcmpT[:, n0:n0 + nn], in_=ps2[:, 0:nn], func=AF.Identity, bias=pcol("kb2")),
                     reads=[ps2, prm], writes=[kcmpT])
        if wi == 1:
            for i in range(NC4):
                ps3 = p_sc.get()
                mm(S, ps3, ps3[:, 0:128], [(hs[:, 128 * i:128 * i + 128], w2bf[:, :]), (cB[0:1, L["ones"]:L["ones"] + 128], vb2b[0:1, :])],
                   reads=[hs, w2bf, cB, vb2b])
                S.op("act", lambda h, ps3=ps3, i=i: h.activation(out=vcmp[:, i, :], in_=ps3[:, 0:128], func=AF.Copy), reads=[ps3], writes=[vcmp])
    q_sb = S.sbuf("q_sb", [128, 4, TQ], BF16)
    gn12 = S.sbuf("gn12", [12, TQ], F32)
    gm0 = S.sbuf("gm0", [128, 4, TQ], F32)
    ec = [S.sbuf("ec%d" % i, [128, TQ], BF16) for i in range(NC4)]
    er = Rot([S.sbuf("er%d" % i, [128, TQ], BF16) for i in range(3)])
    o_sb = [[S.sbuf("o%d_%d" % (b, h), [128, TQ], F32) for h in range(4)] for b in range(3)]
    imp = S.sbuf("imp", [128, 4, 128], F32)
    rd = S.sbuf("rd", [128, 1], F32)
    sc2 = S.sbuf("sc2", [128, 128], F32)
    sc3 = S.sbuf("sc3", [128, 128], F32)
    m8a = S.sbuf("m8a", [128, 8], F32)
    m8b = S.sbuf("m8b", [128, 8], F32)
    negm = S.sbuf("negm", [128, 128], F32)
    negmT = S.sbuf("negmT", [128, TQ], BF16)
    den_sb = S.sbuf("den_sb", [12, TQ], F32)
    fct = S.sbuf("fct", [12, TQ], BF16)
    ya = S.sbuf("ya", [128, TQ], F32)
    tmpm = S.sbuf("tmpm", [128, TQ], F32)

    for qt in range(nQT):
        t0 = qt * TQ
        S.dma(q_sb[:, :, :], qT_d.t.rearrange("h p t -> p h t")[:, :, t0:t0 + TQ], reads=[qT_d], writes=[q_sb])
        S.dma(gn12[:, :], gn_d.t[:, t0:t0 + TQ], reads=[gn_d], writes=[gn12])
        S.dma(gm0[:, :, :], gm_d.t[0:512, :].rearrange("(c p) t -> p c t", p=128)[:, :, t0:t0 + TQ], reads=[gm_d], writes=[gm0])
        ntc = min(NC4, (32 * qt + 30) // 128 + 1)
        sel_kts = list(range(0, 4 * qt + 4))
        win_kts = list(range(max(0, 4 * qt - 4), 4 * qt + 4))
        n_den_total = 4 * (ntc + len(sel_kts) + len(win_kts))
        den_i = [0]

        def den_mm(r, e_t, den_i=den_i, n_den_total=n_den_total):
            i = den_i[0]
            den_i[0] += 1
            mm(S, p_den, p_den[0:12, :], [(onesel(r), e_t[:, :])], reads=[cB, e_t], start=(i == 0), stop=(i == n_den_total - 1))

        for h in range(4):
            for i in range(ntc):
                delta = 2048 * i - 512 * qt
                pairs = [(kcmpT[:, 128 * i:128 * i + 128], q_sb[:, h, :])]
                if -2048 <= delta <= 0:
                    pairs.append((identB, cmask((delta + 2048) // 512)))
                ps = p_sc.get()
                mm(S, ps, ps[:, :], pairs, reads=[kcmpT, q_sb, cB])
                S.op("act", lambda hh, ps=ps, i=i: hh.activation(out=ec[i][:, :], in_=ps[:, :], func=AF.Exp, scale=scale), reads=[ps], writes=[ec[i]])
                mm(S, p_o[0], p_o[0][:, :], [(vcmp[:, i, :], ec[i][:, :])], reads=[vcmp, ec[i]], start=(i == 0), stop=(i == ntc - 1))
                den_mm(3 * h + 0, ec[i])
            S.op("act", lambda hh, h=h: hh.activation(out=o_sb[0][h][:, :], in_=p_o[0][:, :], func=AF.Copy), reads=[p_o[0]], writes=[o_sb[0][h]])
            for s4 in range(4):
                mm(S, p_u, p_u[:, 0:129], [(ec[i][:, s4 * 128:(s4 + 1) * 128], ovl1(i)) for i in range(ntc)], reads=[cB] + ec[0:ntc])
                S.op("dve", lambda hh: hh.tensor_scalar(out=rd[:, :], in0=p_u[:, 128:129], scalar1=1e-30, scalar2=None, op0=ALU.max), reads=[p_u], writes=[rd])
                S.op("dve", lambda hh: hh.reciprocal(out=rd[:, :], in_=rd[:, :]), reads=[rd], writes=[rd])
                if h == 0:
                    S.op("dve", lambda hh, s4=s4: hh.tensor_scalar(out=imp[:, s4, :], in0=p_u[:, 0:128], scalar1=rd[:, 0:1], scalar2=None, op0=ALU.mult),
                         reads=[p_u, rd], writes=[imp])
                else:
                    S.op("dve", lambda hh, s4=s4: hh.scalar_tensor_tensor(out=imp[:, s4, :], in0=p_u[:, 0:128], scalar=rd[:, 0:1], in1=imp[:, s4, :],
                                                                           op0=ALU.mult, op1=ALU.add), reads=[p_u, rd, imp], writes=[imp])
        for s4 in range(4):
            g4 = 4 * qt + s4
            pb0 = CF["pbig"] + 128 - 2 * g4
            S.op("dve", lambda hh, s4=s4, pb0=pb0: hh.tensor_tensor(out=sc2[:, :], in0=imp[:, s4, :], in1=cF[:, pb0:pb0 + 128], op=ALU.add),
                 reads=[imp, cF], writes=[sc2])
            S.op("dve", lambda hh: hh.tensor_scalar(out=sc2[:, 0:1], in0=sc2[:, 0:1], scalar1=1e4, scalar2=None, op0=ALU.add), reads=[sc2], writes=[sc2])
            S.op("dve", lambda hh: hh.max(out=m8a[:, :], in_=sc2[:, :]), reads=[sc2], writes=[m8a])
            S.op("dve", lambda hh: hh.match_replace(out=sc3[:, :], in_to_replace=m8a[:, :], in_values=sc2[:, :], imm_value=-2e9), reads=[sc2, m8a], writes=[sc3])
            S.op("dve", lambda hh: hh.max(out=m8b[:, :], in_=sc3[:, :]), reads=[sc3], writes=[m8b])
            S.op("dve", lambda hh: hh.tensor_scalar(out=negm[:, :], in0=sc2[:, :], scalar1=m8b[:, 7:8], scalar2=NEG, op0=ALU.is_lt, op1=ALU.mult),
                 reads=[sc2, m8b], writes=[negm])
            S.op("pe", lambda hh: hh.transpose(out=p_u[:, 0:128], in_=negm[:, :], identity=identF), reads=[negm, cF], writes=[p_u])
            S.op("dve", lambda hh, s4=s4: hh.tensor_copy(out=negmT[:, s4 * 128:(s4 + 1) * 128], in_=p_u[:, 0:128]), reads=[p_u], writes=[negmT])
        items = []
        for h in range(4):
            for bi, kts in ((1, sel_kts), (2, win_kts)):
                for ii, kt in enumerate(kts):
                    if bi == 1:
                        pairs = [(srcT["ks"][:, 128 * kt:128 * kt + 128], q_sb[:, h, :]), (ebig(kt), negmT[:, :])]
                        if kt >= 4 * qt:
                            pairs.append((identB, wmask(4 + kt - 4 * qt)))
                        rds = [srcT["ks"], q_sb, cB, negmT]
                        vt = vs
                    else:
                        pairs = [(srcT["kw"][:, 128 * kt:128 * kt + 128], q_sb[:, h, :]), (identB, wmask(kt - (4 * qt - 4)))]
                        rds = [srcT["kw"], q_sb, cB]
                        vt = vw
                    items.append((h, bi, kt, pairs, rds, vt, ii == 0, ii == len(kts) - 1))

        def issue_scores(it):
            ps = p_sc.get()
            mm(S, ps, ps[:, :], it[3], reads=it[4])
            e_t = er.get()
            S.op("act", lambda hh, ps=ps, e_t=e_t: hh.activation(out=e_t[:, :], in_=ps[:, :], func=AF.Exp, scale=scale), reads=[ps], writes=[e_t])
            return e_t

        pend = issue_scores(items[0])
        for idx, it in enumerate(items):
            h, bi, kt, _, _, vt, first, last = it
            e_t = pend
            if idx + 1 < len(items):
                pend = issue_scores(items[idx + 1])
            mm(S, p_o[bi], p_o[bi][:, :], [(vt[:, kt, :], e_t[:, :])], reads=[vt, e_t], start=first, stop=last)
            den_mm(3 * h + bi, e_t)
            if last:
                S.op("act", lambda hh, h=h, bi=bi: hh.activation(out=o_sb[bi][h][:, :], in_=p_o[bi][:, :], func=AF.Copy), reads=[p_o[bi]], writes=[o_sb[bi][h]])
        assert den_i[0] == n_den_total
        S.op("dve", lambda hh: hh.tensor_scalar(out=den_sb[:, :], in0=p_den[0:12, :], scalar1=1e-30, scalar2=None, op0=ALU.max), reads=[p_den], writes=[den_sb])
        S.op("dve", lambda hh: hh.reciprocal(out=den_sb[:, :], in_=den_sb[:, :]), reads=[den_sb], writes=[den_sb])
        S.op("dve", lambda hh: hh.tensor_tensor(out=fct[:, :], in0=den_sb[:, :], in1=gn12[:, :], op=ALU.mult), reads=[den_sb, gn12], writes=[fct])
        for h in range(4):
            for b in range(3):
                mm(S, p_x, p_x[:, :], [(sel12(3 * h + b), fct[:, :])], reads=[cB, fct])
                if b == 0:
                    S.op("dve", lambda hh, h=h, b=b: hh.tensor_tensor(out=ya[:, :], in0=o_sb[b][h][:, :], in1=p_x[:, :], op=ALU.mult), reads=[o_sb[b][h], p_x], writes=[ya])
                else:
                    S.op("dve", lambda hh, h=h, b=b: hh.tensor_tensor(out=tmpm[:, :], in0=o_sb[b][h][:, :], in1=p_x[:, :], op=ALU.mult), reads=[o_sb[b][h], p_x], writes=[tmpm])
                    S.op("dve", lambda hh: hh.tensor_tensor(out=ya[:, :], in0=ya[:, :], in1=tmpm[:, :], op=ALU.add), reads=[ya, tmpm], writes=[ya])
            S.op("dve", lambda hh, h=h: hh.tensor_tensor(out=ya[:, :], in0=ya[:, :], in1=gm0[:, h, :], op=ALU.mult), reads=[ya, gm0], writes=[ya])
            S.dma(mA_d.t[h * 128:(h + 1) * 128, t0:t0 + TQ], ya[:, :], reads=[ya], writes=[mA_d])
    S.end_stage()

    S.begin_stage()
    ar_ = Rot([S.sbuf("ca%d" % i, [128, 4, TQ], F32) for i in range(2)])
    br_ = Rot([S.sbuf("cbb%d" % i, [128, 4, TQ], F32) for i in range(2)])
    cr_ = Rot([S.sbuf("ccc%d" % i, [128, 4, TQ], F32) for i in range(2)])
    or_ = Rot([S.sbuf("co%d" % i, [128, 4, TQ], BF16) for i in range(2)])
    for qt in range(nQT):
        t0 = qt * TQ
        a_, b_, c_, o_ = ar_.get(), br_.get(), cr_.get(), or_.get()
        for tl, src in ((a_, mA_d), (b_, mB_d), (c_, mC_d)):
            S.dma(tl[:, :, :], src.t.rearrange("(c p) t -> p c t", p=128)[:, :, t0:t0 + TQ], reads=[src], writes=[tl])
        S.op("dve", lambda hh, a_=a_, b_=b_: hh.tensor_tensor(out=a_[:, :, :], in0=a_[:, :, :], in1=b_[:, :, :], op=ALU.add), reads=[a_, b_], writes=[a_])
        S.op("dve", lambda hh, a_=a_, c_=c_, o_=o_: hh.tensor_tensor(out=o_[:, :, :], in0=a_[:, :, :], in1=c_[:, :, :], op=ALU.add), reads=[a_, c_], writes=[o_])
        mk = mT[t0 // 1024]
        S.dma(mk.t.rearrange("(c p) t -> p c t", p=128)[:, :, (t0 % 1024):(t0 % 1024) + TQ], o_[:, :, :], reads=[o_], writes=[mk])
    S.end_stage()


OFF = {"q": 0, "kc": 2048, "vc": 2560, "ks": 3072, "vs": 3584, "kw": 4096, "vw": 4608, "gn": 5120, "cb": 5168, "cc": 7216,
       "u": 9264, "z": 11312, "xbc": 13360, "dt": 16432, "gm": 16464}


def wcat_cols(g):
    cols = []
    for h in range(4):
        cols.append(np.arange(OFF["q"] + 512 * g + 128 * h, OFF["q"] + 512 * g + 128 * h + 128))
    for nm in ("kc", "ks", "kw", "vc"):
        cols.append(np.arange(OFF[nm] + 128 * g, OFF[nm] + 128 * g + 128))
    for nm in ("cb", "cc", "u", "z"):
        for i in range(4):
            cols.append(np.arange(OFF[nm] + 512 * g + 128 * i, OFF[nm] + 512 * g + 128 * i + 128))
    for i in range(4):
        cols.append(np.arange(OFF["xbc"] + 512 * g + 128 * i, OFF["xbc"] + 512 * g + 128 * i + 128))
    cols.append(np.arange(OFF["xbc"] + 2048 + 128 * g, OFF["xbc"] + 2048 + 128 * g + 128))
    cols.append(np.arange(OFF["xbc"] + 2560 + 128 * g, OFF["xbc"] + 2560 + 128 * g + 128))
    for i in range(12):
        j, c = i // 4, i % 4
        s0 = OFF["gm"] + j * 2048 + 512 * g + 128 * c
        cols.append(np.arange(s0, s0 + 128))
    cols.append(np.arange(OFF["vs"] + 128 * g, OFF["vs"] + 128 * g + 128))
    cols.append(np.arange(OFF["vw"] + 128 * g, OFF["vw"] + 128 * g + 128))
    cols.append(np.arange(OFF["gn"] + 12 * g, OFF["gn"] + 12 * g + 12))
    cols.append(np.arange(OFF["dt"] + 8 * g, OFF["dt"] + 8 * g + 8))
    cols = np.concatenate(cols)
    assert cols.shape[0] == NWC
    return cols


def ssd_ch(g, c):
    p = np.arange(128)
    if c < 4:
        return 512 * g + 128 * c + p
    return (2048 if c == 4 else 2560) + 128 * g + p


def prep_p1_layer(inp, l, g):
    p = np.arange(128)
    prm = np.zeros((128, PRM["n"]), np.float32)
    for c in range(4):
        ch = 512 * g + 128 * c + p
        for j in range(3):
            prm[:, PRM["scw"] + c * 3 + j] = inp["sconv_w"][l, j, ch]
        prm[:, PRM["dsk"] + c] = inp["ssd_d"][l, 8 * g + 2 * c + (p >= 64)]
        prm[:, PRM["ng"] + c] = inp["ssd_norm_g"][l, ch]
    for c in range(6):
        ch = ssd_ch(g, c)
        for j in range(4):
            prm[:, PRM["sdw"] + c * 4 + j] = inp["ssd_conv_w"][l, j, ch]
        prm[:, PRM["sdb"] + c] = inp["ssd_conv_b"][l, ch]
    prm[:, PRM["kb1"]] = inp["phi_k_b1"][l]
    prm[:, PRM["kb2"]] = inp["phi_k_b2"][l]
    prm[:, PRM["vb1"]] = inp["phi_v_b1"][l]
    prm[0:8, PRM["dtb"]] = inp["ssd_dt_bias"][l, 8 * g:8 * g + 8]
    prm[0:8, PRM["alog"]] = inp["ssd_a_log"][l, 8 * g:8 * g + 8]
    return {
        "wcat": np.ascontiguousarray(inp["w_in"][l][:, wcat_cols(g)]),
        "prm": prm,
        "peT": np.ascontiguousarray(np.concatenate([inp["nsa_pe_k"][l].T, inp["nsa_pe_v"][l].T], axis=1)).astype(np.float32),
        "kw1": np.ascontiguousarray(inp["phi_k_w1"][l]), "vw1": np.ascontiguousarray(inp["phi_v_w1"][l]),
        "kw2": np.ascontiguousarray(inp["phi_k_w2"][l]), "vw2": np.ascontiguousarray(inp["phi_v_w2"][l]),
        "vb2": np.ascontiguousarray(inp["phi_v_b2"][l][None, :]),
    }


def emit_p0(nc, S, Tc, xT, gv_d, hb):
    S.begin_stage()
    cst = S.sbuf("cst", [128, 128], BF16)
    gv = S.sbuf("gvs", [128, 32], F32)
    x_sb = S.sbuf("x_sb", [128, KC, TQ], F32)
    a16 = S.sbuf("a16", [128, KC, TQ], BF16)
    h_bf = S.sbuf("h_bf", [128, KC, TQ], BF16)
    rstd = S.sbuf("rstd", [128, TQ], F32)
    psn = S.psum("psn", [128, TQ])
    S.op("dve", lambda h: h.memset(cst[:, :], 1.0), writes=[cst])
    S.op("dve", lambda h: h.memset(gv[:, 16:17], EPS), writes=[gv])
    S.dma(gv[:, 0:16], gv_d[:, :], reads=[gv_d], writes=[gv])
    for tt in range(Tc // TQ):
        t0 = tt * TQ
        S.dma(x_sb[:, :, :], xT.t.rearrange("(kc p) t -> p kc t", p=128)[:, :, t0:t0 + TQ], reads=[xT], writes=[x_sb])
        rmsnorm_fm(S, x_sb, gv, 0, TQ, a16, psn, cst[:, 0:128], cst, gv[:, 16:17], rstd, [h_bf])
        for hh in range(2):
            hk = hb[t0 // 256 + hh]
            S.dma(hk.t.rearrange("(kc p) t -> p kc t", p=128), h_bf[:, :, hh * 256:(hh + 1) * 256], reads=[h_bf], writes=[hk])
    S.end_stage()


NCORES = 8
B_, T_, TC_ = 2, 8192, 2048
DEPTH_ = 4
GROUPS = [[0, 1, 2, 3], [4, 5, 6, 7]]
P1_KEYS = ("wcat", "prm", "peT", "kw1", "vw1", "kw2", "vw2", "vb2")
P1_SHAPES = {"wcat": [D, NWC], "prm": [128, PRM["n"]], "peT": [128, 64], "kw1": [4096, 128], "vw1": [4096, 128],
             "kw2": [128, 128], "vw2": [128, 128], "vb2": [1, 128]}
P2_SHAPES = {"w_o": [16, 128, 16, 128], "w_up": [64, 128, 16, 128], "w_dn": [64, 128, 16, 128], "w_pg": [16, 128, 16, 128],
             "w_pl": [16, 128, 2, 128], "gv": [128, 48], "pT": [256, TC_]}


def _blk(w, kc):
    n = w.shape[1]
    return np.ascontiguousarray(np.asarray(w, np.float32).reshape(kc, 128, n // 128, 128).transpose(2, 1, 0, 3))


def _blk_dn(w):
    return np.ascontiguousarray(np.asarray(w, np.float32).reshape(4, 16, 128, 16, 128).transpose(0, 3, 2, 1, 4)).reshape(64, 128, 16, 128)


def build_fused(T=T_, Tc=TC_, depth=DEPTH_):
    nc = bass.Bass("TRN2", target_bir_lowering=False)
    L = cb_layout(T)
    with ExitStack() as st:
        S = Sched(nc, st)
        ext = lambda n, s, d, k="ExternalInput": S.dram(n, s, d, kind=k)
        xT = ext("xT", [D, Tc], F32)
        gv0 = ext("gv0", [128, 16], F32)
        pos = ext("pos", [1, T], I32)
        cF_d = ext("cF", [128, CF["n"]], F32)
        cB_d = ext("cB", [128, L["n"]], BF16)
        oh_d = ext("oh", [128, 4], F32)
        hf = ext("hf", [D, Tc], F32, "ExternalOutput")
        lay = []
        for l in range(depth):
            dct = {k: ext("%s_%d" % (k, l), P1_SHAPES[k], F32) for k in P1_KEYS}
            dct.update({k: ext("%s_%d" % (k, l), ([256, Tc] if k == "pT" else P2_SHAPES[k]), F32) for k in P2_SHAPES})
            lay.append(dct)
        xres = S.dram("xres", [D, Tc], F32)
        hb_loc = [S.dram("hb_loc%d" % k, [D, 256], BF16, kind=None) for k in range(Tc // 256)]
        hT_all = [S.dram("hT_all%d" % k, [4 * D, 256], BF16, kind=None) for k in range(Tc // 256)]
        mT_loc = [S.dram("mT_loc%d" % k, [512, 1024], BF16, kind=None) for k in range(T // 1024)]
        mT_all = [S.dram("mT_all%d" % k, [D, 1024], BF16, kind=None) for k in range(T // 1024)]
        scr = make_p1_scratch(S, T)
        nq = Tc // TQ

        def hT_fn(tt):
            tq, c0 = tt // nq, (tt % nq) * TQ
            outl = []
            for hh in range(2):
                hk = hT_all[c0 // 256 + hh]
                outl.append((hh * 256, (hh + 1) * 256, hk.t[tq * D:(tq + 1) * D, :].rearrange("(kc p) t -> p kc t", p=128), hk))
            return outl

        emit_p0(nc, S, Tc, xT, gv0, hb_loc)
        for l in range(depth):
            d_ = lay[l]
            for k in range(len(hb_loc)):
                S.collective(hb_loc[k], hT_all[k], GROUPS)
            emit_p1(nc, S, T, {"hT_fn": hT_fn, "wcat": d_["wcat"], "pos": pos, "cF": cF_d, "cB": cB_d, "prm": d_["prm"], "peT": d_["peT"],
                               "w1": [d_["kw1"], d_["vw1"]], "w2": [d_["kw2"], d_["vw2"]], "vb2": d_["vb2"], "mT": mT_loc}, scr)
            for k in range(len(mT_loc)):
                S.collective(mT_loc[k], mT_all[k], GROUPS)
            emit_p2(nc, S, Tc, {"x_in": xT if l == 0 else xres, "x_out": xres, "mT_all": mT_all, "oh": oh_d, "pT": d_["pT"],
                                "w_o": d_["w_o"], "w_up": d_["w_up"], "w_dn": d_["w_dn"], "w_pg": d_["w_pg"], "w_pl": d_["w_pl"], "gv": d_["gv"],
                                "hb": hb_loc, "hf": hf if l == depth - 1 else None})
        with nc.Block() as block:
            S.finish(block)
    return nc


_PROG = {}


def _gcol(gvec):
    return np.ascontiguousarray(np.asarray(gvec, np.float32).reshape(16, 128).T)


def kernel(**inputs):
    return _run(inputs, T_, TC_, DEPTH_)


def _run(inputs, T_, TC_, DEPTH_):
    inp = {k: np.asarray(v) for k, v in inputs.items()}
    cores = list(range(NCORES))
    key = (T_, TC_, DEPTH_)
    if key not in _PROG:
        _PROG[key] = (build_fused(T_, TC_, DEPTH_), make_consts(T_))
    prog, (cF, cB) = _PROG[key]
    x = inp["x"].astype(np.float32, copy=False)
    p1 = [[prep_p1_layer(inp, l, g) for g in range(4)] for l in range(DEPTH_)]
    p2 = []
    for l in range(DEPTH_):
        g_next = inp["g_mix"][l + 1] if l + 1 < DEPTH_ else inp["g_final"]
        p2.append({"w_o": _blk(inp["w_o"][l], 16), "w_up": _blk(inp["w_up"][l], 16),
                   "w_dn": _blk_dn(inp["w_down"][l]), "w_pg": _blk(inp["w_ple_gate"][l], 16),
                   "w_pl": _blk(inp["w_ple"][l], 2),
                   "gv": np.ascontiguousarray(np.concatenate([_gcol(inp["g_mlp"][l]), _gcol(inp["g_ple"][l]), _gcol(g_next)], axis=1))})
    g0 = _gcol(inp["g_mix"][0])
    maps = []
    for c in cores:
        b, g = c // 4, c % 4
        sl = slice(g * TC_, (g + 1) * TC_)
        oh = np.zeros((128, 4), np.float32)
        oh[:, g] = 1.0
        m = {"xT": np.ascontiguousarray(x[b, sl, :].T), "gv0": g0, "pos": np.ascontiguousarray(inp["positions"][b:b + 1, :]).astype(np.int32),
             "cF": cF, "cB": cB, "oh": oh}
        m["pos"] = np.ascontiguousarray(m["pos"][:, :T_])
        for l in range(DEPTH_):
            for k in P1_KEYS:
                m["%s_%d" % (k, l)] = p1[l][g][k]
            for k in ("w_o", "w_up", "w_dn", "w_pg", "w_pl", "gv"):
                m["%s_%d" % (k, l)] = p2[l][k]
            m["pT_%d" % l] = np.ascontiguousarray(inp["p"][l, b, sl, :].T.astype(np.float32))
        maps.append(m)
    res = run_bass_kernel_spmd(prog, maps, core_ids=cores)
    out = np.empty((B_, T_, D), np.float32)
    for c in cores:
        b, g = c // 4, c % 4
        out[b, g * TC_:(g + 1) * TC_, :] = np.asarray(res.results[c]["hf"]).T
    return out
```
